# Optimizing a Trainium2 kernel written in Bass

```python
import functools
import jax, jax.numpy as jnp
from jax import lax
import numpy as np

D_MODEL = 1024
BATCH = 4
SEQ = 8192
DEPTH = 1
DEC_BATCH = 8
DEC_SEQ = 16
PAST_LEN = 4096

CHUNK = 64
BAND_CHUNKS = 8
WINDOW = BAND_CHUNKS * CHUNK
ATT_HEADS = 16
ATT_HEAD_DIM = 64
ATT_WIDTH = ATT_HEADS * ATT_HEAD_DIM
MAX_REL = 256
SSD_WIDTH = 2 * D_MODEL
SSD_HEAD_DIM = 64
SSD_HEADS = SSD_WIDTH // SSD_HEAD_DIM
SSD_GROUPS = 4
SSD_STATE = 128
CONV_WIDTH = 4
CONV_CH = SSD_WIDTH + 2 * SSD_GROUPS * SSD_STATE
N_BRANCH = 2
EPS = 1e-6
IN_SIZES = (ATT_WIDTH, ATT_WIDTH, ATT_WIDTH, ATT_WIDTH, SSD_WIDTH, CONV_CH, SSD_HEADS, N_BRANCH * D_MODEL)
IN_DIM = sum(IN_SIZES)
IN_SPLITS = tuple(int(v) for v in np.cumsum(IN_SIZES)[:-1])

kernel_name = "hybrid_chunkband_ssd_streaming_step"


def rms_norm(x, g):
    xf = x.astype(jnp.float32)
    y = xf * lax.rsqrt(jnp.mean(xf * xf, axis=-1, keepdims=True) + EPS)
    return (y * g.astype(jnp.float32)).astype(x.dtype)


def rel_bias_for(q_pos, k_pos, rel_bias):
    idx = jnp.clip(q_pos[:, None] - k_pos[None, :], -MAX_REL, MAX_REL) + MAX_REL
    return rel_bias[:, idx]


def attend(q, k, v, bias, mask):
    s = jnp.einsum('bqhd,bkhd->bhqk', q, k).astype(jnp.float32) * (ATT_HEAD_DIM ** -0.5) + bias[None].astype(jnp.float32)
    s = jnp.where(mask, s, -1e30)
    p = jax.nn.softmax(s, axis=-1).astype(v.dtype)
    return jnp.einsum('bhqk,bkhd->bqhd', p, v)


def prompt_attention(q, k, v, rel_bias):
    b, s = q.shape[:2]
    nc = s // CHUNK
    band = WINDOW + CHUNK
    kp = jnp.pad(k, ((0, 0), (WINDOW, 0), (0, 0), (0, 0)))
    vp = jnp.pad(v, ((0, 0), (WINDOW, 0), (0, 0), (0, 0)))
    k_loc = jnp.arange(band)
    bias = rel_bias_for(jnp.arange(CHUNK) + WINDOW, k_loc, rel_bias)
    qc = jnp.moveaxis(q.reshape(b, nc, CHUNK, ATT_HEADS, ATT_HEAD_DIM), 1, 0)

    def one_chunk(args):
        i, qi = args
        start = i * CHUNK
        kb = lax.dynamic_slice_in_dim(kp, start, band, axis=1)
        vb = lax.dynamic_slice_in_dim(vp, start, band, axis=1)
        mask = (start - WINDOW + k_loc >= 0)[None, None, None, :]
        return attend(qi, kb, vb, bias, mask)

    o = lax.map(one_chunk, (jnp.arange(nc), qc))
    o = jnp.moveaxis(o, 0, 1).reshape(b, s, ATT_HEADS, ATT_HEAD_DIM)
    rows = min(WINDOW, s)
    return o, k[:, s - rows:], v[:, s - rows:]


def sample_attention(q, k, v, rel_bias, cache_k, cache_v):
    rows, t = cache_k.shape[1], q.shape[1]
    k_all = jnp.concatenate([cache_k.astype(k.dtype), k], axis=1)
    v_all = jnp.concatenate([cache_v.astype(v.dtype), v], axis=1)
    q_pos = PAST_LEN + jnp.arange(t)
    k_pos = PAST_LEN + jnp.arange(-rows, t)
    qc, kc = q_pos // CHUNK, k_pos // CHUNK
    mask = ((kc[None, :] <= qc[:, None]) & (kc[None, :] >= qc[:, None] - BAND_CHUNKS))[None, None]
    o = attend(q, k_all, v_all, rel_bias_for(q_pos, k_pos, rel_bias), mask)
    return o, k, v


def causal_dwconv(u, buf, w, b):
    s = u.shape[1]
    up = jnp.concatenate([buf.astype(u.dtype), u], axis=1)
    y = b
    for j in range(CONV_WIDTH):
        y = y + up[:, j:j + s] * w[j]
    return y, up[:, up.shape[1] - (CONV_WIDTH - 1):]


def ssd_scan(xs, dt, a, bm, cm, h0):
    b, s = xs.shape[:2]
    e = SSD_HEADS // SSD_GROUPS
    ln = min(CHUNK, s)
    nc = s // ln

    def to_chunks(t):
        return jnp.moveaxis(t.reshape((b, nc, ln) + t.shape[2:]), 1, 0)

    xc = to_chunks(xs.reshape(b, s, SSD_GROUPS, e, SSD_HEAD_DIM))
    dtc = to_chunks(dt.reshape(b, s, SSD_GROUPS, e))
    bc, cc = to_chunks(bm), to_chunks(cm)
    ag = a.reshape(SSD_GROUPS, e)
    causal = jnp.tril(jnp.ones((ln, ln), dtype=bool))[None, :, :, None, None]

    def step(h, inp):
        x, d, bk, ck = inp
        a_cs = jnp.cumsum(d * ag, axis=1)
        seg = a_cs[:, :, None] - a_cs[:, None, :]
        decay = jnp.exp(jnp.where(causal, seg, -jnp.inf))
        cb = jnp.einsum('blgn,bsgn->blsg', ck, bk)
        xdt = x * d[..., None]
        y_in = jnp.einsum('blsge,bsgep->blgep', cb[..., None] * decay, xdt)
        y_st = jnp.einsum('blgn,bgepn->blgep', ck, h) * jnp.exp(a_cs)[..., None]
        to_end = jnp.exp(a_cs[:, -1:] - a_cs)
        h_new = h * jnp.exp(a_cs[:, -1])[..., None, None] + jnp.einsum('blgn,blgep->bgepn', bk, xdt * to_end[..., None])
        return h_new, y_in + y_st

    h_fin, yc = lax.scan(step, h0.reshape(b, SSD_GROUPS, e, SSD_HEAD_DIM, SSD_STATE), (xc, dtc, bc, cc))
    y = jnp.moveaxis(yc, 0, 1).reshape(b, s, SSD_HEADS, SSD_HEAD_DIM)
    return y, h_fin.reshape(b, SSD_HEADS, SSD_HEAD_DIM, SSD_STATE)


def trunk_layer(x, c, conv_buf, h0, attn_fn, norm_g, w_ada, b_ada, w_in, q_norm_g, k_norm_g,
                w_att_proj, conv_w, conv_b, dt_bias, a_log, d_skip, ssd_norm_g, w_ssd_proj, w_out):
    b, s = x.shape[:2]
    mod = jax.nn.silu(c) @ w_ada + b_ada
    shift, scale, gate = jnp.split(mod[:, None, :], 3, axis=-1)
    h = rms_norm(x, norm_g) * (1 + scale) + shift
    proj = h @ w_in
    q, k, v, z_att, z_ssd, xbc, dt_raw, gates = jnp.split(proj, IN_SPLITS, axis=-1)
    q = rms_norm(q.reshape(b, s, ATT_HEADS, ATT_HEAD_DIM), q_norm_g)
    k = rms_norm(k.reshape(b, s, ATT_HEADS, ATT_HEAD_DIM), k_norm_g)
    v = v.reshape(b, s, ATT_HEADS, ATT_HEAD_DIM)
    o_att, k_st, v_st = attn_fn(q, k, v)
    att_branch = (o_att.reshape(b, s, ATT_WIDTH) * jax.nn.silu(z_att)) @ w_att_proj
    xbc, conv_new = causal_dwconv(xbc, conv_buf, conv_w, conv_b)
    xbc = jax.nn.silu(xbc)
    xs, bm, cm = jnp.split(xbc, [SSD_WIDTH, SSD_WIDTH + SSD_GROUPS * SSD_STATE], axis=-1)
    xs_h = xs.reshape(b, s, SSD_HEADS, SSD_HEAD_DIM).astype(jnp.float32)
    dt = jax.nn.softplus(dt_raw.astype(jnp.float32) + dt_bias.astype(jnp.float32))
    a = -jnp.exp(a_log.astype(jnp.float32))
    y, h_new = ssd_scan(xs_h, dt, a,
                        bm.reshape(b, s, SSD_GROUPS, SSD_STATE).astype(jnp.float32),
                        cm.reshape(b, s, SSD_GROUPS, SSD_STATE).astype(jnp.float32),
                        h0.astype(jnp.float32))
    y = (y + d_skip.astype(jnp.float32)[:, None] * xs_h).reshape(b, s, SSD_WIDTH).astype(x.dtype)
    y = y * jax.nn.silu(z_ssd)
    y = rms_norm(y.reshape(b, s, SSD_GROUPS, SSD_WIDTH // SSD_GROUPS),
                 ssd_norm_g.reshape(SSD_GROUPS, SSD_WIDTH // SSD_GROUPS)).reshape(b, s, SSD_WIDTH)
    ssd_branch = y @ w_ssd_proj
    g_att, g_ssd = jnp.split(gates, 2, axis=-1)
    merged = jax.nn.sigmoid(g_att) * att_branch + jax.nn.sigmoid(g_ssd) * ssd_branch
    out = x + gate * (merged @ w_out)
    return out, k_st, v_st, conv_new, h_new.astype(h0.dtype)


def setup_inputs(seed: int = 0) -> dict:
    key = jax.random.key(seed)
    ks = jax.random.split(key, 32)
    f32 = jnp.float32
    rows = min(WINDOW, PAST_LEN)
    nrm = lambda k, shp, sc: jax.random.normal(k, shp, f32) * sc
    dt0 = jnp.exp(jax.random.uniform(ks[20], (DEPTH, SSD_HEADS), f32, np.log(1e-3), np.log(1e-1)))
    return {
        "x_prompt": nrm(ks[0], (BATCH, SEQ, D_MODEL), 1.0),
        "x_sample": nrm(ks[1], (DEC_BATCH, DEC_SEQ, D_MODEL), 1.0),
        "c_prompt": nrm(ks[2], (BATCH, D_MODEL), 1.0),
        "c_sample": nrm(ks[3], (DEC_BATCH, D_MODEL), 1.0),
        "cache_k": nrm(ks[4], (DEPTH, DEC_BATCH, rows, ATT_HEADS, ATT_HEAD_DIM), 1.0),
        "cache_v": nrm(ks[5], (DEPTH, DEC_BATCH, rows, ATT_HEADS, ATT_HEAD_DIM), 1.0),
        "state_conv": nrm(ks[6], (DEPTH, DEC_BATCH, CONV_WIDTH - 1, CONV_CH), 1.0),
        "state_ssm": nrm(ks[7], (DEPTH, DEC_BATCH, SSD_HEADS, SSD_HEAD_DIM, SSD_STATE), 0.5),
        "norm_g": 1.0 + nrm(ks[8], (DEPTH, D_MODEL), 0.02),
        "w_ada": nrm(ks[9], (DEPTH, D_MODEL, 3 * D_MODEL), 0.5 * D_MODEL ** -0.5),
        "b_ada": nrm(ks[10], (DEPTH, 3 * D_MODEL), 0.01),
        "w_in": nrm(ks[11], (DEPTH, D_MODEL, IN_DIM), D_MODEL ** -0.5),
        "q_norm_g": 1.0 + nrm(ks[12], (DEPTH, ATT_HEAD_DIM), 0.02),
        "k_norm_g": 1.0 + nrm(ks[13], (DEPTH, ATT_HEAD_DIM), 0.02),
        "rel_bias": nrm(ks[14], (DEPTH, ATT_HEADS, 2 * MAX_REL + 1), 0.1),
        "w_att_proj": nrm(ks[15], (DEPTH, ATT_WIDTH, D_MODEL), ATT_WIDTH ** -0.5),
        "conv_w": nrm(ks[16], (DEPTH, CONV_WIDTH, CONV_CH), CONV_WIDTH ** -0.5),
        "conv_b": nrm(ks[17], (DEPTH, CONV_CH), 0.01),
        "dt_bias": dt0 + jnp.log(-jnp.expm1(-dt0)),
        "a_log": jnp.log(jax.random.uniform(ks[18], (DEPTH, SSD_HEADS), f32, 1.0, 16.0)),
        "d_skip": 1.0 + nrm(ks[19], (DEPTH, SSD_HEADS), 0.1),
        "ssd_norm_g": 1.0 + nrm(ks[21], (DEPTH, SSD_WIDTH), 0.02),
        "w_ssd_proj": nrm(ks[22], (DEPTH, SSD_WIDTH, D_MODEL), SSD_WIDTH ** -0.5),
        "w_out": nrm(ks[23], (DEPTH, D_MODEL, D_MODEL), D_MODEL ** -0.5),
    }


def reference(x_prompt, x_sample, c_prompt, c_sample, cache_k, cache_v, state_conv, state_ssm,
              norm_g, w_ada, b_ada, w_in, q_norm_g, k_norm_g, rel_bias, w_att_proj,
              conv_w, conv_b, dt_bias, a_log, d_skip, ssd_norm_g, w_ssd_proj, w_out):
    xp, xs = x_prompt, x_sample
    kp_l, vp_l, cp_l, hp_l, ks_l, vs_l, cs_l, hs_l = [], [], [], [], [], [], [], []
    for l in range(DEPTH):
        lw = (norm_g[l], w_ada[l], b_ada[l], w_in[l], q_norm_g[l], k_norm_g[l], w_att_proj[l],
              conv_w[l], conv_b[l], dt_bias[l], a_log[l], d_skip[l], ssd_norm_g[l], w_ssd_proj[l], w_out[l])
        conv0 = jnp.zeros((xp.shape[0], CONV_WIDTH - 1, CONV_CH), xp.dtype)
        h0 = jnp.zeros((xp.shape[0], SSD_HEADS, SSD_HEAD_DIM, SSD_STATE), state_ssm.dtype)
        xp, kp, vp, cp, hp = trunk_layer(xp, c_prompt, conv0, h0,
                                         functools.partial(prompt_attention, rel_bias=rel_bias[l]), *lw)
        xs, ks, vs, cs, hs = trunk_layer(xs, c_sample, state_conv[l], state_ssm[l],
                                         functools.partial(sample_attention, rel_bias=rel_bias[l],
                                                           cache_k=cache_k[l], cache_v=cache_v[l]), *lw)
        kp_l.append(kp); vp_l.append(vp); cp_l.append(cp); hp_l.append(hp)
        ks_l.append(ks); vs_l.append(vs); cs_l.append(cs); hs_l.append(hs)
    return (xp, xs, jnp.stack(kp_l), jnp.stack(vp_l), jnp.stack(cp_l), jnp.stack(hp_l),
            jnp.stack(ks_l), jnp.stack(vs_l), jnp.stack(cs_l), jnp.stack(hs_l))
```

```python
import numpy as np
from contextlib import ExitStack
import concourse.bass as bass
import concourse.mybir as mybir
from concourse.bass_utils import run_bass_kernel_spmd

F32 = mybir.dt.float32
BF = mybir.dt.bfloat16
AF = mybir.ActivationFunctionType
ALU = mybir.AluOpType
AX = mybir.AxisListType

D = 1024
KC = 8
T = 256
NT = 16
TS = 16
IN_DIM = 11296
EPS = 1e-6
C_ID, C_J, C_TRI, C_STRI, C_CSEL, C_TRIL, C_BONES = 0, 128, 256, 384, 512, 768, 832
NCONST = 960


class Buf:
    __slots__ = ("name", "w", "r", "acc")

    def __init__(self, name):
        self.name = name
        self.w = []
        self.r = []
        self.acc = {}


class DSem:
    __slots__ = ("key", "ops")

    def __init__(self, key):
        self.key = key
        self.ops = []


class _Op:
    __slots__ = ("idx", "eng", "fn", "preds", "succs", "cost", "dsem", "nbytes", "is_output", "seq", "cum",
                 "npred", "ready", "fin", "tag", "start", "blame", "gap", "aset")


class Sched:
    ENGS = ("pe", "act", "dve", "pool", "sp")
    XLAT = 250.0
    SLAT = 100.0
    STARVE = 8000.0

    def __init__(self, same_engine_sync=True, reorder=True):
        self.ops = []
        self.dsems = {}
        self.same_engine_sync = same_engine_sync
        self.reorder = reorder
        self.tag = ""

    def buf(self, name):
        return Buf(name)

    def bufs(self, name, n):
        return [Buf("%s%d" % (name, i)) for i in range(n)]

    def dsem(self, name):
        d = DSem("D_" + name)
        self.dsems[d.key] = d
        return d

    def _add(self, eng, fn, reads, writes, banks, cost, dsem=None, nbytes=0, is_output=False, aset=None):
        o = _Op()
        o.aset = aset
        o.idx = len(self.ops)
        o.eng, o.fn, o.cost, o.dsem, o.nbytes, o.is_output = eng, fn, cost, dsem, nbytes, is_output
        o.succs = []
        o.tag = self.tag
        o.blame = None
        p = set()
        for b in banks:
            p.update(b.acc.values())
            b.acc[eng] = o.idx
        for b in reads:
            p.update(b.w)
        for b in writes:
            p.update(b.w)
            p.update(b.r)
        p.discard(o.idx)
        o.preds = p
        for b in reads:
            b.r.append(o.idx)
        for b in writes:
            b.w = [o.idx]
            b.r = []
        self.ops.append(o)
        if dsem is not None:
            dsem.ops.append(o.idx)
        return o

    def op(self, engname, fn, reads=(), writes=(), banks=(), cost=100.0, aset=None):
        self._add(engname, fn, reads, writes, banks, cost, aset=aset)

    def dma(self, qname, fn, dsem, reads=(), writes=(), is_output=False, nbytes=0):
        self._add(qname, fn, reads, writes, (), 60.0, dsem=dsem, nbytes=nbytes, is_output=is_output)

    def seal(self, dsem, bufs):
        for b in bufs:
            b.w = list(dsem.ops)
            b.r = []

    def _order(self):
        ops = self.ops
        n = len(ops)
        for o in ops:
            o.npred = len(o.preds)
            o.ready = 0.0
            for p in o.preds:
                ops[p].succs.append(o.idx)
        if not self.reorder:
            return {e: [o.idx for o in ops if o.eng == e] for e in self.ENGS}
        ready = {e: [] for e in self.ENGS}
        for o in ops:
            if o.npred == 0:
                ready[o.eng].append(o.idx)
        free = {e: 0.0 for e in self.ENGS}
        order = {e: [] for e in self.ENGS}
        dma_pipe = 0.0
        cur_set = None
        self.n_switch = 0
        done = 0
        WIN = 4000
        oldest = 0
        sched = [False] * n
        while done < n:
            best = None
            while oldest < n and sched[oldest]:
                oldest += 1
            for e in self.ENGS:
                r = ready[e]
                if not r:
                    continue
                f = free[e]
                cand = None
                soon = None
                for i in r:
                    if i > oldest + WIN:
                        continue
                    o = ops[i]
                    if o.ready <= f:
                        if cand is None or i < cand:
                            cand = i
                    elif soon is None or o.ready < ops[soon].ready or (o.ready == ops[soon].ready and i < soon):
                        soon = i
                if e == "act" and cand is not None and cur_set is not None:
                    oa = ops[cand].aset
                    if oa is not None and oa != cur_set and f - ops[cand].ready < self.STARVE:
                        alt = None
                        for i in r:
                            if i > oldest + WIN:
                                continue
                            o2 = ops[i]
                            if o2.ready <= f and (o2.aset is None or o2.aset == cur_set) and (alt is None or i < alt):
                                alt = i
                        if alt is not None:
                            cand = alt
                pick = cand if cand is not None else soon
                if pick is None:
                    continue
                st = max(f, ops[pick].ready)
                if best is None or st < best[0] or (st == best[0] and pick < best[2]):
                    best = (st, e, pick)
            if best is None:
                WIN *= 2
                continue
            st, e, i = best
            o = ops[i]
            ready[e].remove(i)
            sched[i] = True
            o.start = st
            o.gap = st - free[e]
            if o.dsem is not None:
                t0 = max(st + o.cost, dma_pipe)
                dma_pipe = t0 + o.nbytes / 280.0
                o.fin = dma_pipe + 1800.0
                free[e] = st + o.cost
            else:
                sw = 0.0
                if e == "act" and o.aset is not None and o.aset != cur_set:
                    sw = 1300.0
                    cur_set = o.aset
                    self.n_switch += 1
                o.fin = st + o.cost + sw
                free[e] = o.fin
            order[e].append(i)
            done += 1
            for s_ in o.succs:
                so = ops[s_]
                if so.eng == e and o.dsem is None:
                    lat = 0.0 if e == "pe" else self.SLAT
                else:
                    lat = self.XLAT
                if o.fin + lat > so.ready:
                    so.ready = o.fin + lat
                    so.blame = o.idx
                so.npred -= 1
                if so.npred == 0:
                    ready[so.eng].append(s_)
        self.sim_ns = max(o.fin for o in ops)
        return order

    def finish(self):
        self.final_order = self._order()

    def emit(self, nc, stack):
        ops = self.ops
        order = self.final_order
        ekey = {e: "E_" + e for e in self.ENGS}
        sems = {}
        for e in self.ENGS:
            sems[ekey[e]] = stack.enter_context(nc.semaphore("s_" + e))
        for k in self.dsems:
            sems[k] = stack.enter_context(nc.semaphore("s_" + k))
        cnt = {e: 0 for e in self.ENGS}
        dcnt = {k: 0 for k in self.dsems}
        for e in self.ENGS:
            for i in order[e]:
                o = ops[i]
                if o.dsem is not None:
                    dcnt[o.dsem.key] += 16
                    o.cum = dcnt[o.dsem.key]
                    o.seq = None
                else:
                    cnt[e] += 1
                    o.seq = cnt[e]
        progs = {}
        out_ev = {}
        for e in self.ENGS:
            waited = {}
            prog = []
            for i in order[e]:
                o = ops[i]
                need = {}
                for p in o.preds:
                    po = ops[p]
                    if po.dsem is not None:
                        k, v = po.dsem.key, po.cum
                    else:
                        if po.eng == e and (e == "pe" or e in NO_SELF_SYNC or not self.same_engine_sync):
                            continue
                        k, v = ekey[po.eng], po.seq
                    if need.get(k, 0) < v:
                        need[k] = v
                waits = []
                for k, v in need.items():
                    if waited.get(k, 0) >= v:
                        continue
                    waited[k] = v
                    waits.append((k, v))
                if o.dsem is not None:
                    prog.append((waits, o.fn, (o.dsem.key, 16)))
                    if o.is_output:
                        out_ev[o.dsem.key] = max(out_ev.get(o.dsem.key, 0), o.cum)
                else:
                    prog.append((waits, o.fn, (ekey[e], 1)))
            progs[e] = (prog, waited)
        prog, waited = progs["sp"]
        waits = [(k, v) for k, v in out_ev.items() if waited.get(k, 0) < v]
        for e in self.ENGS:
            if e != "sp" and cnt[e] > 0:
                waits.append((ekey[e], cnt[e]))
        prog.append((waits, None, None))

        def replay(prog):
            def run(eng):
                for waits, fn, inc in prog:
                    for s, v in waits:
                        eng.wait_ge(sems[s], v)
                    if fn is not None:
                        fn(eng).then_inc(sems[inc[0]], inc[1])
            return run

        with nc.Block() as block:
            block.tensor(replay(progs["pe"][0]))
            block.scalar(replay(progs["act"][0]))
            block.vector(replay(progs["dve"][0]))
            block.gpsimd(replay(progs["pool"][0]))
            block.sync(replay(progs["sp"][0]))


WB = {}
for _i in range(2):
    WB["q%d" % _i] = ("w_in", 8, 0 + 512 * _i, 512)
    WB["k%d" % _i] = ("w_in", 8, 1024 + 512 * _i, 512)
    WB["v%d" % _i] = ("w_in", 8, 2048 + 512 * _i, 512)
    WB["za%d" % _i] = ("w_in", 8, 3072 + 512 * _i, 512)
    WB["ga%d" % _i] = ("w_in", 8, 9248 + 512 * _i, 512)
    WB["gs%d" % _i] = ("w_in", 8, 10272 + 512 * _i, 512)
    WB["wa%d" % _i] = ("w_att", 8, 512 * _i, 512)
    WB["wo%d" % _i] = ("w_out", 8, 512 * _i, 512)
for _i in range(4):
    WB["zs%d" % _i] = ("w_in", 8, 4096 + 512 * _i, 512)
    WB["ws%d" % _i] = ("w_ssd", 16, 256 * _i, 256)
for _i in range(6):
    WB["xb%d" % _i] = ("w_in", 8, 6144 + 512 * _i, 512)
WB["dt"] = ("w_in", 8, 9216, 32)

for _i in range(12):
    WB["ad%d" % _i] = ("w_ada", 8, 256 * _i, 256)
ADA_BLOCKS = ["ad%d" % i for i in range(12)]
P1_BLOCKS = ["q0", "q1", "k0", "k1", "v0", "v1", "za0", "za1"]
P2_BLOCKS = ["ga0", "ga1", "wa0", "wa1", "dt"] + ["xb%d" % i for i in range(6)] + ["zs%d" % i for i in range(4)]
P3_BLOCKS = ["gs0", "gs1"] + ["ws%d" % i for i in range(4)] + ["wo0", "wo1"]
MAIN_BLOCKS = P1_BLOCKS + P2_BLOCKS + P3_BLOCKS
STATE_BLOCKS = ["dt"] + ["xb%d" % i for i in range(5)]
STATE_KV_BLOCKS = ["k0", "k1", "v0", "v1"] + STATE_BLOCKS
STATE_LAST_BLOCKS = STATE_KV_BLOCKS + ["xb5"]
NW = 3
SAME_ENGINE_SYNC = True
NO_SELF_SYNC = ()


class Builder:
    def __init__(self, dbg=()):
        self.dbg = set(dbg)
        self.dbg_out = {}
        self.nc = bass.Bass("TRN2", target_bir_lowering=False)
        self.S = Sched(same_engine_sync=SAME_ENGINE_SYNC)
        self.st = ExitStack()

    def sb(self, name, shape, dt):
        return self.st.enter_context(self.nc.sbuf_tensor(name, shape, dt))

    def din(self, name, shape, dt=F32):
        return self.nc.dram_tensor(name, shape, dt, kind="ExternalInput")

    def dout(self, name, shape, dt=F32):
        return self.nc.dram_tensor(name, shape, dt, kind="ExternalOutput")

    POOLS = {"R": [5, 6, 7], "L": [0, 1, 2, 3, 4], "ALL": [0, 1, 2, 3, 4, 5, 6, 7]}

    def bank_get(self, k=1):
        banks = self.POOLS[self.pool]
        n = len(banks)
        nxt = self.bnext.get(self.pool, 0)
        for _ in range(2 * n + 4):
            s = nxt
            if s + k > n:
                s = 0
            cand = banks[s:s + k]
            if len(cand) == k and all(b not in self.bpinned for b in cand):
                self.bnext[self.pool] = (s + k) % n
                return cand[0]
            nxt = (s + 1) % n
        raise RuntimeError("no psum banks in pool " + self.pool)

    def bank_pin(self, k):
        self.bnext[self.pool] = 0
        s = self.bank_get(k)
        self.bpinned |= set(range(s, s + k))
        return s

    def bank_unpin(self, s, k):
        self.bpinned -= set(range(s, s + k))

    def PS(self, bank, c0, c1, r0=0, r1=128):
        return self.psum[r0:r1, bank * 512 + c0: bank * 512 + c1]

    def PSB(self, bank, nb, c0, c1, r0=0, r1=128):
        return self.psum[:, bank * 512:(bank + nb) * 512].bitcast(BF)[r0:r1, c0:c1]

    def bk(self, bank, n=1):
        return [self.pb[bank + i] for i in range(n)]

    @staticmethod
    def _fd(ap):
        n = 1
        for d in ap.shape[1:]:
            n *= int(d)
        return n

    def _ecost(self, eng, out, ins=()):
        n = self._fd(out)
        if eng == "pool":
            return 200.0 + 1.7 * n
        if eng == "act":
            return 150.0 + 0.75 * n
        allbf = out.dtype == BF and all(getattr(a, "dtype", None) == BF for a in ins)
        return 70.0 + (0.6 if allbf else 1.3) * n

    def mm(self, out, lhsT, rhs, start, stop, reads, banks):
        cost = max(64, self._fd(out)) * (4 if lhsT.dtype == F32 else 1) / 2.4 + 16
        self.S.op("pe", lambda e: e.matmul(out, lhsT=lhsT, rhs=rhs, start=start, stop=stop), reads=reads, banks=banks, cost=cost)

    def tp(self, out, in_, ident, reads, banks):
        cost = max(64, self._fd(in_)) * (2 if in_.dtype == F32 else 1) / 2.4 + 16
        self.S.op("pe", lambda e: e.transpose(out, in_, ident), reads=reads, banks=banks, cost=cost)

    def act(self, out, in_, func, reads=(), writes=(), banks=(), **kw):
        cost = self._ecost("act", out) + (100.0 if "accum_out" in kw else 0.0)
        aset = {AF.Silu: "silu", AF.Sigmoid: "sig", AF.Exp: "exp", AF.Ln: "exp"}.get(func)
        self.S.op("act", lambda e: e.activation(out=out, in_=in_, func=func, **kw), reads=reads, writes=writes, banks=banks, cost=cost,
                  aset=aset)

    def tt(self, eng, out, in0, in1, op, reads=(), writes=(), banks=()):
        self.S.op(eng, lambda e: e.tensor_tensor(out=out, in0=in0, in1=in1, op=op), reads=reads, writes=writes, banks=banks,
                  cost=self._ecost(eng, out, (in0, in1)))

    def ts(self, eng, out, in0, s1, s2, op0, op1=None, reads=(), writes=(), banks=()):
        cost = self._ecost(eng, out, (in0,))
        if op1 is None:
            self.S.op(eng, lambda e: e.tensor_scalar(out=out, in0=in0, scalar1=s1, scalar2=None, op0=op0),
                      reads=reads, writes=writes, banks=banks, cost=cost)
        else:
            self.S.op(eng, lambda e: e.tensor_scalar(out=out, in0=in0, scalar1=s1, scalar2=s2, op0=op0, op1=op1),
                      reads=reads, writes=writes, banks=banks, cost=cost)

    def stt(self, out, in0, scalar, in1, op0, op1, reads=(), writes=(), banks=()):
        self.S.op("dve", lambda e: e.scalar_tensor_tensor(out=out, in0=in0, scalar=scalar, in1=in1, op0=op0, op1=op1),
                  reads=reads, writes=writes, banks=banks, cost=70.0 + 1.7 * self._fd(out))

    def cp(self, eng, out, in_, reads=(), writes=(), banks=()):
        cost = self._ecost(eng, out, (in_,))
        if eng == "act":
            self.S.op("act", lambda e: e.copy(out=out, in_=in_), reads=reads, writes=writes, banks=banks, cost=cost)
        else:
            self.S.op(eng, lambda e: e.tensor_copy(out=out, in_=in_), reads=reads, writes=writes, banks=banks, cost=cost)

    def memset(self, eng, ap, val, writes):
        self.S.op(eng, lambda e: e.memset(ap, val), writes=writes, cost=100.0 + 0.5 * self._fd(ap))

    def dma(self, q, out, in_, dsem, reads=(), writes=(), is_output=False, slow=False):
        nbytes = (2 if (out.dtype == BF and in_.dtype == BF) else 4) * int(out.shape[0]) * self._fd(out)
        if slow:
            self.S.dma(q, lambda e: e.dma_start(out=out, in_=in_, allow_slow_non_contiguous=True), dsem,
                       reads=reads, writes=writes, is_output=is_output, nbytes=nbytes * 8)
        else:
            self.S.dma(q, lambda e: e.dma_start(out=out, in_=in_), dsem, reads=reads, writes=writes, is_output=is_output,
                       nbytes=nbytes)

    def dump(self, name, ap, reads):
        if name not in self.dbg:
            return
        t = self.dout("dbg_" + name, list(ap.shape))
        self.dbg_out["dbg_" + name] = list(ap.shape)
        self.dma("sp", t.ap(), ap, self.d_out, reads=reads, is_output=True)

    def wget(self, bid):
        i = self.wpos
        assert self.wseq[i] == bid, (i, self.wseq[i], bid)
        while self.wissued < min(len(self.wseq), i + NW):
            j = self.wissued
            b = self.wseq[j]
            wname, kc, c0, n = WB[b]
            slot = j % NW
            if wname == "w_ada":
                src = self.I[wname].ap()[:, c0:c0 + n].rearrange("(kc p) n -> p kc n", p=128)
                dst = self.wring[slot][:, :].bitcast(F32)[:, 0:kc * n].rearrange("p (kc n) -> p kc n", kc=kc)
                self.dma("sp", dst, src, self.d_w[slot], writes=[self.B_w[slot]])
            else:
                src = self.wscr[wname].ap()[:, c0:c0 + n].rearrange("(kc p) n -> p kc n", p=128)
                dst = self.wring[slot][:, 0:kc * n].rearrange("p (kc n) -> p kc n", kc=kc)
                self.dma("sp", dst, src, self.d_w[slot], reads=[self.scrbuf[b]], writes=[self.B_w[slot]])
            self.wissued += 1
        self.wpos += 1
        slot = i % NW
        wname, kc, c0, n = WB[bid]
        if wname == "w_ada":
            return self.wring[slot][:, :].bitcast(F32)[:, 0:kc * n].rearrange("p (kc n) -> p kc n", kc=kc), self.B_w[slot]
        return self.wring[slot][:, 0:kc * n].rearrange("p (kc n) -> p kc n", kc=kc), self.B_w[slot]

    def build(self, n_state=NT, n_main=NT, do_sample=True):
        nc, S = self.nc, self.S
        self.n_state, self.n_main, self.do_sample = n_state, n_main, do_sample
        I = {}
        for name, shape in [("x_main", [NT * T, D]), ("x_prev", [NT * T, D]), ("x_s", [TS, D]), ("c2", [2, D]),
                            ("cache_k", [512, D]), ("cache_v", [512, D]), ("state_conv", [3, 3072]),
                            ("state_ssm", [2048, 128]), ("flags", [128, 2]), ("consts", [128, NCONST]),
                            ("norm_g", [D]), ("w_ada", [D, 3 * D]), ("b_ada", [3 * D]), ("w_in", [D, IN_DIM]),
                            ("q_norm_g", [64]), ("k_norm_g", [64]), ("rel_bias", [16, 513]), ("w_att", [D, D]),
                            ("conv_w", [4, 3072]), ("conv_b", [3072]), ("dt_bias", [32]), ("a_log", [32]),
                            ("d_skip", [32]), ("ssd_norm_g", [2048]), ("w_ssd", [2048, D]), ("w_out", [D, D])]:
            I[name] = self.din(name, shape)
        self.I = I
        O = {}
        for name, shape in [("y_main", [NT * T, D]), ("y_s", [TS, D]), ("nk", [512, D]), ("nv", [512, D]),
                            ("nconv", [3, 3072]), ("nssm", [2048, 128]), ("nk_s", [TS, D]), ("nv_s", [TS, D]),
                            ("nconv_s", [3, 3072]), ("nssm_s", [2048, 128])]:
            O[name] = self.dout(name, shape)
        self.O = O
        self.wscr = {}
        for wname, shape in [("w_in", [D, IN_DIM]), ("w_att", [D, D]), ("w_ssd", [2048, D]), ("w_out", [D, D])]:
            self.wscr[wname] = nc.dram_tensor(wname + "_b", shape, BF, kind="Internal")
        self.ext = nc.dram_tensor("ext_bias", [16, 1024], F32, kind="Internal")

        self.psum = self.st.enter_context(nc.psum_tensor("psum", [128, 4096], F32))
        self.pb = S.bufs("bank", 8)
        self.bnext = {}
        self.pool = "ALL"
        self.bpinned = set()

        self.d_w = [S.dsem("w%d" % i) for i in range(NW)]
        self.d_cv = S.dsem("cv")
        self.d_par = S.dsem("par")
        self.d_x = [S.dsem("x%d" % i) for i in range(4)]
        self.d_out = S.dsem("out")
        self.d_misc = S.dsem("misc")
        self.d_ost = [S.dsem("ost0")] * 2
        self.d_xo = [S.dsem("xo%d" % i) for i in range(4)]
        self.d_cvo = S.dsem("cvo")
        self.d_sso = S.dsem("sso")
        self.d_cvs = [S.dsem("cv%d" % i) for i in range(4)]
        self.B_cvslot = S.bufs("cvslot", 4)

        sb = self.sb
        self.cst = sb("cst", [128, NCONST], F32)
        self.identb = sb("identb", [128, 128], BF)
        self.strib = sb("strib", [128, 128], BF)
        self.bonesb = sb("bonesb", [128, 128], BF)
        self.wring = [sb("wring%d" % i, [128, 4096], BF) for i in range(NW)]
        self.xt = sb("xt", [128, 4, D], F32)
        self.xn = [sb("xn%d" % i, [128, D], BF) for i in range(2)]
        self.hT2 = [sb("hT%d" % i, [128, 8, T], BF) for i in range(2)]
        self.hT = self.hT2[0]
        self.qT = sb("qT", [128, 8, T], BF)
        self.kT = sb("kT", [128, 8, 3 * T], BF)
        self.Vr = sb("Vr", [128, 6, 1040], BF)
        self.zatt = sb("zatt", [128, 2, D], BF)
        self.zssd = sb("zssd", [128, 2, 2048], BF)
        self.sq = [sb("sq%d" % i, [128, T], BF) for i in range(2)]
        self.rs = [sb("rs%d" % i, [128, T], F32) for i in range(2)]
        self.PT = [sb("PT%d" % i, [128, 640], BF) for i in range(2)]
        self.tab = sb("tab", [128, 16, 384], BF)
        self.bconst = sb("bconst", [128, 16], F32)
        self.rec = sb("rec", [128, 16], F32)
        self.Osb = sb("Osb", [128, D], BF)
        self.OgT = sb("OgT", [128, 8, T], BF)
        self.sig = sb("sig", [128, 8, T], BF)
        self.pre = sb("pre", [128, 8, T], BF)
        self.tmpm = [sb("tmpm%d" % i, [128, T], BF) for i in range(1)]
        self.craw = [sb("craw%d" % i, [128, T + 3], F32) for i in range(2)]
        self.cacc = [sb("cacc%d" % i, [128, T], F32) for i in range(2)]
        self.xsT = sb("xsT", [128, 16, T], BF)
        self.BT = sb("BT", [128, 4, T], BF)
        self.CT = sb("CT", [128, 4, T], BF)
        self.halo = sb("halo", [128, 24, 3], F32)
        self.dtraw = sb("dtraw", [128, 2, 32], F32)
        self.dtt = sb("dtt", [128, 32], F32)
        self.dt = sb("dt", [128, 32], F32)
        self.dta = sb("dta", [128, 32], F32)
        self.EW = sb("EW", [128, 64], F32)
        self.EA = sb("EA", [128, 64], F32)
        self.R = sb("R", [128, 1024], BF)
        self.dec = sb("dec", [128, 2048], BF)
        self.cbm = sb("cbm", [128, 256], BF)
        self.xdt = sb("xdt", [128, 2048], BF)
        self.xsD = sb("xsD", [128, 2048], BF)
        self.xdtw = None
        self.Btok = sb("Btok", [128, 512], BF)
        self.y = sb("y", [128, 2048], F32)
        self.ytmp = [sb("ytmp%d" % i, [128, 512], F32) for i in range(2)]
        self.ynorm = sb("ynorm", [128, 2048], BF)
        self.xdtw = self.ynorm
        self.ss = sb("ss", [128, 8], F32)
        self.yT = sb("yT", [128, 16, T], BF)
        self.H = sb("H", [128, 2048], F32)
        self.Hbf = sb("Hbf", [128, 2048], BF)
        self.ost = [sb("ost0", [128, D], F32)] * 2
        self.gate_tok = sb("gate_tok", [128, D], F32)
        self.normgT = sb("normgT", [128, 8], F32)
        self.badaT = sb("badaT", [128, 24], F32)
        self.cT = sb("cT", [128, 2, 8], F32)
        self.siluT = sb("siluT", [128, 8, 2], F32)
        self.modT = sb("modT", [128, 24, 2], F32)
        self.gmodT = sb("gmodT", [128, 8, 2], F32)
        self.gqk = sb("gqk", [128, 2], F32)
        self.convwT = sb("convwT", [128, 4, 24], F32)
        self.convbT = sb("convbT", [128, 24], F32)
        self.gssdT = sb("gssdT", [128, 16], F32)
        self.rows = sb("rows", [128, 3, 32], F32)
        self.arow = sb("arow", [128, 32], F32)
        self.flg = sb("flg", [128, 2], F32)
        self.Atile = self.ost[0][:, 0:128]
        print("sbuf bytes remaining:", nc.sbuf_bytes_remaining)

        nb = S.buf
        self.B_cst = nb("cst")
        self.B_par = nb("par")
        self.B_w = S.bufs("w", NW)
        self.B_xt = S.bufs("xt", 4)
        self.B_xn = S.bufs("xn", 2)
        self.B_hT2 = [S.bufs("hT%d_" % i, 2) for i in range(2)]
        self.B_hT = self.B_hT2[0]
        self.B_qT = S.bufs("qT", 8)
        self.B_kT = [S.bufs("kT%d_" % s, 8) for s in range(3)]
        self.B_V = [S.bufs("V%d_" % s, 2) for s in range(3)]
        self.B_zatt = S.bufs("zatt", 2)
        self.B_zssd = S.bufs("zssd", 2)
        self.B_sq = S.bufs("sq", 2)
        self.B_rs = S.bufs("rs", 2)
        self.B_PT = S.bufs("PT", 2)
        self.B_tab = nb("tab")
        self.B_rec = nb("rec")
        self.B_Osb = nb("Osb")
        self.B_OgT = nb("OgT")
        self.B_sig = S.bufs("sig", 8)
        self.B_pre = S.bufs("pre", 8)
        self.B_tmpm = S.bufs("tmpm", 2)
        self.B_craw = S.bufs("craw", 2)
        self.B_cacc = S.bufs("cacc", 2)
        self.B_xsT = S.bufs("xsT", 16)
        self.B_BT = S.bufs("BT", 4)
        self.B_CT = S.bufs("CT", 4)
        self.B_halo = S.bufs("halo", 24)
        self.B_dtraw = S.bufs("dtraw", 2)
        self.B_dts = nb("dts")
        self.B_R = nb("R")
        self.B_dec = S.bufs("dec", 2)
        self.B_cbm = nb("cbm")
        self.B_xdt = nb("xdt")
        self.B_xsD = nb("xsD")
        self.B_xdtw = None
        self.B_Btok = nb("Btok")
        self.B_y = S.bufs("y", 4)
        self.B_ytmp = S.bufs("ytmp", 2)
        self.B_ynorm = S.bufs("ynorm", 4)
        self.B_xdtw = self.B_ynorm
        self.B_ss = nb("ss")
        self.B_yT = S.bufs("yT", 2)
        self.B_H = S.bufs("H", 4)
        self.B_Hbf = S.bufs("Hbf", 4)
        self.B_ost = [S.buf("ost")] * 2
        self.B_gate = nb("gate")
        self.B_mod = nb("mod")
        self.B_A = self.B_ost[0]
        self.scrbuf = {b: nb("scr_" + b) for b in WB}
        self.B_ext = nb("ext")
        self.rot = {}
        self.state_mode = False
        self.crawx = list(self.craw) + [self.y[:, k * 260:k * 260 + T + 3] for k in range(3)]
        self.caccx = list(self.cacc) + [self.y[:, 1024 + k * 256:1024 + k * 256 + T] for k in range(3)]
        self.B_crawx = list(self.B_craw) + S.bufs("crawx", 3)
        self.B_caccx = list(self.B_cacc) + S.bufs("caccx", 3)

        seq = list(ADA_BLOCKS)
        if do_sample:
            seq += MAIN_BLOCKS
        for i in range(n_state):
            seq += STATE_LAST_BLOCKS if i == n_state - 1 else (STATE_KV_BLOCKS if i == n_state - 2 else STATE_BLOCKS)
        if n_main > 0:
            seq += P1_BLOCKS
        for i in range(n_main):
            seq += P2_BLOCKS + (P1_BLOCKS if i + 1 < n_main else []) + P3_BLOCKS
        self.wseq, self.wpos, self.wissued = seq, 0, 0

        self.setup()
        if do_sample:
            self.sample_tile()
        self.state_pass()
        self.main_pass()
        assert self.wpos == len(self.wseq), (self.wpos, len(self.wseq))
        S.finish()
        S.emit(nc, self.st)
        self.st.close()
        return nc

    def rotn(self, key, n=2):
        i = self.rot.get(key, 0)
        self.rot[key] = (i + 1) % n
        return i

    def setup(self):
        nc, S, I = self.nc, self.S, self.I
        cst = self.cst
        self.dma("sp", cst[:], I["consts"].ap(), self.d_par, writes=[self.B_cst])
        order = MAIN_BLOCKS
        for bi, b in enumerate(order):
            wname, kc, c0, n = WB[b]
            self.dma("pool", self.wscr[wname].ap()[:, c0:c0 + n], I[wname].ap()[:, c0:c0 + n], self.d_cvs[bi % 4],
                     writes=[self.scrbuf[b], self.B_cvslot[bi % 4]])
        self.dma("sp", self.y[0:16, 0:513], I["rel_bias"].ap(), self.d_misc, writes=[self.B_y[0], self.B_y[1]])
        self.cp("dve", self.y[0:16, 513:1024], self.y[0:16, 512:513].to_broadcast([16, 511]), reads=[self.B_y[0], self.B_y[1]],
                writes=[self.B_y[0], self.B_y[1]])
        self.dma("sp", self.ext.ap(), self.y[0:16, 0:1024], self.d_misc, reads=[self.B_y[0], self.B_y[1]], writes=[self.B_ext])
        P = self.d_par
        dm = lambda out, in_: self.dma("sp", out, in_, P, writes=[self.B_par], slow=True)
        dm(self.normgT[:], I["norm_g"].ap().rearrange("(kc p) -> p kc", p=128))
        dm(self.badaT[:], I["b_ada"].ap().rearrange("(t p) -> p t", p=128))
        for s_ in range(2):
            dm(self.cT[:, s_, :], I["c2"].ap()[s_, :].rearrange("(kc p) -> p kc", p=128))
        for half in range(2):
            dm(self.gqk[half * 64:(half + 1) * 64, 0:1], I["q_norm_g"].ap().rearrange("(p o) -> p o", o=1))
            dm(self.gqk[half * 64:(half + 1) * 64, 1:2], I["k_norm_g"].ap().rearrange("(p o) -> p o", o=1))
        for j_ in range(4):
            dm(self.convwT[:, j_, :], I["conv_w"].ap()[j_, :].rearrange("(t p) -> p t", p=128))
        dm(self.convbT[:], I["conv_b"].ap().rearrange("(t p) -> p t", p=128))
        dm(self.gssdT[:], I["ssd_norm_g"].ap().rearrange("(t p) -> p t", p=128))
        dm(self.rows[:, 0, :], I["dt_bias"].ap().partition_broadcast(128))
        dm(self.rows[:, 1, :], I["a_log"].ap().partition_broadcast(128))
        dm(self.rows[:, 2, :], I["d_skip"].ap().partition_broadcast(128))
        dm(self.flg[:], I["flags"].ap())
        S.seal(P, [self.B_cst, self.B_par])
        Bc, Bp = self.B_cst, self.B_par
        self.cp("dve", self.identb[:], cst[:, C_ID:C_ID + 128], reads=[Bc], writes=[Bc])
        self.cp("dve", self.strib[:], cst[:, C_STRI:C_STRI + 128], reads=[Bc], writes=[Bc])
        self.cp("dve", self.bonesb[:], cst[:, C_BONES:C_BONES + 128], reads=[Bc], writes=[Bc])
        self.act(self.arow[:], self.rows[:, 1, :], AF.Exp, reads=[Bp], writes=[Bp])
        self.ts("dve", self.arow[:], self.arow[:], -1.0, None, ALU.mult, reads=[Bp], writes=[Bp])
        self.memset("pool", self.Vr[:].rearrange("p b (h c) -> p b h c", c=65)[:, :, :, 64:65], 1.0,
                    writes=[b for s in self.B_V for b in s])
        self.act(self.siluT[:], self.cT[:].rearrange("p s k -> p k s"), AF.Silu, reads=[Bp], writes=[Bp])
        bank = self.bank_get()
        for bi, bid in enumerate(ADA_BLOCKS):
            wt, wbuf = self.wget(bid)
            for ct in range(2):
                t = bi * 2 + ct
                for kc in range(8):
                    self.mm(self.PS(bank, 2 * t, 2 * t + 2), wt[:, kc, ct * 128:(ct + 1) * 128], self.siluT[:, kc, :],
                            kc == 0, kc == 7, reads=[wbuf, Bp], banks=self.bk(bank))
        self.tt("dve", self.modT[:], self.PS(bank, 0, 48).rearrange("p (t s) -> p t s", s=2),
                self.badaT[:].unsqueeze(2).to_broadcast([128, 24, 2]), ALU.add,
                reads=[Bp], writes=[self.B_mod], banks=self.bk(bank))
        self.ts("dve", self.gmodT[:], self.modT[:, 8:16, :], 1.0, None, ALU.add, reads=[self.B_mod], writes=[self.B_mod])
        self.tt("dve", self.gmodT[:], self.gmodT[:], self.normgT[:].unsqueeze(2).to_broadcast([128, 8, 2]), ALU.mult,
                reads=[Bp, self.B_mod], writes=[self.B_mod])
        self.dump("modT", self.modT[:], [self.B_mod])
        hk = self.y
        for h in range(16):
            src = bass.AP(self.ext, h * 1024 + 129, [[1, 128], [1, 384]])
            self.dma("sp", hk[:, 0:384], src, self.d_misc, reads=[self.B_ext], writes=[self.B_y[0]], slow=True)
            bank = self.bank_get()
            self.mm(self.PS(bank, 0, 384), cst[:, C_J:C_J + 128], hk[:, 0:384], True, True,
                    reads=[Bc, self.B_y[0]], banks=self.bk(bank))
            self.act(self.tab[:, h, :], self.PS(bank, 0, 384), AF.Exp, writes=[self.B_tab], banks=self.bk(bank))
        self.memset("pool", self.tab[64:128, :, 0:64], 0.0, writes=[self.B_tab])
        srcc = bass.AP(self.ext, 600, [[0, 128], [1024, 16], [1, 1]])
        self.dma("sp", self.bconst[:, :].unsqueeze(2), srcc, self.d_misc, reads=[self.B_ext], writes=[self.B_tab], slow=True)
        S.seal(self.d_misc, [self.B_tab])
        self.dump("tab", self.tab[:, 0, :], [self.B_tab])

    def make_gate_tok(self, seq):
        cst = self.cst
        bank = self.bank_get(2)
        for kc in range(8):
            self.act(self.Atile, cst[:, 0:128], AF.Identity, reads=[self.B_mod, self.B_cst], writes=[self.B_A],
                     scale=0.0, bias=self.modT[:, 16 + kc, seq:seq + 1])
            self.mm(self.psum[:, bank * 512 + kc * 128: bank * 512 + (kc + 1) * 128], self.Atile, cst[:, C_ID:C_ID + 128],
                    True, True, reads=[self.B_A, self.B_cst], banks=self.bk(bank + kc // 4))
        self.cp("dve", self.gate_tok[:], self.psum[:, bank * 512: bank * 512 + 1024], writes=[self.B_gate],
                banks=self.bk(bank, 2))

    def load_norm(self, src_ap, row0, Tn, seq, plain_kslot=None, par=0):
        cst = self.cst
        nbk = (Tn + 127) // 128
        for b in range(nbk):
            bs = min(128, Tn - 128 * b)
            xb_ = 2 * par + b
            self.dma("sp", self.xt[0:bs, xb_, :], src_ap[row0 + 128 * b: row0 + 128 * b + bs, :], self.d_x[xb_],
                     writes=[self.B_xt[xb_]])
            i = self.rotn("xn")
            xn, Bxn = self.xn[i], self.B_xn[i]
            if plain_kslot is None:
                self.act(xn[0:bs, :], self.xt[0:bs, xb_, :], AF.Square, reads=[self.B_xt[xb_]], writes=[Bxn, self.B_ss],
                         accum_out=self.ss[0:bs, 4:5])
                self.act(self.ss[0:bs, 5:6], self.ss[0:bs, 4:5], AF.Ln, reads=[self.B_ss], writes=[self.B_ss],
                         scale=1.0 / D, bias=EPS)
                self.act(self.ss[0:bs, 5:6], self.ss[0:bs, 5:6], AF.Exp, reads=[self.B_ss], writes=[self.B_ss], scale=-0.5)
                self.act(xn[0:bs, :], self.xt[0:bs, xb_, :], AF.Copy, reads=[self.B_xt[xb_], self.B_ss], writes=[Bxn],
                         scale=self.ss[0:bs, 5:6])
            else:
                self.cp("act", xn[0:bs, :], self.xt[0:bs, xb_, :], reads=[self.B_xt[xb_]], writes=[Bxn])
            bank = self.bank_get()
            for kc in range(8):
                self.tp(self.PSB(bank, 1, kc * 128, kc * 128 + bs), xn[0:bs, kc * 128:(kc + 1) * 128],
                        self.identb[0:bs, 0:bs], reads=[Bxn, self.B_cst], banks=self.bk(bank))
            if plain_kslot is None:
                for kc in range(8):
                    self.ts("dve", self.hT[:, kc, 128 * b:128 * b + bs], self.PSB(bank, 1, kc * 128, kc * 128 + bs),
                            self.gmodT[:, kc, seq:seq + 1], self.modT[:, kc, seq:seq + 1], ALU.mult, ALU.add,
                            reads=[self.B_mod], writes=[self.B_hT[b]], banks=self.bk(bank))
            else:
                slot, half = plain_kslot[b]
                self.cp("dve", self.kT[:, :, slot * T + half * 128: slot * T + half * 128 + bs],
                        self.PSB(bank, 1, 0, 1024).rearrange("p (k t) -> p k t", t=128)[:, :, 0:bs],
                        writes=self.B_kT[slot], banks=self.bk(bank))

    def proj_fm(self, wt, wbuf, ct, Tn, bank, c0):
        for kc in range(8):
            self.mm(self.PS(bank, c0, c0 + Tn), wt[:, kc, ct * 128:(ct + 1) * 128], self.hT[:, kc, 0:Tn],
                    kc == 0, kc == 7, reads=[wbuf] + self.B_hT, banks=self.bk(bank))

    def proj_tm(self, wt, wbuf, b, bs, ncols, bank):
        for kc in range(8):
            self.mm(self.PS(bank, 0, ncols, 0, bs), self.hT[:, kc, 128 * b:128 * b + bs], wt[:, kc, 0:ncols],
                    kc == 0, kc == 7, reads=[wbuf, self.B_hT[b]], banks=self.bk(bank))

    def qk_tile(self, which, wt, wbuf, ctl, hp, Tn, slot):
        bank = self.bank_get()
        self.proj_fm(wt, wbuf, ctl, Tn, bank, 0)
        ps = self.PS(bank, 0, Tn)
        i = self.rotn("sq")
        sq, rs = self.sq[i], self.rs[i]
        self.act(sq[:, 0:Tn], ps, AF.Square, writes=[self.B_sq[i]], banks=self.bk(bank))
        bank2 = self.bank_get()
        self.mm(self.PS(bank2, 0, Tn), self.bonesb[:], sq[:, 0:Tn], True, True, reads=[self.B_sq[i], self.B_cst],
                banks=self.bk(bank2))
        self.act(rs[:, 0:Tn], self.PS(bank2, 0, Tn), AF.Ln, writes=[self.B_rs[i]], banks=self.bk(bank2),
                 scale=1.0 / 64, bias=EPS)
        self.act(rs[:, 0:Tn], rs[:, 0:Tn], AF.Exp, reads=[self.B_rs[i]], writes=[self.B_rs[i]], scale=-0.5)
        if which == "q":
            dst, Bd, g = self.qT[:, hp, 0:Tn], self.B_qT[hp], self.gqk[:, 0:1]
        else:
            dst, Bd, g = self.kT[:, hp, slot * T: slot * T + Tn], self.B_kT[slot][hp], self.gqk[:, 1:2]
        self.stt(dst, ps, g, rs[:, 0:Tn], ALU.mult, ALU.mult, reads=[self.B_rs[i], self.B_par], writes=[Bd],
                 banks=self.bk(bank))

    def keyblock(self, g, slot, bss):
        if g >= 0:
            return slot * T + g * 128, slot * 2 + g, (slot, g), bss[g]
        j = g + 4
        hs = (slot - 2) % 3 if j < 2 else (slot - 1) % 3
        half = j % 2
        return hs * T + half * 128, hs * 2 + half, (hs, half), 128

    def attention(self, Tn, slot, tile_idx):
        nbk = (Tn + 127) // 128
        bss = [min(128, Tn - 128 * b) for b in range(nbk)]
        for p in range(nbk):
            qn = bss[p]
            ob = self.bank_pin(3)
            for h in range(16):
                hp, base = h // 2, 64 * (h % 2)
                sbk = self.bank_get(2)
                pi = self.rotn("PT")
                PT, BPT = self.PT[pi], self.B_PT[pi]
                kinfo = []
                for r in range(5):
                    g = p - r
                    kcol, vblk, vsh, kn = self.keyblock(g, slot, bss)
                    kslot = kcol // T
                    kinfo.append((kcol, vblk, vsh, kn, kslot))
                    self.mm(self.psum[0:kn, sbk * 512 + r * 128: sbk * 512 + r * 128 + qn],
                            self.kT[base:base + 64, hp, kcol:kcol + kn], self.qT[base:base + 64, hp, p * 128:p * 128 + qn],
                            True, True, reads=[self.B_kT[kslot][hp], self.B_qT[hp]], banks=self.bk(sbk + r // 4))
                runs = []
                for r in range(5):
                    kn = kinfo[r][3]
                    masked = (tile_idx is not None) and (tile_idx * 2 + p - r < 0)
                    key = (kn, masked, r >= 3)
                    if runs and runs[-1][0] == key:
                        runs[-1][2] = r + 1
                    else:
                        runs.append([key, r, r + 1])
                for (kn, masked, const), r0, r1 in runs:
                    src = self.psum[0:kn, sbk * 512 + r0 * 128: sbk * 512 + r1 * 128].rearrange("p (r i) -> p r i", i=128)[:, :, 0:qn]
                    dst = PT[0:kn, r0 * 128:r1 * 128].rearrange("p (r i) -> p r i", i=128)[:, :, 0:qn]
                    if masked:
                        self.act(dst, src, AF.Exp, reads=[self.B_par], writes=[BPT], banks=self.bk(sbk, 2), scale=0.125,
                                 bias=self.flg[0:kn, 1:2])
                    elif const:
                        self.act(dst, src, AF.Exp, reads=[self.B_tab], writes=[BPT], banks=self.bk(sbk, 2), scale=0.125,
                                 bias=self.bconst[0:kn, h:h + 1])
                    else:
                        self.act(dst, src, AF.Exp, writes=[BPT], banks=self.bk(sbk, 2), scale=0.125)
                    if not const:
                        tb = self.tab[0:kn, h, r0 * 128:r1 * 128].rearrange("p (r i) -> p r i", i=128)[:, :, 0:qn]
                        self.tt("dve", dst, dst, tb, ALU.mult, reads=[self.B_tab, BPT], writes=[BPT])
                if qn > 64 and kinfo[4][3] == 128:
                    self.memset("pool", PT[0:64, 4 * 128 + 64: 4 * 128 + qn], 0.0, writes=[BPT])
                obank = ob + h // 7
                oc = (h % 7) * 65
                for r in range(5):
                    kcol, vblk, vsh, kn, kslot = kinfo[r]
                    self.mm(self.psum[0:qn, obank * 512 + oc: obank * 512 + oc + 65], PT[0:kn, r * 128:r * 128 + qn],
                            self.Vr[0:kn, vblk, h * 65:(h + 1) * 65], r == 0, r == 4,
                            reads=[BPT, self.B_V[vsh[0]][vsh[1]]], banks=self.bk(obank))
            for bi, (h0, nh) in enumerate([(0, 7), (7, 7), (14, 2)]):
                o3 = self.psum[0:qn, (ob + bi) * 512: (ob + bi) * 512 + nh * 65].rearrange("p (h c) -> p h c", c=65)
                rec_o, rec_i = self.rec[0:qn, h0:h0 + nh].unsqueeze(2), o3[:, :, 64:65]
                self.S.op("dve", lambda e, rec_o=rec_o, rec_i=rec_i: e.reciprocal(out=rec_o, in_=rec_i),
                          writes=[self.B_rec], banks=self.bk(ob + bi), cost=200.0)
                self.tt("dve", self.Osb[0:qn, h0 * 64:(h0 + nh) * 64].rearrange("p (h c) -> p h c", c=64), o3[:, :, 0:64],
                        self.rec[0:qn, h0:h0 + nh].unsqueeze(2).to_broadcast([qn, nh, 64]), ALU.mult,
                        reads=[self.B_rec], writes=[self.B_Osb], banks=self.bk(ob + bi))
            self.bank_unpin(ob, 3)
            self.tt("pool", self.Osb[0:qn, :], self.Osb[0:qn, :], self.zatt[0:qn, p, :], ALU.mult,
                    reads=[self.B_Osb, self.B_zatt[p]], writes=[self.B_Osb])
            bank = self.bank_get()
            for kc in range(8):
                self.tp(self.PSB(bank, 1, kc * 128, kc * 128 + qn), self.Osb[0:qn, kc * 128:(kc + 1) * 128],
                        self.identb[0:qn, 0:qn], reads=[self.B_Osb, self.B_cst], banks=self.bk(bank))
            self.cp("act", self.OgT[:, :, p * 128:p * 128 + qn],
                    self.PSB(bank, 1, 0, 1024).rearrange("p (k t) -> p k t", t=128)[:, :, 0:qn],
                    writes=[self.B_OgT], banks=self.bk(bank))

    def conv_tile(self, ct, bank, c0, Tn):
        if self.state_mode:
            i = self.rotn("convx", 5)
            raw, acc = self.crawx[i], self.caccx[i]
            Br, Ba = self.B_crawx[i], self.B_caccx[i]
        else:
            i = self.rotn("conv")
            raw, acc = self.craw[i], self.cacc[i]
            Br, Ba = self.B_craw[i], self.B_cacc[i]
        self.cp("pool", raw[:, 0:3], self.halo[:, ct, :], reads=[self.B_halo[ct]], writes=[Br])
        self.cp("act", raw[:, 3:3 + Tn], self.PS(bank, c0, c0 + Tn), writes=[Br], banks=self.bk(bank))
        self.cp("pool", self.halo[:, ct, :], raw[:, Tn:Tn + 3], reads=[Br], writes=[self.B_halo[ct]])
        self.act(acc[:, 0:Tn], raw[:, 0:Tn], AF.Identity, reads=[Br, self.B_par], writes=[Ba],
                 scale=self.convwT[:, 0, ct:ct + 1], bias=self.convbT[:, ct:ct + 1])
        for j in range(1, 4):
            self.stt(acc[:, 0:Tn], raw[:, j:j + Tn], self.convwT[:, j, ct:ct + 1], acc[:, 0:Tn], ALU.mult, ALU.add,
                     reads=[Br, Ba, self.B_par], writes=[Ba])
        if ct < 16:
            dst, Bd = self.xsT[:, ct, 0:Tn], self.B_xsT[ct]
        elif ct < 20:
            dst, Bd = self.BT[:, ct - 16, 0:Tn], self.B_BT[ct - 16]
        else:
            dst, Bd = self.CT[:, ct - 20, 0:Tn], self.B_CT[ct - 20]
        self.act(dst, acc[:, 0:Tn], AF.Silu, reads=[Ba], writes=[Bd])

    def ssd_block(self, b, bs, state_only):
        cst = self.cst
        nch = (bs + 63) // 64
        Lc = min(64, bs)
        t0 = 128 * b
        Bd = self.B_dts
        x = self.dtraw[0:bs, b, :]
        self.act(self.dtt[0:bs, :], x, AF.Abs, reads=[self.B_dtraw[b]], writes=[Bd])
        self.act(self.dtt[0:bs, :], self.dtt[0:bs, :], AF.Exp, reads=[Bd], writes=[Bd], scale=-1.0)
        self.act(self.dtt[0:bs, :], self.dtt[0:bs, :], AF.Ln, reads=[Bd], writes=[Bd], bias=1.0)
        self.stt(self.dt[0:bs, :], x, 0.0, self.dtt[0:bs, :], ALU.max, ALU.add, reads=[self.B_dtraw[b], Bd], writes=[Bd])
        self.tt("dve", self.dta[0:bs, :], self.dt[0:bs, :], self.arow[0:bs, :], ALU.mult, reads=[Bd, self.B_par], writes=[Bd])
        bk = self.bank_get()
        self.mm(self.PS(bk, 0, 32, 0, bs), cst[0:bs, C_TRI:C_TRI + bs], self.dta[0:bs, :], True, True, reads=[Bd, self.B_cst], banks=self.bk(bk))
        mrg = state_only and bs == 128
        if mrg:
            self.mm(self.PS(bk, 32, 64, 0, bs), cst[0:bs, C_STRI:C_STRI + bs], self.dta[0:bs, :], True, False, reads=[Bd, self.B_cst], banks=self.bk(bk))
            self.mm(self.PS(bk, 32, 64, 0, 64), cst[0:bs, C_CSEL + 128:C_CSEL + 192], self.dta[0:bs, :], False, True, reads=[Bd, self.B_cst], banks=self.bk(bk))
            for c in range(2):
                self.mm(self.PS(bk, 64, 96), cst[0:bs, C_CSEL + 128 * c:C_CSEL + 128 * c + 128], self.dta[0:bs, :],
                        c == 0, c == 1, reads=[Bd, self.B_cst], banks=self.bk(bk))
        else:
            self.mm(self.PS(bk, 32, 64, 0, bs), cst[0:bs, C_STRI:C_STRI + bs], self.dta[0:bs, :], True, True, reads=[Bd, self.B_cst], banks=self.bk(bk))
            for c in range(nch):
                self.mm(self.PS(bk, 64 + 32 * c, 96 + 32 * c), cst[0:bs, C_CSEL + 128 * c:C_CSEL + 128 * c + 128], self.dta[0:bs, :],
                        True, True, reads=[Bd, self.B_cst], banks=self.bk(bk))
        self.act(self.EW[0:bs, :], self.PS(bk, 0, 64, 0, bs), AF.Exp, writes=[Bd], banks=self.bk(bk))
        nea = 1 if mrg else nch
        self.act(self.EA[:, 0:32 * nea], self.PS(bk, 64, 64 + 32 * nea), AF.Exp, writes=[Bd], banks=self.bk(bk))
        E3 = self.EW[0:bs, 0:32].unsqueeze(2)
        W3 = self.EW[0:bs, 32:64].unsqueeze(2)
        dt3 = self.dt[0:bs, :].unsqueeze(2)
        xb = self.bank_get(2)
        for ct in range(16):
            self.tp(self.PSB(xb, 2, ct * 128, ct * 128 + 128, 0, bs), self.xsT[:, ct, t0:t0 + bs], self.identb[:, :],
                    reads=[self.B_xsT[ct], self.B_cst], banks=self.bk(xb + ct // 8))
        xs3 = self.PSB(xb, 2, 0, 2048, 0, bs).rearrange("p (h c) -> p h c", c=64)
        self.tt("dve", self.xdt[0:bs, :].rearrange("p (h c) -> p h c", c=64), xs3, dt3.to_broadcast([bs, 32, 64]), ALU.mult,
                reads=[Bd], writes=[self.B_xdt], banks=self.bk(xb, 2))
        if not state_only:
            self.tt("dve", self.xsD[0:bs, :].rearrange("p (h c) -> p h c", c=64), xs3,
                    self.rows[0:bs, 2, :].unsqueeze(2).to_broadcast([bs, 32, 64]), ALU.mult,
                    reads=[self.B_par], writes=[self.B_xsD], banks=self.bk(xb, 2))
        bb = self.bank_get()
        for g in range(4):
            self.tp(self.PSB(bb, 1, g * 128, g * 128 + 128, 0, bs), self.BT[:, g, t0:t0 + bs], self.identb[:, :],
                    reads=[self.B_BT[g], self.B_cst], banks=self.bk(bb))
        self.cp("act", self.Btok[0:bs, :], self.PSB(bb, 1, 0, 512, 0, bs), writes=[self.B_Btok], banks=self.bk(bb))

        if not state_only:
            cb = self.bank_get()
            for g in range(4):
                for c in range(nch):
                    cs = t0 + 64 * c
                    self.mm(self.PS(cb, g * 64, g * 64 + Lc, 64 * c, 64 * c + Lc), self.BT[:, g, cs:cs + Lc], self.CT[:, g, cs:cs + Lc],
                            True, True, reads=[self.B_BT[g], self.B_CT[g]], banks=self.bk(cb))
            self.tt("dve", self.cbm[0:bs, :].rearrange("p (g l) -> p g l", l=64)[:, :, 0:Lc],
                    self.PS(cb, 0, 256, 0, bs).rearrange("p (g l) -> p g l", l=64)[:, :, 0:Lc],
                    cst[0:bs, C_TRIL:C_TRIL + Lc].unsqueeze(1).to_broadcast([bs, 4, Lc]), ALU.mult,
                    reads=[self.B_cst], writes=[self.B_cbm], banks=self.bk(cb))
            for hh in range(2):
                R3 = self.R[0:bs, 0:16 * Lc].rearrange("p (h l) -> p h l", l=Lc)
                self.tt("dve", R3, cst[0:bs, C_TRIL:C_TRIL + Lc].unsqueeze(1).to_broadcast([bs, 16, Lc]),
                        self.dta[0:bs, 16 * hh:16 * hh + 16].unsqueeze(2).to_broadcast([bs, 16, Lc]), ALU.mult,
                        reads=[Bd, self.B_cst], writes=[self.B_R])
                sg = self.bank_get(2)
                ncol = 16 * Lc
                for j in range((ncol + 511) // 512):
                    cw = min(512, ncol - 512 * j)
                    self.mm(self.PS(sg + j, 0, cw, 0, bs), self.strib[0:bs, 0:bs], self.R[0:bs, 512 * j:512 * j + cw], True, True,
                            reads=[self.B_R, self.B_cst], banks=self.bk(sg + j))
                d3 = self.dec[0:bs, hh * 1024: hh * 1024 + ncol]
                self.act(d3, self.psum[0:bs, sg * 512: sg * 512 + ncol], AF.Exp, writes=[self.B_dec[hh]], banks=self.bk(sg, 2))
                d4 = d3.rearrange("p (g e l) -> p g e l", g=2, l=Lc)
                c4 = self.cbm[0:bs, :].rearrange("p (g l) -> p g l", l=64)[:, 2 * hh:2 * hh + 2, 0:Lc].unsqueeze(2).to_broadcast([bs, 2, 8, Lc])
                self.tt("dve", d4, d4, c4, ALU.mult, reads=[self.B_cbm, self.B_dec[hh]], writes=[self.B_dec[hh]])

        self.tt("pool", self.xdtw[0:bs, :].rearrange("p (h c) -> p h c", c=64), self.xdt[0:bs, :].rearrange("p (h c) -> p h c", c=64),
                W3.to_broadcast([bs, 32, 64]), ALU.mult, reads=[Bd, self.B_xdt], writes=list(self.B_xdtw))
        for gh in ((0, 1), (2, 3)):
            ysb = None
            if not state_only:
                ysb = self.bank_pin(2)
                for g in gh:
                    ysg = ysb + g - gh[0]
                    yi = self.bank_get()
                    hh = g // 2
                    for c in range(nch):
                        r0 = 64 * c
                        for e in range(8):
                            h = 8 * g + e
                            hl = h - 16 * hh
                            self.mm(self.PS(yi, e * 64, e * 64 + 64, r0, r0 + Lc),
                                    self.dec[r0:r0 + Lc, hh * 1024 + hl * Lc: hh * 1024 + hl * Lc + Lc],
                                    self.xdt[r0:r0 + Lc, h * 64:(h + 1) * 64], True, True,
                                    reads=[self.B_dec[hh], self.B_xdt], banks=self.bk(yi))
                        if c == 0:
                            self.mm(self.PS(ysg, 0, 512, 0, Lc), self.CT[:, g, t0:t0 + Lc], self.Hbf[:, g * 512:(g + 1) * 512],
                                    True, True, reads=[self.B_CT[g], self.B_Hbf[g]], banks=self.bk(ysg))
                    self.tt("dve", self.y[0:bs, g * 512:(g + 1) * 512], self.PS(yi, 0, 512, 0, bs), self.xsD[0:bs, g * 512:(g + 1) * 512],
                            ALU.add, reads=[self.B_xsD], writes=[self.B_y[g]], banks=self.bk(yi))
            for c in range(1 if mrg else nch):
                r0 = 64 * c
                if mrg:
                    Lc = 128
                for g in gh:
                    if (not state_only) and c > 0:
                        ysg = ysb + g - gh[0]
                        self.mm(self.PS(ysg, 0, 512, r0, r0 + Lc), self.CT[:, g, t0 + r0:t0 + r0 + Lc], self.Hbf[:, g * 512:(g + 1) * 512],
                                True, True, reads=[self.B_CT[g], self.B_Hbf[g]], banks=self.bk(ysg))
                    dh = self.bank_get()
                    self.mm(self.PS(dh, 0, 512), self.Btok[r0:r0 + Lc, g * 128:(g + 1) * 128], self.xdtw[r0:r0 + Lc, g * 512:(g + 1) * 512],
                            True, True, reads=[self.B_Btok, self.B_xdtw[g]], banks=self.bk(dh))
                    Hg = self.H[:, g * 512:(g + 1) * 512]
                    H3 = Hg.rearrange("p (h c) -> p h c", c=64)
                    self.tt("pool" if state_only else "dve", H3, H3,
                            self.EA[:, 32 * c + 8 * g:32 * c + 8 * g + 8].unsqueeze(2).to_broadcast([128, 8, 64]), ALU.mult,
                            reads=[Bd, self.B_H[g]], writes=[self.B_H[g]])
                    self.tt("dve", Hg, Hg, self.PS(dh, 0, 512), ALU.add, reads=[self.B_H[g]], writes=[self.B_H[g]], banks=self.bk(dh))
                    if not state_only:
                        self.cp("act", self.Hbf[:, g * 512:(g + 1) * 512], Hg, reads=[self.B_H[g]], writes=[self.B_Hbf[g]])
            if state_only:
                continue
            for g in gh:
                ysg = ysb + g - gh[0]
                i = self.rotn("ytmp")
                self.tt("dve", self.ytmp[i][0:bs, :].rearrange("p (h c) -> p h c", c=64),
                        self.PS(ysg, 0, 512, 0, bs).rearrange("p (h c) -> p h c", c=64),
                        E3[:, 8 * g:8 * g + 8, :].to_broadcast([bs, 8, 64]), ALU.mult,
                        reads=[Bd], writes=[self.B_ytmp[i]], banks=self.bk(ysg))
                yg = self.y[0:bs, g * 512:(g + 1) * 512]
                self.tt("dve", yg, yg, self.ytmp[i][0:bs, :], ALU.add, reads=[self.B_ytmp[i], self.B_y[g]], writes=[self.B_y[g]])
                self.tt("pool", yg, yg, self.zssd[0:bs, b, g * 512:(g + 1) * 512], ALU.mult, reads=[self.B_zssd[b], self.B_y[g]],
                        writes=[self.B_y[g]])
                self.act(self.ynorm[0:bs, g * 512:(g + 1) * 512], yg, AF.Square, reads=[self.B_y[g]], writes=[self.B_ynorm[g], self.B_ss],
                         accum_out=self.ss[0:bs, g:g + 1])
            self.bank_unpin(ysb, 2)
        if state_only:
            return
        self.act(self.ss[0:bs, 0:4], self.ss[0:bs, 0:4], AF.Ln, reads=[self.B_ss], writes=[self.B_ss], scale=1.0 / 512, bias=EPS)
        self.act(self.ss[0:bs, 0:4], self.ss[0:bs, 0:4], AF.Exp, reads=[self.B_ss], writes=[self.B_ss], scale=-0.5)
        for g in range(4):
            self.act(self.ynorm[0:bs, g * 512:(g + 1) * 512], self.y[0:bs, g * 512:(g + 1) * 512], AF.Copy,
                     reads=[self.B_y[g], self.B_ss], writes=[self.B_ynorm[g]], scale=self.ss[0:bs, g:g + 1])
        yb = self.bank_get(2)
        for ct in range(16):
            self.tp(self.PSB(yb, 2, ct * 128, ct * 128 + bs), self.ynorm[0:bs, ct * 128:(ct + 1) * 128], self.identb[0:bs, 0:bs],
                    reads=[self.B_ynorm[ct // 4], self.B_cst], banks=self.bk(yb + ct // 8))
        self.tt("dve", self.yT[:, :, t0:t0 + bs], self.PSB(yb, 2, 0, 2048).rearrange("p (k t) -> p k t", t=128)[:, :, 0:bs],
                self.gssdT[:, :].unsqueeze(2).to_broadcast([128, 16, bs]), ALU.mult,
                reads=[self.B_par], writes=[self.B_yT[b]], banks=self.bk(yb, 2))

    def tile_ctx(self, Tn):
        nbk = (Tn + 127) // 128
        return nbk, [min(128, Tn - 128 * b) for b in range(nbk)]

    def p1(self, src_ap, row0, Tn, seq, slot, par, full=True, kv=True):
        nbk, bss = self.tile_ctx(Tn)
        self.pool = "R"
        self.S.tag = "norm"
        self.hT, self.B_hT = self.hT2[par], self.B_hT2[par]
        self.load_norm(src_ap, row0, Tn, seq, par=par)
        self.dump("hT", self.hT[:, :, 0:Tn], self.B_hT)
        self.S.tag = "qkv"
        if full:
            for j in range(2):
                wt, wbuf = self.wget("q%d" % j)
                for ctl in range(4):
                    self.qk_tile("q", wt, wbuf, ctl, 4 * j + ctl, Tn, slot)
        if full or kv:
            for j in range(2):
                wt, wbuf = self.wget("k%d" % j)
                for ctl in range(4):
                    self.qk_tile("k", wt, wbuf, ctl, 4 * j + ctl, Tn, slot)
            for j in range(2):
                wt, wbuf = self.wget("v%d" % j)
                for b in range(nbk):
                    bank = self.bank_get()
                    self.proj_tm(wt, wbuf, b, bss[b], 512, bank)
                    dst = self.Vr[0:bss[b], slot * 2 + b, j * 520:(j + 1) * 520].rearrange("p (h c) -> p h c", c=65)[:, :, 0:64]
                    self.cp("act", dst, self.PS(bank, 0, 512, 0, bss[b]).rearrange("p (h c) -> p h c", c=64),
                            writes=[self.B_V[slot][b]], banks=self.bk(bank))
        if full:
            for j in range(2):
                wt, wbuf = self.wget("za%d" % j)
                for b in range(nbk):
                    bank = self.bank_get()
                    self.proj_tm(wt, wbuf, b, bss[b], 512, bank)
                    self.act(self.zatt[0:bss[b], b, j * 512:(j + 1) * 512], self.PS(bank, 0, 512, 0, bss[b]), AF.Silu,
                             writes=[self.B_zatt[b]], banks=self.bk(bank))

    def sig_proj(self, name, Tn):
        for j in range(2):
            wt, wbuf = self.wget("%s%d" % (name, j))
            for c2 in range(2):
                bank = self.bank_get()
                for u in range(2):
                    self.proj_fm(wt, wbuf, 2 * c2 + u, Tn, bank, u * T)
                for u in range(2):
                    dtile = 4 * j + 2 * c2 + u
                    self.act(self.sig[:, dtile, 0:Tn], self.PS(bank, u * T, u * T + Tn), AF.Sigmoid,
                             writes=[self.B_sig[dtile]], banks=self.bk(bank))

    def p2(self, Tn, slot, par, tile_idx, full=True, with_c=True):
        nbk, bss = self.tile_ctx(Tn)
        self.hT, self.B_hT = self.hT2[par], self.B_hT2[par]
        if full:
            self.S.tag = "attn"
            self.pool = "ALL"
            self.attention(Tn, slot, tile_idx)
            self.pool = "R"
            self.dump("OgT", self.OgT[:, :, 0:Tn], [self.B_OgT])
            self.S.tag = "attproj"
            self.sig_proj("ga", Tn)
            self.S.tag = "attproj"
            for j in range(2):
                wt, wbuf = self.wget("wa%d" % j)
                for c2 in range(2):
                    bank = self.bank_get()
                    for u in range(2):
                        for kc in range(8):
                            self.mm(self.PS(bank, u * T, u * T + Tn), wt[:, kc, (2 * c2 + u) * 128:(2 * c2 + u + 1) * 128],
                                    self.OgT[:, kc, 0:Tn], kc == 0, kc == 7, reads=[wbuf, self.B_OgT], banks=self.bk(bank))
                    for u in range(2):
                        dtile = 4 * j + 2 * c2 + u
                        self.tt("dve", self.pre[:, dtile, 0:Tn], self.PS(bank, u * T, u * T + Tn), self.sig[:, dtile, 0:Tn], ALU.mult,
                                reads=[self.B_sig[dtile]], writes=[self.B_pre[dtile]], banks=self.bk(bank))
        self.S.tag = "xbc"
        wt, wbuf = self.wget("dt")
        for b in range(nbk):
            bank = self.bank_get()
            self.proj_tm(wt, wbuf, b, bss[b], 32, bank)
            self.tt("dve", self.dtraw[0:bss[b], b, :], self.PS(bank, 0, 32, 0, bss[b]), self.rows[0:bss[b], 0, :], ALU.add,
                    reads=[self.B_par], writes=[self.B_dtraw[b]], banks=self.bk(bank))
        for j in range(6 if (full or with_c) else 5):
            wt, wbuf = self.wget("xb%d" % j)
            for c2 in range(2):
                bank = self.bank_get()
                for u in range(2):
                    self.proj_fm(wt, wbuf, 2 * c2 + u, Tn, bank, u * T)
                for u in range(2):
                    self.conv_tile(4 * j + 2 * c2 + u, bank, u * T, Tn)
        if full:
            for j in range(4):
                wt, wbuf = self.wget("zs%d" % j)
                for b in range(nbk):
                    bank = self.bank_get()
                    self.proj_tm(wt, wbuf, b, bss[b], 512, bank)
                    self.act(self.zssd[0:bss[b], b, j * 512:(j + 1) * 512], self.PS(bank, 0, 512, 0, bss[b]), AF.Silu,
                             writes=[self.B_zssd[b]], banks=self.bk(bank))
        self.S.tag = "ssd"
        self.pool = "L"
        for b in range(nbk):
            self.ssd_block(b, bss[b], not full)
        self.pool = "R"

    def p3(self, Tn, par, out, out_rows):
        nbk, bss = self.tile_ctx(Tn)
        self.hT, self.B_hT = self.hT2[par], self.B_hT2[par]
        self.pool = "R"
        self.S.tag = "tail"
        self.dump("yT", self.yT[:, :, 0:Tn], self.B_yT)
        self.sig_proj("gs", Tn)
        for j in range(4):
            wt, wbuf = self.wget("ws%d" % j)
            bank = self.bank_get()
            for u in range(2):
                for kc in range(16):
                    self.mm(self.PS(bank, u * T, u * T + Tn), wt[:, kc, u * 128:(u + 1) * 128], self.yT[:, kc, 0:Tn],
                            kc == 0, kc == 15, reads=[wbuf] + self.B_yT, banks=self.bk(bank))
            for u in range(2):
                dtile = 2 * j + u
                i = self.rotn("tmpm", 1)
                self.tt("dve", self.tmpm[i][:, 0:Tn], self.PS(bank, u * T, u * T + Tn), self.sig[:, dtile, 0:Tn], ALU.mult,
                        reads=[self.B_sig[dtile]], writes=[self.B_tmpm[i]], banks=self.bk(bank))
                self.tt("pool", self.pre[:, dtile, 0:Tn], self.pre[:, dtile, 0:Tn], self.tmpm[i][:, 0:Tn], ALU.add,
                        reads=[self.B_tmpm[i], self.B_pre[dtile]], writes=[self.B_pre[dtile]])
        self.dump("merged", self.pre[:, :, 0:Tn], self.B_pre)
        for j in range(2):
            wt, wbuf = self.wget("wo%d" % j)
            for b in range(nbk):
                bs = bss[b]
                bank = self.bank_get()
                for kc in range(8):
                    self.mm(self.PS(bank, 0, 512, 0, bs), self.pre[:, kc, 128 * b:128 * b + bs], wt[:, kc, 0:512], kc == 0, kc == 7,
                            reads=[wbuf, self.B_pre[kc]], banks=self.bk(bank))
                xb_ = 2 * par + b
                oc = self.xt[0:bs, xb_, j * 512:(j + 1) * 512]
                pso = self.PS(bank, 0, 512, 0, bs)
                self.tt("dve", pso, pso, self.gate_tok[0:bs, j * 512:(j + 1) * 512], ALU.mult,
                        reads=[self.B_gate], banks=self.bk(bank))
                self.tt("dve", oc, oc, pso, ALU.add, reads=[self.B_xt[xb_]], writes=[self.B_xt[xb_]], banks=self.bk(bank))
        for b in range(nbk):
            bs = bss[b]
            xb_ = 2 * par + b
            self.dma("sp", out.ap()[out_rows + 128 * b: out_rows + 128 * b + bs, :], self.xt[0:bs, xb_, :], self.d_xo[xb_],
                     reads=[self.B_xt[xb_]], is_output=True)

    def emit_kv_out(self, slot, Tn, nk, nv, row0):
        nbk = (Tn + 127) // 128
        for b in range(nbk):
            bs = min(128, Tn - 128 * b)
            bank = self.bank_get()
            for hp in range(8):
                self.tp(self.PSB(bank, 1, hp * 128, hp * 128 + 128, 0, bs), self.kT[:, hp, slot * T + 128 * b: slot * T + 128 * b + bs],
                        self.identb[:, :], reads=[self.B_kT[slot][hp], self.B_cst], banks=self.bk(bank))
            i = self.rotn("kvo")
            self.cp("dve", self.ost[i][0:bs, :], self.PSB(bank, 1, 0, 1024, 0, bs), writes=[self.B_ost[i]], banks=self.bk(bank))
            self.dma("sp", nk.ap()[row0 + 128 * b: row0 + 128 * b + bs, :], self.ost[i][0:bs, :], self.d_ost[i],
                     reads=[self.B_ost[i]], is_output=True)
            i = self.rotn("kvo")
            self.cp("act", self.ost[i][0:bs, :].rearrange("p (h c) -> p h c", c=64),
                    self.Vr[0:bs, slot * 2 + b, :].rearrange("p (h c) -> p h c", c=65)[:, :, 0:64],
                    reads=[self.B_V[slot][b]], writes=[self.B_ost[i]])
            self.dma("sp", nv.ap()[row0 + 128 * b: row0 + 128 * b + bs, :], self.ost[i][0:bs, :], self.d_ost[i],
                     reads=[self.B_ost[i]], is_output=True)

    def emit_conv_out(self, out):
        cst = self.cst
        for q3 in range(3):
            bank = self.bank_get(2)
            for u in range(8):
                t = 8 * q3 + u
                self.tp(self.psum[0:3, bank * 512 + u * 128: bank * 512 + u * 128 + 128], self.halo[:, t, :], cst[:, C_ID:C_ID + 128],
                        reads=[self.B_halo[t], self.B_cst], banks=self.bk(bank + u // 4))
            i = self.rotn("kvo")
            self.cp("dve", self.ost[i][0:3, :], self.psum[0:3, bank * 512: bank * 512 + 1024], writes=[self.B_ost[i]], banks=self.bk(bank, 2))
            self.dma("sp", out.ap()[:, q3 * 1024:(q3 + 1) * 1024], self.ost[i][0:3, :], self.d_ost[i], reads=[self.B_ost[i]], is_output=True)

    def emit_ssm_out(self, out):
        cst = self.cst
        for q4 in range(4):
            bank = self.bank_get()
            for u in range(4):
                t = 4 * q4 + u
                self.tp(self.PS(bank, u * 128, u * 128 + 128), self.H[:, t * 128:(t + 1) * 128], cst[:, C_ID:C_ID + 128],
                        reads=[self.B_H[t // 4], self.B_cst], banks=self.bk(bank))
            self.cp("dve", self.y[:, q4 * 512:(q4 + 1) * 512], self.PS(bank, 0, 512), writes=[self.B_y[q4]], banks=self.bk(bank))
        self.dma("sp", out.ap().rearrange("(t p) n -> p t n", p=128), self.y[:].rearrange("p (t n) -> p t n", n=128), self.d_sso,
                 reads=self.B_y, is_output=True)

    def zero_state(self):
        self.memset("pool", self.H[:], 0.0, writes=self.B_H)
        self.memset("pool", self.Hbf[:], 0.0, writes=self.B_Hbf)
        self.memset("pool", self.halo[:], 0.0, writes=self.B_halo)

    def sample_tile(self):
        I, O, cst = self.I, self.O, self.cst
        self.dma("sp", self.y[:].rearrange("p (t n) -> p t n", n=128), I["state_ssm"].ap().rearrange("(t p) n -> p t n", p=128),
                 self.d_misc, writes=self.B_y)
        for q4 in range(4):
            bank = self.bank_get()
            for u in range(4):
                t = 4 * q4 + u
                self.tp(self.PS(bank, u * 128, u * 128 + 128), self.y[:, t * 128:(t + 1) * 128], cst[:, C_ID:C_ID + 128],
                        reads=[self.B_y[t // 4], self.B_cst], banks=self.bk(bank))
            self.cp("dve", self.H[:, q4 * 512:(q4 + 1) * 512], self.PS(bank, 0, 512), writes=[self.B_H[q4]], banks=self.bk(bank))
            self.cp("act", self.Hbf[:, q4 * 512:(q4 + 1) * 512], self.H[:, q4 * 512:(q4 + 1) * 512], reads=[self.B_H[q4]],
                    writes=[self.B_Hbf[q4]])
        for j_ in range(3):
            self.dma("sp", self.halo[:, :, j_], I["state_conv"].ap()[j_, :].rearrange("(t p) -> p t", p=128), self.d_misc,
                     writes=self.B_halo, slow=True)
        self.S.seal(self.d_misc, list(self.B_y) + list(self.B_halo))
        self.load_norm(I["cache_k"].ap(), 0, 256, 1, plain_kslot={0: (0, 0), 1: (0, 1)})
        self.load_norm(I["cache_k"].ap(), 256, 256, 1, plain_kslot={0: (1, 0), 1: (1, 1)})
        for blk in range(4):
            i = blk % 2
            self.dma("sp", self.ost[i][:, :], I["cache_v"].ap()[128 * blk:128 * blk + 128, :], self.d_ost[i], writes=[self.B_ost[i]])
            self.cp("dve", self.Vr[:, blk, :].rearrange("p (h c) -> p h c", c=65)[:, :, 0:64],
                    self.ost[i][:, :].rearrange("p (h c) -> p h c", c=64), reads=[self.B_ost[i]], writes=[self.B_V[blk // 2][blk % 2]])
        self.make_gate_tok(1)
        self.p1(I["x_s"].ap(), 0, TS, 1, 2, 0)
        self.p2(TS, 2, 0, None)
        self.p3(TS, 0, O["y_s"], 0)
        self.emit_kv_out(2, TS, O["nk_s"], O["nv_s"], 0)
        self.emit_conv_out(O["nconv_s"])
        self.emit_ssm_out(O["nssm_s"])

    def state_pass(self):
        I = self.I
        self.zero_state()
        n = self.n_state
        xb_ = list(self.B_crawx[2:]) + list(self.B_caccx[2:])
        self.memset("pool", self.y[:, 2047:2048], 0.0, writes=list(self.B_y) + xb_)
        self.state_mode = True
        for i in range(n):
            kv = i >= n - 2
            slot = 1 if i == n - 2 else (2 if i == n - 1 else 0)
            par = i % 2
            self.p1(I["x_prev"].ap(), i * T, T, 0, slot, par, full=False, kv=kv)
            self.p2(T, slot, par, None, full=False, with_c=(i == n - 1))
        self.state_mode = False
        self.memset("pool", self.y[:, 2047:2048], 0.0, writes=list(self.B_y) + xb_)
        f = self.flg[:, 0:1]
        for g in range(4):
            Hg = self.H[:, g * 512:(g + 1) * 512]
            self.ts("pool", Hg, Hg, f, None, ALU.mult, reads=[self.B_par, self.B_H[g]], writes=[self.B_H[g]])
            self.cp("act", self.Hbf[:, g * 512:(g + 1) * 512], Hg, reads=[self.B_H[g]], writes=[self.B_Hbf[g]])
        self.ts("pool", self.halo[:], self.halo[:], f, None, ALU.mult, reads=[self.B_par] + list(self.B_halo), writes=self.B_halo)

    def main_pass(self):
        I, O = self.I, self.O
        self.make_gate_tok(0)
        n = self.n_main
        if n > 0:
            self.p1(I["x_main"].ap(), 0, T, 0, 0, 0)
        for t in range(n):
            self.p2(T, t % 3, t % 2, t)
            if t + 1 < n:
                self.p1(I["x_main"].ap(), (t + 1) * T, T, 0, (t + 1) % 3, (t + 1) % 2)
            self.p3(T, t % 2, O["y_main"], t * T)
        if n >= 2:
            self.emit_kv_out((n - 2) % 3, T, O["nk"], O["nv"], 0)
        self.emit_kv_out((n - 1) % 3, T, O["nk"], O["nv"], 256)
        self.emit_conv_out(O["nconv"])
        self.emit_ssm_out(O["nssm"])


def make_consts():
    c = np.zeros((128, NCONST), np.float32)
    s = np.arange(128)[:, None]
    l = np.arange(128)[None, :]
    same = (s // 64) == (l // 64)
    c[:, C_ID:C_ID + 128] = np.eye(128)
    c[:, C_J:C_J + 128] = np.eye(128)[::-1]
    c[:, C_TRI:C_TRI + 128] = (same & (s <= l))
    c[:, C_STRI:C_STRI + 128] = (same & (s > l))
    for ch in range(2):
        c[:, C_CSEL + 128 * ch:C_CSEL + 128 * ch + 128] = ((s // 64) == ch)
    c[:, C_TRIL:C_TRIL + 64] = ((s % 64) <= np.arange(64)[None, :])
    c[:, C_BONES:C_BONES + 128] = same
    return c


_CACHE = {}


def get_program(key=("full",), **kw):
    if key not in _CACHE:
        b = Builder(**{k: v for k, v in kw.items() if k == "dbg"})
        nc = b.build(**{k: v for k, v in kw.items() if k != "dbg"})
        _CACHE[key] = (nc, b)
    return _CACHE[key]


def make_in_maps(inp):
    f = lambda a: np.ascontiguousarray(np.asarray(a, dtype=np.float32))
    xp = f(inp["x_prompt"])
    consts = make_consts()
    shared = {
        "consts": consts,
        "norm_g": f(inp["norm_g"][0]), "w_ada": f(inp["w_ada"][0]), "b_ada": f(inp["b_ada"][0]), "w_in": f(inp["w_in"][0]),
        "q_norm_g": f(inp["q_norm_g"][0]), "k_norm_g": f(inp["k_norm_g"][0]), "rel_bias": f(inp["rel_bias"][0]),
        "w_att": f(inp["w_att_proj"][0]), "conv_w": f(inp["conv_w"][0]), "conv_b": f(inp["conv_b"][0]),
        "dt_bias": f(inp["dt_bias"][0]), "a_log": f(inp["a_log"][0]), "d_skip": f(inp["d_skip"][0]),
        "ssd_norm_g": f(inp["ssd_norm_g"][0]), "w_ssd": f(inp["w_ssd_proj"][0]), "w_out": f(inp["w_out"][0]),
    }
    maps = []
    for c in range(8):
        b, half = c // 2, c % 2
        flags = np.zeros((128, 2), np.float32)
        flags[:, 0] = float(half)
        flags[:, 1] = 0.0 if half else -30000.0
        m = dict(shared)
        m.update({
            "x_main": f(xp[b, half * 4096:(half + 1) * 4096]),
            "x_prev": f(xp[b, 0:4096]),
            "x_s": f(inp["x_sample"][c]),
            "c2": f(np.stack([np.asarray(inp["c_prompt"])[b], np.asarray(inp["c_sample"])[c]])),
            "cache_k": f(np.asarray(inp["cache_k"])[0, c].reshape(512, 1024)),
            "cache_v": f(np.asarray(inp["cache_v"])[0, c].reshape(512, 1024)),
            "state_conv": f(np.asarray(inp["state_conv"])[0, c]),
            "state_ssm": f(np.asarray(inp["state_ssm"])[0, c].reshape(2048, 128)),
            "flags": flags,
        })
        maps.append(m)
    return maps


def assemble(res):
    R = res
    y_prompt = np.zeros((4, 8192, 1024), np.float32)
    y_sample = np.zeros((8, 16, 1024), np.float32)
    nkp = np.zeros((1, 4, 512, 16, 64), np.float32)
    nvp = np.zeros((1, 4, 512, 16, 64), np.float32)
    ncp = np.zeros((1, 4, 3, 3072), np.float32)
    nhp = np.zeros((1, 4, 32, 64, 128), np.float32)
    nks = np.zeros((1, 8, 16, 16, 64), np.float32)
    nvs = np.zeros((1, 8, 16, 16, 64), np.float32)
    ncs = np.zeros((1, 8, 3, 3072), np.float32)
    nhs = np.zeros((1, 8, 32, 64, 128), np.float32)
    for c in range(8):
        b, half = c // 2, c % 2
        r = R[c]
        y_prompt[b, half * 4096:(half + 1) * 4096] = r["y_main"]
        y_sample[c] = r["y_s"]
        if half == 1:
            nkp[0, b] = r["nk"].reshape(512, 16, 64)
            nvp[0, b] = r["nv"].reshape(512, 16, 64)
            ncp[0, b] = r["nconv"]
            nhp[0, b] = r["nssm"].reshape(32, 64, 128)
        nks[0, c] = r["nk_s"].reshape(16, 16, 64)
        nvs[0, c] = r["nv_s"].reshape(16, 16, 64)
        ncs[0, c] = r["nconv_s"]
        nhs[0, c] = r["nssm_s"].reshape(32, 64, 128)
    return (y_prompt, y_sample, nkp, nvp, ncp, nhp, nks, nvs, ncs, nhs)


def kernel(**inputs):
    nc, _ = get_program()
    maps = make_in_maps(inputs)
    res = run_bass_kernel_spmd(nc, maps, core_ids=list(range(8)))
    return assemble(res.results)
```

```python
import numpy as np
from contextlib import ExitStack
import concourse.bass as bass
import concourse.mybir as mybir
from concourse.bass_utils import run_bass_kernel_spmd

F32 = mybir.dt.float32
BF = mybir.dt.bfloat16
AF = mybir.ActivationFunctionType
ALU = mybir.AluOpType
AX = mybir.AxisListType

D = 1024
KC = 8
T = 256
NT = 16
TS = 16
IN_DIM = 11296
EPS = 1e-6
C_ID, C_J, C_TRI, C_STRI, C_CSEL, C_TRIL, C_BONES = 0, 128, 256, 384, 512, 768, 832
NCONST = 960


class Buf:
    __slots__ = ("name", "w", "r", "acc")

    def __init__(self, name):
        self.name = name
        self.w = []
        self.r = []
        self.acc = {}


class DSem:
    __slots__ = ("key", "ops")

    def __init__(self, key):
        self.key = key
        self.ops = []


class _Op:
    __slots__ = ("idx", "eng", "fn", "preds", "succs", "cost", "dsem", "nbytes", "is_output", "seq", "cum",
                 "npred", "ready", "fin", "tag", "start", "blame", "gap", "aset")


class Sched:
    ENGS = ("pe", "act", "dve", "pool", "sp")
    XLAT = 250.0
    SLAT = 100.0
    STARVE = 8000.0

    def __init__(self, same_engine_sync=True, reorder=True):
        self.ops = []
        self.dsems = {}
        self.same_engine_sync = same_engine_sync
        self.reorder = reorder
        self.tag = ""

    def buf(self, name):
        return Buf(name)

    def bufs(self, name, n):
        return [Buf("%s%d" % (name, i)) for i in range(n)]

    def dsem(self, name):
        d = DSem("D_" + name)
        self.dsems[d.key] = d
        return d

    def _add(self, eng, fn, reads, writes, banks, cost, dsem=None, nbytes=0, is_output=False, aset=None):
        o = _Op()
        o.aset = aset
        o.idx = len(self.ops)
        o.eng, o.fn, o.cost, o.dsem, o.nbytes, o.is_output = eng, fn, cost, dsem, nbytes, is_output
        o.succs = []
        o.tag = self.tag
        o.blame = None
        p = set()
        for b in banks:
            p.update(b.acc.values())
            b.acc[eng] = o.idx
        for b in reads:
            p.update(b.w)
        for b in writes:
            p.update(b.w)
            p.update(b.r)
        p.discard(o.idx)
        o.preds = p
        for b in reads:
            b.r.append(o.idx)
        for b in writes:
            b.w = [o.idx]
            b.r = []
        self.ops.append(o)
        if dsem is not None:
            dsem.ops.append(o.idx)
        return o

    def op(self, engname, fn, reads=(), writes=(), banks=(), cost=100.0, aset=None):
        self._add(engname, fn, reads, writes, banks, cost, aset=aset)

    def dma(self, qname, fn, dsem, reads=(), writes=(), is_output=False, nbytes=0):
        self._add(qname, fn, reads, writes, (), 60.0, dsem=dsem, nbytes=nbytes, is_output=is_output)

    def seal(self, dsem, bufs):
        for b in bufs:
            b.w = list(dsem.ops)
            b.r = []

    def _order(self):
        ops = self.ops
        n = len(ops)
        for o in ops:
            o.npred = len(o.preds)
            o.ready = 0.0
            for p in o.preds:
                ops[p].succs.append(o.idx)
        if not self.reorder:
            return {e: [o.idx for o in ops if o.eng == e] for e in self.ENGS}
        ready = {e: [] for e in self.ENGS}
        for o in ops:
            if o.npred == 0:
                ready[o.eng].append(o.idx)
        free = {e: 0.0 for e in self.ENGS}
        order = {e: [] for e in self.ENGS}
        dma_pipe = 0.0
        cur_set = None
        self.n_switch = 0
        done = 0
        WIN = 4000
        oldest = 0
        sched = [False] * n
        while done < n:
            best = None
            while oldest < n and sched[oldest]:
                oldest += 1
            for e in self.ENGS:
                r = ready[e]
                if not r:
                    continue
                f = free[e]
                cand = None
                soon = None
                for i in r:
                    if i > oldest + WIN:
                        continue
                    o = ops[i]
                    if o.ready <= f:
                        if cand is None or i < cand:
                            cand = i
                    elif soon is None or o.ready < ops[soon].ready or (o.ready == ops[soon].ready and i < soon):
                        soon = i
                if e == "act" and cand is not None and cur_set is not None:
                    oa = ops[cand].aset
                    if oa is not None and oa != cur_set and f - ops[cand].ready < self.STARVE:
                        alt = None
                        for i in r:
                            if i > oldest + WIN:
                                continue
                            o2 = ops[i]
                            if o2.ready <= f and (o2.aset is None or o2.aset == cur_set) and (alt is None or i < alt):
                                alt = i
                        if alt is not None:
                            cand = alt
                pick = cand if cand is not None else soon
                if pick is None:
                    continue
                st = max(f, ops[pick].ready)
                if best is None or st < best[0] or (st == best[0] and pick < best[2]):
                    best = (st, e, pick)
            if best is None:
                WIN *= 2
                continue
            st, e, i = best
            o = ops[i]
            ready[e].remove(i)
            sched[i] = True
            o.start = st
            o.gap = st - free[e]
            if o.dsem is not None:
                t0 = max(st + o.cost, dma_pipe)
                dma_pipe = t0 + o.nbytes / 280.0
                o.fin = dma_pipe + 1800.0
                free[e] = st + o.cost
            else:
                sw = 0.0
                if e == "act" and o.aset is not None and o.aset != cur_set:
                    sw = 1300.0
                    cur_set = o.aset
                    self.n_switch += 1
                o.fin = st + o.cost + sw
                free[e] = o.fin
            order[e].append(i)
            done += 1
            for s_ in o.succs:
                so = ops[s_]
                if so.eng == e and o.dsem is None:
                    lat = 0.0 if e == "pe" else self.SLAT
                else:
                    lat = self.XLAT
                if o.fin + lat > so.ready:
                    so.ready = o.fin + lat
                    so.blame = o.idx
                so.npred -= 1
                if so.npred == 0:
                    ready[so.eng].append(s_)
        self.sim_ns = max(o.fin for o in ops)
        return order

    def finish(self):
        self.final_order = self._order()

    def emit(self, nc, stack):
        ops = self.ops
        order = self.final_order
        ekey = {e: "E_" + e for e in self.ENGS}
        sems = {}
        for e in self.ENGS:
            sems[ekey[e]] = stack.enter_context(nc.semaphore("s_" + e))
        for k in self.dsems:
            sems[k] = stack.enter_context(nc.semaphore("s_" + k))
        cnt = {e: 0 for e in self.ENGS}
        dcnt = {k: 0 for k in self.dsems}
        for e in self.ENGS:
            for i in order[e]:
                o = ops[i]
                if o.dsem is not None:
                    dcnt[o.dsem.key] += 16
                    o.cum = dcnt[o.dsem.key]
                    o.seq = None
                else:
                    cnt[e] += 1
                    o.seq = cnt[e]
        progs = {}
        out_ev = {}
        for e in self.ENGS:
            waited = {}
            prog = []
            for i in order[e]:
                o = ops[i]
                need = {}
                for p in o.preds:
                    po = ops[p]
                    if po.dsem is not None:
                        k, v = po.dsem.key, po.cum
                    else:
                        if po.eng == e and (e == "pe" or e in NO_SELF_SYNC or not self.same_engine_sync):
                            continue
                        k, v = ekey[po.eng], po.seq
                    if need.get(k, 0) < v:
                        need[k] = v
                waits = []
                for k, v in need.items():
                    if waited.get(k, 0) >= v:
                        continue
                    waited[k] = v
                    waits.append((k, v))
                if o.dsem is not None:
                    prog.append((waits, o.fn, (o.dsem.key, 16)))
                    if o.is_output:
                        out_ev[o.dsem.key] = max(out_ev.get(o.dsem.key, 0), o.cum)
                else:
                    prog.append((waits, o.fn, (ekey[e], 1)))
            progs[e] = (prog, waited)
        prog, waited = progs["sp"]
        waits = [(k, v) for k, v in out_ev.items() if waited.get(k, 0) < v]
        for e in self.ENGS:
            if e != "sp" and cnt[e] > 0:
                waits.append((ekey[e], cnt[e]))
        prog.append((waits, None, None))

        def replay(prog):
            def run(eng):
                for waits, fn, inc in prog:
                    for s, v in waits:
                        eng.wait_ge(sems[s], v)
                    if fn is not None:
                        fn(eng).then_inc(sems[inc[0]], inc[1])
            return run

        with nc.Block() as block:
            block.tensor(replay(progs["pe"][0]))
            block.scalar(replay(progs["act"][0]))
            block.vector(replay(progs["dve"][0]))
            block.gpsimd(replay(progs["pool"][0]))
            block.sync(replay(progs["sp"][0]))


WB = {}
for _i in range(2):
    WB["q%d" % _i] = ("w_in", 8, 0 + 512 * _i, 512)
    WB["k%d" % _i] = ("w_in", 8, 1024 + 512 * _i, 512)
    WB["v%d" % _i] = ("w_in", 8, 2048 + 512 * _i, 512)
    WB["za%d" % _i] = ("w_in", 8, 3072 + 512 * _i, 512)
    WB["ga%d" % _i] = ("w_in", 8, 9248 + 512 * _i, 512)
    WB["gs%d" % _i] = ("w_in", 8, 10272 + 512 * _i, 512)
    WB["wa%d" % _i] = ("w_att", 8, 512 * _i, 512)
    WB["wo%d" % _i] = ("w_out", 8, 512 * _i, 512)
for _i in range(4):
    WB["zs%d" % _i] = ("w_in", 8, 4096 + 512 * _i, 512)
    WB["ws%d" % _i] = ("w_ssd", 16, 256 * _i, 256)
for _i in range(6):
    WB["xb%d" % _i] = ("w_in", 8, 6144 + 512 * _i, 512)
WB["dt"] = ("w_in", 8, 9216, 32)

for _i in range(12):
    WB["ad%d" % _i] = ("w_ada", 8, 256 * _i, 256)
ADA_BLOCKS = ["ad%d" % i for i in range(12)]
MAIN_BLOCKS = (["q0", "q1", "k0", "k1", "v0", "v1", "za0", "za1", "ga0", "ga1", "wa0", "wa1", "dt"]
               + ["xb%d" % i for i in range(6)] + ["zs%d" % i for i in range(4)]
               + ["gs0", "gs1"] + ["ws%d" % i for i in range(4)] + ["wo0", "wo1"])
STATE_BLOCKS = ["dt"] + ["xb%d" % i for i in range(5)]
STATE_KV_BLOCKS = ["k0", "k1", "v0", "v1"] + STATE_BLOCKS
STATE_LAST_BLOCKS = STATE_KV_BLOCKS + ["xb5"]
NW = 3
SAME_ENGINE_SYNC = True
NO_SELF_SYNC = ()


class Builder:
    def __init__(self, dbg=()):
        self.dbg = set(dbg)
        self.dbg_out = {}
        self.nc = bass.Bass("TRN2", target_bir_lowering=False)
        self.S = Sched(same_engine_sync=SAME_ENGINE_SYNC)
        self.st = ExitStack()

    def sb(self, name, shape, dt):
        return self.st.enter_context(self.nc.sbuf_tensor(name, shape, dt))

    def din(self, name, shape, dt=F32):
        return self.nc.dram_tensor(name, shape, dt, kind="ExternalInput")

    def dout(self, name, shape, dt=F32):
        return self.nc.dram_tensor(name, shape, dt, kind="ExternalOutput")

    def bank_get(self, k=1):
        for _ in range(32):
            s = self.bnext
            if s + k > 8:
                s = 0
            if all((s + i) not in self.bpinned for i in range(k)):
                self.bnext = (s + k) % 8
                return s
            self.bnext = (s + 1) % 8
        raise RuntimeError("no psum banks")

    def bank_pin(self, k):
        s = self.bank_get(k)
        self.bpinned |= set(range(s, s + k))
        return s

    def bank_unpin(self, s, k):
        self.bpinned -= set(range(s, s + k))

    def PS(self, bank, c0, c1, r0=0, r1=128):
        return self.psum[r0:r1, bank * 512 + c0: bank * 512 + c1]

    def PSB(self, bank, nb, c0, c1, r0=0, r1=128):
        return self.psum[:, bank * 512:(bank + nb) * 512].bitcast(BF)[r0:r1, c0:c1]

    def bk(self, bank, n=1):
        return [self.pb[bank + i] for i in range(n)]

    @staticmethod
    def _fd(ap):
        n = 1
        for d in ap.shape[1:]:
            n *= int(d)
        return n

    def _ecost(self, eng, out, ins=()):
        n = self._fd(out)
        if eng == "pool":
            return 200.0 + 1.7 * n
        if eng == "act":
            return 150.0 + 0.75 * n
        allbf = out.dtype == BF and all(getattr(a, "dtype", None) == BF for a in ins)
        return 70.0 + (0.6 if allbf else 1.3) * n

    def mm(self, out, lhsT, rhs, start, stop, reads, banks, writes=()):
        n_ = self._fd(out)
        if lhsT.dtype == F32:
            cost = 131.0 if n_ <= 64 else 4 * (55.0 + 0.27 * n_)
        elif n_ <= 256:
            cost = 55.0 + 0.27 * max(n_, 64)
        else:
            cost = 124.0 + (n_ - 256) * 0.78
        self.S.op("pe", lambda e: e.matmul(out, lhsT=lhsT, rhs=rhs, start=start, stop=stop), reads=reads, writes=writes, banks=banks,
                  cost=cost)

    def tp(self, out, in_, ident, reads, banks):
        cost = max(64, self._fd(in_)) * (2 if in_.dtype == F32 else 1) / 2.4 + 16
        self.S.op("pe", lambda e: e.transpose(out, in_, ident), reads=reads, banks=banks, cost=cost)

    def act(self, out, in_, func, reads=(), writes=(), banks=(), **kw):
        cost = self._ecost("act", out) + (100.0 if "accum_out" in kw else 0.0)
        aset = {AF.Silu: "silu", AF.Sigmoid: "sig", AF.Exp: "exp", AF.Ln: "exp"}.get(func)
        self.S.op("act", lambda e: e.activation(out=out, in_=in_, func=func, **kw), reads=reads, writes=writes, banks=banks, cost=cost,
                  aset=aset)

    def tt(self, eng, out, in0, in1, op, reads=(), writes=(), banks=()):
        self.S.op(eng, lambda e: e.tensor_tensor(out=out, in0=in0, in1=in1, op=op), reads=reads, writes=writes, banks=banks,
                  cost=self._ecost(eng, out, (in0, in1)))

    def ts(self, eng, out, in0, s1, s2, op0, op1=None, reads=(), writes=(), banks=()):
        cost = self._ecost(eng, out, (in0,))
        if op1 is None:
            self.S.op(eng, lambda e: e.tensor_scalar(out=out, in0=in0, scalar1=s1, scalar2=None, op0=op0),
                      reads=reads, writes=writes, banks=banks, cost=cost)
        else:
            self.S.op(eng, lambda e: e.tensor_scalar(out=out, in0=in0, scalar1=s1, scalar2=s2, op0=op0, op1=op1),
                      reads=reads, writes=writes, banks=banks, cost=cost)

    def stt(self, out, in0, scalar, in1, op0, op1, reads=(), writes=(), banks=()):
        self.S.op("dve", lambda e: e.scalar_tensor_tensor(out=out, in0=in0, scalar=scalar, in1=in1, op0=op0, op1=op1),
                  reads=reads, writes=writes, banks=banks, cost=70.0 + 1.7 * self._fd(out))

    def cp(self, eng, out, in_, reads=(), writes=(), banks=()):
        cost = self._ecost(eng, out, (in_,))
        if eng == "act":
            self.S.op("act", lambda e: e.copy(out=out, in_=in_), reads=reads, writes=writes, banks=banks, cost=cost)
        else:
            self.S.op(eng, lambda e: e.tensor_copy(out=out, in_=in_), reads=reads, writes=writes, banks=banks, cost=cost)

    def memset(self, eng, ap, val, writes):
        self.S.op(eng, lambda e: e.memset(ap, val), writes=writes, cost=100.0 + 0.5 * self._fd(ap))

    def dma(self, q, out, in_, dsem, reads=(), writes=(), is_output=False, slow=False):
        nbytes = (2 if (out.dtype == BF and in_.dtype == BF) else 4) * int(out.shape[0]) * self._fd(out)
        if slow:
            self.S.dma(q, lambda e: e.dma_start(out=out, in_=in_, allow_slow_non_contiguous=True), dsem,
                       reads=reads, writes=writes, is_output=is_output, nbytes=nbytes * 8)
        else:
            self.S.dma(q, lambda e: e.dma_start(out=out, in_=in_), dsem, reads=reads, writes=writes, is_output=is_output,
                       nbytes=nbytes)

    def dump(self, name, ap, reads):
        if name not in self.dbg:
            return
        t = self.dout("dbg_" + name, list(ap.shape))
        self.dbg_out["dbg_" + name] = list(ap.shape)
        self.dma("sp", t.ap(), ap, self.d_out, reads=reads, is_output=True)

    def wget(self, bid):
        i = self.wpos
        assert self.wseq[i] == bid, (i, self.wseq[i], bid)
        while self.wissued < min(len(self.wseq), i + NW):
            j = self.wissued
            b = self.wseq[j]
            wname, kc, c0, n = WB[b]
            slot = j % NW
            if wname == "w_ada":
                src = self.I[wname].ap()[:, c0:c0 + n].rearrange("(kc p) n -> p kc n", p=128)
                dst = self.wring[slot][:, :].bitcast(F32)[:, 0:kc * n].rearrange("p (kc n) -> p kc n", kc=kc)
                self.dma("sp", dst, src, self.d_w[slot], writes=[self.B_w[slot]])
            else:
                src = self.wscr[wname].ap()[:, c0:c0 + n].rearrange("(kc p) n -> p kc n", p=128)
                dst = self.wring[slot][:, 0:kc * n].rearrange("p (kc n) -> p kc n", kc=kc)
                self.dma("sp", dst, src, self.d_w[slot], reads=[self.scrbuf[b]], writes=[self.B_w[slot]])
            self.wissued += 1
        self.wpos += 1
        slot = i % NW
        wname, kc, c0, n = WB[bid]
        if wname == "w_ada":
            return self.wring[slot][:, :].bitcast(F32)[:, 0:kc * n].rearrange("p (kc n) -> p kc n", kc=kc), self.B_w[slot]
        return self.wring[slot][:, 0:kc * n].rearrange("p (kc n) -> p kc n", kc=kc), self.B_w[slot]

    def build(self, n_state=NT, n_main=NT, do_sample=True):
        nc, S = self.nc, self.S
        self.n_state, self.n_main, self.do_sample = n_state, n_main, do_sample
        I = {}
        for name, shape in [("x_main", [NT * T, D]), ("x_prev", [NT * T, D]), ("x_s", [TS, D]), ("c2", [2, D]),
                            ("cache_k", [512, D]), ("cache_v", [512, D]), ("state_conv", [3, 3072]),
                            ("state_ssm", [2048, 128]), ("flags", [128, 2]), ("consts", [128, NCONST]),
                            ("norm_g", [D]), ("w_ada", [D, 3 * D]), ("b_ada", [3 * D]), ("w_in", [D, IN_DIM]),
                            ("q_norm_g", [64]), ("k_norm_g", [64]), ("rel_bias", [16, 513]), ("w_att", [D, D]),
                            ("conv_w", [4, 3072]), ("conv_b", [3072]), ("dt_bias", [32]), ("a_log", [32]),
                            ("d_skip", [32]), ("ssd_norm_g", [2048]), ("w_ssd", [2048, D]), ("w_out", [D, D])]:
            I[name] = self.din(name, shape)
        self.I = I
        O = {}
        for name, shape in [("y_main", [NT * T, D]), ("y_s", [TS, D]), ("nk", [512, D]), ("nv", [512, D]),
                            ("nconv", [3, 3072]), ("nssm", [2048, 128]), ("nk_s", [TS, D]), ("nv_s", [TS, D]),
                            ("nconv_s", [3, 3072]), ("nssm_s", [2048, 128])]:
            O[name] = self.dout(name, shape)
        self.O = O
        self.wscr = {}
        for wname, shape in [("w_in", [D, IN_DIM]), ("w_att", [D, D]), ("w_ssd", [2048, D]), ("w_out", [D, D])]:
            self.wscr[wname] = nc.dram_tensor(wname + "_b", shape, BF, kind="Internal")
        self.ext = nc.dram_tensor("ext_bias", [16, 1024], F32, kind="Internal")

        self.psum = self.st.enter_context(nc.psum_tensor("psum", [128, 4096], F32))
        self.pb = S.bufs("bank", 8)
        self.bnext = 0
        self.bpinned = set()

        self.d_w = [S.dsem("w%d" % i) for i in range(NW)]
        self.d_cv = S.dsem("cv")
        self.d_par = S.dsem("par")
        self.d_x = [S.dsem("x%d" % i) for i in range(4)]
        self.d_out = S.dsem("out")
        self.d_misc = S.dsem("misc")
        self.d_bc = S.dsem("bc")
        self.d_sc = S.dsem("sc")
        self.d_ss = S.dsem("ss")
        self.d_ost = [S.dsem("ost0")] * 2
        self.d_xo = [S.dsem("xo%d" % i) for i in range(4)]
        self.d_cvo = S.dsem("cvo")
        self.d_sso = S.dsem("sso")
        self.d_cvs = [S.dsem("cv%d" % i) for i in range(4)]
        self.B_cvslot = S.bufs("cvslot", 4)

        sb = self.sb
        self.cst = sb("cst", [128, NCONST], F32)
        self.identb = sb("identb", [128, 128], BF)
        self.strib = sb("strib", [128, 128], BF)
        self.bonesb = sb("bonesb", [128, 128], BF)
        self.wring = [sb("wring%d" % i, [128, 4096], BF) for i in range(NW)]
        self.xt = sb("xt", [128, 4, D], F32)
        self.xn = [sb("xn%d" % i, [128, D], BF) for i in range(2)]
        self.hT2 = [sb("hT%d" % i, [128, 8, T], BF) for i in range(2)]
        self.hT = self.hT2[0]
        self.qT = sb("qT", [128, 8, T], BF)
        self.kT = sb("kT", [128, 8, 3 * T], BF)
        self.Vr = sb("Vr", [128, 6, 1040], BF)
        self.zatt = sb("zatt", [128, 2, D], BF)
        self.zssd = sb("zssd", [128, 2, 2048], BF)
        self.sq = [sb("sq%d" % i, [128, T], BF) for i in range(2)]
        self.rs = [sb("rs%d" % i, [128, T], F32) for i in range(2)]
        self.PT = [sb("PT%d" % i, [128, 640], BF) for i in range(2)]
        self.tab = sb("tab", [128, 16, 384], BF)
        self.bconst = sb("bconst", [128, 16], F32)
        self.negb = sb("negb", [128, 16], F32)
        self.rec = sb("rec", [128, 16], F32)
        self.Osb = sb("Osb", [128, D], BF)
        self.OgT = sb("OgT", [128, 8, T], BF)
        self.sig = sb("sig", [128, 8, T], BF)
        self.pre = sb("pre", [128, 8, T], BF)
        self.tmpm = [sb("tmpm%d" % i, [128, T], BF) for i in range(1)]
        self.craw = [sb("craw%d" % i, [128, T + 3], F32) for i in range(2)]
        self.cacc = [sb("cacc%d" % i, [128, T], F32) for i in range(2)]
        self.xsT = sb("xsT", [128, 16, T], BF)
        self.BT = sb("BT", [128, 4, T], BF)
        self.CT = sb("CT", [128, 4, T], BF)
        self.halo = sb("halo", [128, 24, 3], F32)
        self.dtraw = sb("dtraw", [128, 2, 32], F32)
        self.dtt = sb("dtt", [128, 32], F32)
        self.dt = sb("dt", [128, 32], F32)
        self.dta = sb("dta", [128, 32], F32)
        self.EW = sb("EW", [128, 64], F32)
        self.EA = sb("EA", [128, 64], F32)
        self.R = sb("R", [128, 1024], BF)
        self.dec = sb("dec", [128, 2048], BF)
        self.cbm = sb("cbm", [128, 256], BF)
        self.xdt = sb("xdt", [128, 2048], BF)
        self.xsD = sb("xsD", [128, 2048], BF)
        self.xdtw = self.xsD
        self.Btok = sb("Btok", [128, 512], BF)
        self.y = sb("y", [128, 2048], F32)
        self.ytmp = [sb("ytmp%d" % i, [128, 512], F32) for i in range(2)]
        self.ynorm = sb("ynorm", [128, 2048], BF)
        self.ss = sb("ss", [128, 8], F32)
        self.yT = sb("yT", [128, 16, T], BF)
        self.H = sb("H", [128, 2048], F32)
        self.Hbf = sb("Hbf", [128, 2048], BF)
        self.ost = [sb("ost0", [128, D], F32)] * 2
        self.gate_tok = sb("gate_tok", [128, D], F32)
        self.normgT = sb("normgT", [128, 8], F32)
        self.badaT = sb("badaT", [128, 24], F32)
        self.cT = sb("cT", [128, 2, 8], F32)
        self.siluT = sb("siluT", [128, 8, 2], F32)
        self.modT = sb("modT", [128, 24, 2], F32)
        self.gmodT = sb("gmodT", [128, 8, 2], F32)
        self.gqk = sb("gqk", [128, 2], F32)
        self.convwT = sb("convwT", [128, 4, 24], F32)
        self.convbT = sb("convbT", [128, 24], F32)
        self.gssdT = sb("gssdT", [128, 16], F32)
        self.rows = sb("rows", [128, 3, 32], F32)
        self.arow = sb("arow", [128, 32], F32)
        self.flg = sb("flg", [128, 2], F32)
        self.Atile = self.ost[0][:, 0:128]
        print("sbuf bytes remaining:", nc.sbuf_bytes_remaining)

        nb = S.buf
        self.B_cst = nb("cst")
        self.B_par = nb("par")
        self.B_w = S.bufs("w", NW)
        self.B_xt = S.bufs("xt", 4)
        self.B_xn = S.bufs("xn", 2)
        self.B_hT2 = [S.bufs("hT%d_" % i, 2) for i in range(2)]
        self.B_hT = self.B_hT2[0]
        self.B_qT = S.bufs("qT", 8)
        self.B_kT = [S.bufs("kT%d_" % s, 8) for s in range(3)]
        self.B_V = [S.bufs("V%d_" % s, 2) for s in range(3)]
        self.B_zatt = S.bufs("zatt", 2)
        self.B_zssd = S.bufs("zssd", 2)
        self.B_sq = S.bufs("sq", 2)
        self.B_rs = S.bufs("rs", 2)
        self.B_PT = S.bufs("PT", 2)
        self.B_tab = nb("tab")
        self.B_rowsep = S.bufs("rowsep", 4)
        self.B_tabw = nb("tabw")
        self.B_rec = nb("rec")
        self.B_Osb = nb("Osb")
        self.B_OgT = nb("OgT")
        self.B_sig = S.bufs("sig", 8)
        self.B_pre = S.bufs("pre", 8)
        self.B_tmpm = S.bufs("tmpm", 2)
        self.B_craw = S.bufs("craw", 2)
        self.B_cacc = S.bufs("cacc", 2)
        self.B_xsT = S.bufs("xsT", 16)
        self.B_BT = S.bufs("BT", 4)
        self.B_CT = S.bufs("CT", 4)
        self.B_halo = S.bufs("halo", 24)
        self.B_dtraw = S.bufs("dtraw", 2)
        self.B_dts = nb("dts")
        self.B_R = nb("R")
        self.B_dec = S.bufs("dec", 2)
        self.B_cbm = nb("cbm")
        self.B_xdt = nb("xdt")
        self.B_xsD = nb("xsD")
        self.B_xdtw = self.B_xsD
        self.B_Btok = nb("Btok")
        self.B_y = S.bufs("y", 4)
        self.B_ytmp = S.bufs("ytmp", 2)
        self.B_ynorm = nb("ynorm")
        self.B_ss = nb("ss")
        self.B_yT = S.bufs("yT", 2)
        self.B_H = S.bufs("H", 4)
        self.B_Hbf = S.bufs("Hbf", 4)
        self.B_ost = [S.buf("ost")] * 2
        self.B_gate = nb("gate")
        self.B_mod = nb("mod")
        self.B_A = self.B_ost[0]
        self.scrbuf = {b: nb("scr_" + b) for b in WB}
        self.B_ext = nb("ext")
        self.rot = {}
        self.state_mode = False
        self.crawx = list(self.craw) + [self.y[:, k * 260:k * 260 + T + 3] for k in range(3)]
        self.caccx = list(self.cacc) + [self.y[:, 1024 + k * 256:1024 + k * 256 + T] for k in range(3)]
        self.B_crawx = list(self.B_craw) + S.bufs("crawx", 3)
        self.B_caccx = list(self.B_cacc) + S.bufs("caccx", 3)

        seq = list(ADA_BLOCKS)
        if do_sample:
            seq += MAIN_BLOCKS
        for i in range(n_state):
            seq += STATE_LAST_BLOCKS if i == n_state - 1 else (STATE_KV_BLOCKS if i == n_state - 2 else STATE_BLOCKS)
        for i in range(n_main):
            seq += MAIN_BLOCKS
        self.wseq, self.wpos, self.wissued = seq, 0, 0

        self.setup()
        if do_sample:
            self.sample_tile()
        self.state_pass()
        self.main_pass()
        assert self.wpos == len(self.wseq), (self.wpos, len(self.wseq))
        S.finish()
        S.emit(nc, self.st)
        self.st.close()
        return nc

    def rotn(self, key, n=2):
        i = self.rot.get(key, 0)
        self.rot[key] = (i + 1) % n
        return i

    def setup(self):
        nc, S, I = self.nc, self.S, self.I
        cst = self.cst
        self.dma("sp", cst[:], I["consts"].ap(), self.d_par, writes=[self.B_cst])
        order = MAIN_BLOCKS
        for bi, b in enumerate(order):
            wname, kc, c0, n = WB[b]
            self.dma("pool", self.wscr[wname].ap()[:, c0:c0 + n], I[wname].ap()[:, c0:c0 + n], self.d_cvs[bi % 4],
                     writes=[self.scrbuf[b], self.B_cvslot[bi % 4]])
        self.dma("sp", self.y[0:16, 0:513], I["rel_bias"].ap(), self.d_misc, writes=[self.B_y[0], self.B_y[1]])
        self.cp("dve", self.y[0:16, 513:1024], self.y[0:16, 512:513].to_broadcast([16, 511]), reads=[self.B_y[0], self.B_y[1]],
                writes=[self.B_y[0], self.B_y[1]])
        self.dma("sp", self.ext.ap(), self.y[0:16, 0:1024], self.d_misc, reads=[self.B_y[0], self.B_y[1]], writes=[self.B_ext])
        P = self.d_par
        dm = lambda out, in_: self.dma("sp", out, in_, P, writes=[self.B_par], slow=True)
        dm(self.normgT[:], I["norm_g"].ap().rearrange("(kc p) -> p kc", p=128))
        dm(self.badaT[:], I["b_ada"].ap().rearrange("(t p) -> p t", p=128))
        for s_ in range(2):
            dm(self.cT[:, s_, :], I["c2"].ap()[s_, :].rearrange("(kc p) -> p kc", p=128))
        for half in range(2):
            dm(self.gqk[half * 64:(half + 1) * 64, 0:1], I["q_norm_g"].ap().rearrange("(p o) -> p o", o=1))
            dm(self.gqk[half * 64:(half + 1) * 64, 1:2], I["k_norm_g"].ap().rearrange("(p o) -> p o", o=1))
        for j_ in range(4):
            dm(self.convwT[:, j_, :], I["conv_w"].ap()[j_, :].rearrange("(t p) -> p t", p=128))
        dm(self.convbT[:], I["conv_b"].ap().rearrange("(t p) -> p t", p=128))
        dm(self.gssdT[:], I["ssd_norm_g"].ap().rearrange("(t p) -> p t", p=128))
        dm(self.rows[:, 0, :], I["dt_bias"].ap().partition_broadcast(128))
        dm(self.rows[:, 1, :], I["a_log"].ap().partition_broadcast(128))
        dm(self.rows[:, 2, :], I["d_skip"].ap().partition_broadcast(128))
        dm(self.flg[:], I["flags"].ap())
        S.seal(P, [self.B_cst, self.B_par])
        Bc, Bp = self.B_cst, self.B_par
        self.cp("dve", self.identb[:], cst[:, C_ID:C_ID + 128], reads=[Bc], writes=[Bc])
        self.cp("dve", self.strib[:], cst[:, C_STRI:C_STRI + 128], reads=[Bc], writes=[Bc])
        self.cp("dve", self.bonesb[:], cst[:, C_BONES:C_BONES + 128], reads=[Bc], writes=[Bc])
        self.act(self.arow[:], self.rows[:, 1, :], AF.Exp, reads=[Bp], writes=[Bp])
        self.ts("dve", self.arow[:], self.arow[:], -1.0, None, ALU.mult, reads=[Bp], writes=[Bp])
        self.memset("pool", self.Vr[:].rearrange("p b (h c) -> p b h c", c=65)[:, :, :, 64:65], 1.0,
                    writes=[b for s in self.B_V for b in s])
        self.act(self.siluT[:], self.cT[:].rearrange("p s k -> p k s"), AF.Silu, reads=[Bp], writes=[Bp])
        bank = self.bank_get()
        for bi, bid in enumerate(ADA_BLOCKS):
            wt, wbuf = self.wget(bid)
            for ct in range(2):
                t = bi * 2 + ct
                for kc in range(8):
                    self.mm(self.PS(bank, 2 * t, 2 * t + 2), wt[:, kc, ct * 128:(ct + 1) * 128], self.siluT[:, kc, :],
                            kc == 0, kc == 7, reads=[wbuf, Bp], banks=self.bk(bank))
        self.tt("dve", self.modT[:], self.PS(bank, 0, 48).rearrange("p (t s) -> p t s", s=2),
                self.badaT[:].unsqueeze(2).to_broadcast([128, 24, 2]), ALU.add,
                reads=[Bp], writes=[self.B_mod], banks=self.bk(bank))
        self.ts("dve", self.gmodT[:], self.modT[:, 8:16, :], 1.0, None, ALU.add, reads=[self.B_mod], writes=[self.B_mod])
        self.tt("dve", self.gmodT[:], self.gmodT[:], self.normgT[:].unsqueeze(2).to_broadcast([128, 8, 2]), ALU.mult,
                reads=[Bp, self.B_mod], writes=[self.B_mod])
        self.dump("modT", self.modT[:], [self.B_mod])
        srcc = bass.AP(self.ext, 600, [[0, 128], [1024, 16], [1, 1]])
        self.dma("sp", self.bconst[:, :].unsqueeze(2), srcc, self.d_bc, reads=[self.B_ext], writes=[self.B_tab], slow=True)
        self.ts("dve", self.negb[:], self.bconst[:], -1.0, None, ALU.mult, reads=[self.B_tab], writes=[self.B_tab])
        hk = self.y
        for h in range(16):
            src = bass.AP(self.ext, h * 1024 + 129, [[1, 128], [1, 384]])
            self.dma("sp", hk[:, 0:384], src, self.d_misc, reads=[self.B_ext], writes=[self.B_y[0]], slow=True)
            bank = self.bank_get()
            self.mm(self.PS(bank, 0, 384), cst[:, C_J:C_J + 128], hk[:, 0:384], True, True,
                    reads=[Bc, self.B_y[0]], banks=self.bk(bank))
            self.act(self.tab[:, h, :], self.PS(bank, 0, 384), AF.Exp, reads=[self.B_tab], writes=[self.B_tabw], banks=self.bk(bank),
                     bias=self.negb[:, h:h + 1])
        self.memset("pool", self.tab[64:128, :, 0:64], 0.0, writes=[self.B_tabw])
        self.S.op("pool", lambda e: e.memset(self.negb[:, 0:1], 0.0), reads=[self.B_tabw], writes=[self.B_tab], cost=100.0)
        self.dump("tab", self.tab[:, 0, :], [self.B_tab])

    def make_gate_tok(self, seq):
        cst = self.cst
        bank = self.bank_get(2)
        for kc in range(8):
            self.act(self.Atile, cst[:, 0:128], AF.Identity, reads=[self.B_mod, self.B_cst], writes=[self.B_A],
                     scale=0.0, bias=self.modT[:, 16 + kc, seq:seq + 1])
            self.mm(self.psum[:, bank * 512 + kc * 128: bank * 512 + (kc + 1) * 128], self.Atile, cst[:, C_ID:C_ID + 128],
                    True, True, reads=[self.B_A, self.B_cst], banks=self.bk(bank + kc // 4))
        self.cp("dve", self.gate_tok[:], self.psum[:, bank * 512: bank * 512 + 1024], writes=[self.B_gate],
                banks=self.bk(bank, 2))

    def load_norm(self, src_ap, row0, Tn, seq, plain_kslot=None, par=0):
        cst = self.cst
        nbk = (Tn + 127) // 128
        for b in range(nbk):
            bs = min(128, Tn - 128 * b)
            xb_ = 2 * par + b
            self.dma("sp", self.xt[0:bs, xb_, :], src_ap[row0 + 128 * b: row0 + 128 * b + bs, :], self.d_x[xb_],
                     writes=[self.B_xt[xb_]])
            i = self.rotn("xn")
            xn, Bxn = self.xn[i], self.B_xn[i]
            if plain_kslot is None:
                self.act(xn[0:bs, :], self.xt[0:bs, xb_, :], AF.Square, reads=[self.B_xt[xb_]], writes=[Bxn, self.B_ss],
                         accum_out=self.ss[0:bs, 4:5])
                self.act(self.ss[0:bs, 5:6], self.ss[0:bs, 4:5], AF.Ln, reads=[self.B_ss], writes=[self.B_ss],
                         scale=1.0 / D, bias=EPS)
                self.act(self.ss[0:bs, 5:6], self.ss[0:bs, 5:6], AF.Exp, reads=[self.B_ss], writes=[self.B_ss], scale=-0.5)
                self.act(xn[0:bs, :], self.xt[0:bs, xb_, :], AF.Copy, reads=[self.B_xt[xb_], self.B_ss], writes=[Bxn],
                         scale=self.ss[0:bs, 5:6])
            else:
                self.cp("act", xn[0:bs, :], self.xt[0:bs, xb_, :], reads=[self.B_xt[xb_]], writes=[Bxn])
            bank = self.bank_get()
            for kc in range(8):
                self.tp(self.PSB(bank, 1, kc * 128, kc * 128 + bs), xn[0:bs, kc * 128:(kc + 1) * 128],
                        self.identb[0:bs, 0:bs], reads=[Bxn, self.B_cst], banks=self.bk(bank))
            if plain_kslot is None:
                for kc in range(8):
                    self.ts("dve", self.hT[:, kc, 128 * b:128 * b + bs], self.PSB(bank, 1, kc * 128, kc * 128 + bs),
                            self.gmodT[:, kc, seq:seq + 1], self.modT[:, kc, seq:seq + 1], ALU.mult, ALU.add,
                            reads=[self.B_mod], writes=[self.B_hT[b]], banks=self.bk(bank))
            else:
                slot, half = plain_kslot[b]
                self.cp("dve", self.kT[:, :, slot * T + half * 128: slot * T + half * 128 + bs],
                        self.PSB(bank, 1, 0, 1024).rearrange("p (k t) -> p k t", t=128)[:, :, 0:bs],
                        writes=self.B_kT[slot], banks=self.bk(bank))

    def proj_fm(self, wt, wbuf, ct, Tn, bank, c0):
        for kc in range(8):
            self.mm(self.PS(bank, c0, c0 + Tn), wt[:, kc, ct * 128:(ct + 1) * 128], self.hT[:, kc, 0:Tn],
                    kc == 0, kc == 7, reads=[wbuf] + self.B_hT, banks=self.bk(bank))

    def proj_tm(self, wt, wbuf, b, bs, ncols, bank):
        for kc in range(8):
            self.mm(self.PS(bank, 0, ncols, 0, bs), self.hT[:, kc, 128 * b:128 * b + bs], wt[:, kc, 0:ncols],
                    kc == 0, kc == 7, reads=[wbuf, self.B_hT[b]], banks=self.bk(bank))

    def qk_tile(self, which, wt, wbuf, ctl, hp, Tn, slot):
        bank = self.bank_get()
        self.proj_fm(wt, wbuf, ctl, Tn, bank, 0)
        ps = self.PS(bank, 0, Tn)
        i = self.rotn("sq")
        sq, rs = self.sq[i], self.rs[i]
        self.act(sq[:, 0:Tn], ps, AF.Square, writes=[self.B_sq[i]], banks=self.bk(bank))
        bank2 = self.bank_get()
        self.mm(self.PS(bank2, 0, Tn), self.bonesb[:], sq[:, 0:Tn], True, True, reads=[self.B_sq[i], self.B_cst],
                banks=self.bk(bank2))
        self.act(rs[:, 0:Tn], self.PS(bank2, 0, Tn), AF.Ln, writes=[self.B_rs[i]], banks=self.bk(bank2),
                 scale=1.0 / 64, bias=EPS)
        self.act(rs[:, 0:Tn], rs[:, 0:Tn], AF.Exp, reads=[self.B_rs[i]], writes=[self.B_rs[i]], scale=-0.5)
        if which == "q":
            dst, Bd, g = self.qT[:, hp, 0:Tn], self.B_qT[hp], self.gqk[:, 0:1]
        else:
            dst, Bd, g = self.kT[:, hp, slot * T: slot * T + Tn], self.B_kT[slot][hp], self.gqk[:, 1:2]
        self.stt(dst, ps, g, rs[:, 0:Tn], ALU.mult, ALU.mult, reads=[self.B_rs[i], self.B_par], writes=[Bd],
                 banks=self.bk(bank))

    def keyblock(self, g, slot, bss):
        if g >= 0:
            return slot * T + g * 128, slot * 2 + g, (slot, g), bss[g]
        j = g + 4
        hs = (slot - 2) % 3 if j < 2 else (slot - 1) % 3
        half = j % 2
        return hs * T + half * 128, hs * 2 + half, (hs, half), 128

    def attention(self, Tn, slot, tile_idx):
        nbk = (Tn + 127) // 128
        bss = [min(128, Tn - 128 * b) for b in range(nbk)]
        for p in range(nbk):
            qn = bss[p]
            ob = self.bank_pin(3)
            for h in range(16):
                hp, base = h // 2, 64 * (h % 2)
                sbk = self.bank_get(2)
                pi = self.rotn("PT")
                PT, BPT = self.PT[pi], self.B_PT[pi]
                kinfo = []
                for r in range(5):
                    g = p - r
                    kcol, vblk, vsh, kn = self.keyblock(g, slot, bss)
                    kslot = kcol // T
                    kinfo.append((kcol, vblk, vsh, kn, kslot))
                    self.mm(self.psum[0:kn, sbk * 512 + r * 128: sbk * 512 + r * 128 + qn],
                            self.kT[base:base + 64, hp, kcol:kcol + kn], self.qT[base:base + 64, hp, p * 128:p * 128 + qn],
                            True, True, reads=[self.B_kT[kslot][hp], self.B_qT[hp]], banks=self.bk(sbk + r // 4))
                runs = []
                for r in range(5):
                    kn = kinfo[r][3]
                    masked = (tile_idx is not None) and (tile_idx * 2 + p - r < 0)
                    key = (kn, masked)
                    if runs and runs[-1][0] == key:
                        runs[-1][2] = r + 1
                    else:
                        runs.append([key, r, r + 1])
                for (kn, masked), r0, r1 in runs:
                    src = self.psum[0:kn, sbk * 512 + r0 * 128: sbk * 512 + r1 * 128].rearrange("p (r i) -> p r i", i=128)[:, :, 0:qn]
                    dst = PT[0:kn, r0 * 128:r1 * 128].rearrange("p (r i) -> p r i", i=128)[:, :, 0:qn]
                    if masked:
                        self.act(dst, src, AF.Exp, reads=[self.B_par], writes=[BPT], banks=self.bk(sbk, 2), scale=0.125,
                                 bias=self.flg[0:kn, 1:2])
                    else:
                        self.act(dst, src, AF.Exp, reads=[self.B_tab], writes=[BPT], banks=self.bk(sbk, 2), scale=0.125,
                                 bias=self.bconst[0:kn, h:h + 1])
                    r1t = min(r1, 3)
                    if r0 < r1t:
                        dstt = PT[0:kn, r0 * 128:r1t * 128].rearrange("p (r i) -> p r i", i=128)[:, :, 0:qn]
                        tb = self.tab[0:kn, h, r0 * 128:r1t * 128].rearrange("p (r i) -> p r i", i=128)[:, :, 0:qn]
                        self.tt("dve", dstt, dstt, tb, ALU.mult, reads=[self.B_tab, BPT], writes=[BPT])
                if qn > 64 and kinfo[4][3] == 128:
                    self.memset("pool", PT[0:64, 4 * 128 + 64: 4 * 128 + qn], 0.0, writes=[BPT])
                obank = ob + h // 7
                oc = (h % 7) * 65
                for r in range(5):
                    kcol, vblk, vsh, kn, kslot = kinfo[r]
                    self.mm(self.psum[0:qn, obank * 512 + oc: obank * 512 + oc + 65], PT[0:kn, r * 128:r * 128 + qn],
                            self.Vr[0:kn, vblk, h * 65:(h + 1) * 65], r == 0, r == 4,
                            reads=[BPT, self.B_V[vsh[0]][vsh[1]]], banks=self.bk(obank))
            for bi, (h0, nh) in enumerate([(0, 7), (7, 7), (14, 2)]):
                o3 = self.psum[0:qn, (ob + bi) * 512: (ob + bi) * 512 + nh * 65].rearrange("p (h c) -> p h c", c=65)
                rec_o, rec_i = self.rec[0:qn, h0:h0 + nh].unsqueeze(2), o3[:, :, 64:65]
                self.S.op("dve", lambda e, rec_o=rec_o, rec_i=rec_i: e.reciprocal(out=rec_o, in_=rec_i),
                          writes=[self.B_rec], banks=self.bk(ob + bi), cost=200.0)
                self.tt("dve", self.Osb[0:qn, h0 * 64:(h0 + nh) * 64].rearrange("p (h c) -> p h c", c=64), o3[:, :, 0:64],
                        self.rec[0:qn, h0:h0 + nh].unsqueeze(2).to_broadcast([qn, nh, 64]), ALU.mult,
                        reads=[self.B_rec], writes=[self.B_Osb], banks=self.bk(ob + bi))
            self.bank_unpin(ob, 3)
            self.tt("pool", self.Osb[0:qn, :], self.Osb[0:qn, :], self.zatt[0:qn, p, :], ALU.mult,
                    reads=[self.B_Osb, self.B_zatt[p]], writes=[self.B_Osb])
            bank = self.bank_get()
            for kc in range(8):
                self.tp(self.PSB(bank, 1, kc * 128, kc * 128 + qn), self.Osb[0:qn, kc * 128:(kc + 1) * 128],
                        self.identb[0:qn, 0:qn], reads=[self.B_Osb, self.B_cst], banks=self.bk(bank))
            self.cp("act", self.OgT[:, :, p * 128:p * 128 + qn],
                    self.PSB(bank, 1, 0, 1024).rearrange("p (k t) -> p k t", t=128)[:, :, 0:qn],
                    writes=[self.B_OgT], banks=self.bk(bank))

    def conv_tile(self, ct, bank, c0, Tn):
        if self.state_mode:
            i = self.rotn("convx", 5)
            raw, acc = self.crawx[i], self.caccx[i]
            Br, Ba = self.B_crawx[i], self.B_caccx[i]
        else:
            i = self.rotn("conv")
            raw, acc = self.craw[i], self.cacc[i]
            Br, Ba = self.B_craw[i], self.B_cacc[i]
        self.cp("pool", raw[:, 0:3], self.halo[:, ct, :], reads=[self.B_halo[ct]], writes=[Br])
        self.cp("act", raw[:, 3:3 + Tn], self.PS(bank, c0, c0 + Tn), writes=[Br], banks=self.bk(bank))
        self.cp("pool", self.halo[:, ct, :], raw[:, Tn:Tn + 3], reads=[Br], writes=[self.B_halo[ct]])
        self.act(acc[:, 0:Tn], raw[:, 0:Tn], AF.Identity, reads=[Br, self.B_par], writes=[Ba],
                 scale=self.convwT[:, 0, ct:ct + 1], bias=self.convbT[:, ct:ct + 1])
        for j in range(1, 4):
            self.stt(acc[:, 0:Tn], raw[:, j:j + Tn], self.convwT[:, j, ct:ct + 1], acc[:, 0:Tn], ALU.mult, ALU.add,
                     reads=[Br, Ba, self.B_par], writes=[Ba])
        if ct < 16:
            dst, Bd = self.xsT[:, ct, 0:Tn], self.B_xsT[ct]
        elif ct < 20:
            dst, Bd = self.BT[:, ct - 16, 0:Tn], self.B_BT[ct - 16]
        else:
            dst, Bd = self.CT[:, ct - 20, 0:Tn], self.B_CT[ct - 20]
        self.act(dst, acc[:, 0:Tn], AF.Silu, reads=[Ba], writes=[Bd])

    def ssd_block(self, b, bs, state_only):
        cst = self.cst
        nch = (bs + 63) // 64
        Lc = min(64, bs)
        t0 = 128 * b
        Bd = self.B_dts
        x = self.dtraw[0:bs, b, :]
        self.act(self.dtt[0:bs, :], x, AF.Abs, reads=[self.B_dtraw[b]], writes=[Bd])
        self.act(self.dtt[0:bs, :], self.dtt[0:bs, :], AF.Exp, reads=[Bd], writes=[Bd], scale=-1.0)
        self.act(self.dtt[0:bs, :], self.dtt[0:bs, :], AF.Ln, reads=[Bd], writes=[Bd], bias=1.0)
        self.stt(self.dt[0:bs, :], x, 0.0, self.dtt[0:bs, :], ALU.max, ALU.add, reads=[self.B_dtraw[b], Bd], writes=[Bd])
        self.tt("dve", self.dta[0:bs, :], self.dt[0:bs, :], self.arow[0:bs, :], ALU.mult, reads=[Bd, self.B_par], writes=[Bd])
        bk = self.bank_get()
        self.mm(self.PS(bk, 0, 32, 0, bs), cst[0:bs, C_TRI:C_TRI + bs], self.dta[0:bs, :], True, True, reads=[Bd, self.B_cst], banks=self.bk(bk))
        mrg = state_only and bs == 128
        if mrg:
            self.mm(self.PS(bk, 32, 64, 0, bs), cst[0:bs, C_STRI:C_STRI + bs], self.dta[0:bs, :], True, False, reads=[Bd, self.B_cst], banks=self.bk(bk))
            self.mm(self.PS(bk, 32, 64, 0, 64), cst[0:bs, C_CSEL + 128:C_CSEL + 192], self.dta[0:bs, :], False, True, reads=[Bd, self.B_cst], banks=self.bk(bk))
            for c in range(2):
                self.mm(self.PS(bk, 64, 96), cst[0:bs, C_CSEL + 128 * c:C_CSEL + 128 * c + 128], self.dta[0:bs, :],
                        c == 0, c == 1, reads=[Bd, self.B_cst], banks=self.bk(bk))
        else:
            self.mm(self.PS(bk, 32, 64, 0, bs), cst[0:bs, C_STRI:C_STRI + bs], self.dta[0:bs, :], True, True, reads=[Bd, self.B_cst], banks=self.bk(bk))
            for c in range(nch):
                self.mm(self.PS(bk, 64 + 32 * c, 96 + 32 * c), cst[0:bs, C_CSEL + 128 * c:C_CSEL + 128 * c + 128], self.dta[0:bs, :],
                        True, True, reads=[Bd, self.B_cst], banks=self.bk(bk))
        self.act(self.EW[0:bs, :], self.PS(bk, 0, 64, 0, bs), AF.Exp, writes=[Bd], banks=self.bk(bk))
        nea = 1 if mrg else nch
        self.act(self.EA[:, 0:32 * nea], self.PS(bk, 64, 64 + 32 * nea), AF.Exp, writes=[Bd], banks=self.bk(bk))
        E3 = self.EW[0:bs, 0:32].unsqueeze(2)
        W3 = self.EW[0:bs, 32:64].unsqueeze(2)
        dt3 = self.dt[0:bs, :].unsqueeze(2)
        xb = self.bank_get(2)
        for ct in range(16):
            self.tp(self.PSB(xb, 2, ct * 128, ct * 128 + 128, 0, bs), self.xsT[:, ct, t0:t0 + bs], self.identb[:, :],
                    reads=[self.B_xsT[ct], self.B_cst], banks=self.bk(xb + ct // 8))
        xs3 = self.PSB(xb, 2, 0, 2048, 0, bs).rearrange("p (h c) -> p h c", c=64)
        self.tt("dve", self.xdt[0:bs, :].rearrange("p (h c) -> p h c", c=64), xs3, dt3.to_broadcast([bs, 32, 64]), ALU.mult,
                reads=[Bd], writes=[self.B_xdt], banks=self.bk(xb, 2))
        if not state_only:
            self.tt("dve", self.xsD[0:bs, :].rearrange("p (h c) -> p h c", c=64), xs3,
                    self.rows[0:bs, 2, :].unsqueeze(2).to_broadcast([bs, 32, 64]), ALU.mult,
                    reads=[self.B_par], writes=[self.B_xsD], banks=self.bk(xb, 2))
        bb = self.bank_get()
        for g in range(4):
            self.tp(self.PSB(bb, 1, g * 128, g * 128 + 128, 0, bs), self.BT[:, g, t0:t0 + bs], self.identb[:, :],
                    reads=[self.B_BT[g], self.B_cst], banks=self.bk(bb))
        self.cp("act", self.Btok[0:bs, :], self.PSB(bb, 1, 0, 512, 0, bs), writes=[self.B_Btok], banks=self.bk(bb))

        if not state_only:
            cb = self.bank_get()
            for g in range(4):
                for c in range(nch):
                    cs = t0 + 64 * c
                    self.mm(self.PS(cb, g * 64, g * 64 + Lc, 64 * c, 64 * c + Lc), self.BT[:, g, cs:cs + Lc], self.CT[:, g, cs:cs + Lc],
                            True, True, reads=[self.B_BT[g], self.B_CT[g]], banks=self.bk(cb))
            self.tt("dve", self.cbm[0:bs, :].rearrange("p (g l) -> p g l", l=64)[:, :, 0:Lc],
                    self.PS(cb, 0, 256, 0, bs).rearrange("p (g l) -> p g l", l=64)[:, :, 0:Lc],
                    cst[0:bs, C_TRIL:C_TRIL + Lc].unsqueeze(1).to_broadcast([bs, 4, Lc]), ALU.mult,
                    reads=[self.B_cst], writes=[self.B_cbm], banks=self.bk(cb))
            for hh in range(2):
                R3 = self.R[0:bs, 0:16 * Lc].rearrange("p (h l) -> p h l", l=Lc)
                self.tt("dve", R3, cst[0:bs, C_TRIL:C_TRIL + Lc].unsqueeze(1).to_broadcast([bs, 16, Lc]),
                        self.dta[0:bs, 16 * hh:16 * hh + 16].unsqueeze(2).to_broadcast([bs, 16, Lc]), ALU.mult,
                        reads=[Bd, self.B_cst], writes=[self.B_R])
                sg = self.bank_get(2)
                ncol = 16 * Lc
                for j in range((ncol + 511) // 512):
                    cw = min(512, ncol - 512 * j)
                    self.mm(self.PS(sg + j, 0, cw, 0, bs), self.strib[0:bs, 0:bs], self.R[0:bs, 512 * j:512 * j + cw], True, True,
                            reads=[self.B_R, self.B_cst], banks=self.bk(sg + j))
                d3 = self.dec[0:bs, hh * 1024: hh * 1024 + ncol]
                self.act(d3, self.psum[0:bs, sg * 512: sg * 512 + ncol], AF.Exp, writes=[self.B_dec[hh]], banks=self.bk(sg, 2))
                d4 = d3.rearrange("p (g e l) -> p g e l", g=2, l=Lc)
                c4 = self.cbm[0:bs, :].rearrange("p (g l) -> p g l", l=64)[:, 2 * hh:2 * hh + 2, 0:Lc].unsqueeze(2).to_broadcast([bs, 2, 8, Lc])
                self.tt("dve", d4, d4, c4, ALU.mult, reads=[self.B_cbm, self.B_dec[hh]], writes=[self.B_dec[hh]])

        ysb = None
        if not state_only:
            ysb = self.bank_pin(4)
            for g in range(4):
                yi = self.bank_get()
                hh = g // 2
                for c in range(nch):
                    r0 = 64 * c
                    for e in range(8):
                        h = 8 * g + e
                        hl = h - 16 * hh
                        self.mm(self.PS(yi, e * 64, e * 64 + 64, r0, r0 + Lc),
                                self.dec[r0:r0 + Lc, hh * 1024 + hl * Lc: hh * 1024 + hl * Lc + Lc],
                                self.xdt[r0:r0 + Lc, h * 64:(h + 1) * 64], True, True,
                                reads=[self.B_dec[hh], self.B_xdt, self.B_rowsep[g]], banks=self.bk(yi))
                    if c == 0:
                        self.mm(self.PS(ysb + g, 0, 512, 0, Lc), self.CT[:, g, t0:t0 + Lc], self.Hbf[:, g * 512:(g + 1) * 512],
                                True, True, reads=[self.B_CT[g], self.B_Hbf[g]], banks=self.bk(ysb + g), writes=[self.B_rowsep[g]])
                self.tt("dve", self.y[0:bs, g * 512:(g + 1) * 512], self.PS(yi, 0, 512, 0, bs), self.xsD[0:bs, g * 512:(g + 1) * 512],
                        ALU.add, reads=[self.B_xsD], writes=[self.B_y[g]], banks=self.bk(yi))
        self.tt("pool", self.xdtw[0:bs, :].rearrange("p (h c) -> p h c", c=64), self.xdt[0:bs, :].rearrange("p (h c) -> p h c", c=64),
                W3.to_broadcast([bs, 32, 64]), ALU.mult, reads=[Bd, self.B_xdt], writes=[self.B_xdtw])
        for c in range(1 if mrg else nch):
            r0 = 64 * c
            if mrg:
                Lc = 128
            for g in range(4):
                if (not state_only) and c > 0:
                    self.mm(self.PS(ysb + g, 0, 512, r0, r0 + Lc), self.CT[:, g, t0 + r0:t0 + r0 + Lc], self.Hbf[:, g * 512:(g + 1) * 512],
                            True, True, reads=[self.B_CT[g], self.B_Hbf[g]], banks=self.bk(ysb + g))
                dh = self.bank_get()
                self.mm(self.PS(dh, 0, 512), self.Btok[r0:r0 + Lc, g * 128:(g + 1) * 128], self.xdtw[r0:r0 + Lc, g * 512:(g + 1) * 512],
                        True, True, reads=[self.B_Btok, self.B_xdtw], banks=self.bk(dh))
                Hg = self.H[:, g * 512:(g + 1) * 512]
                H3 = Hg.rearrange("p (h c) -> p h c", c=64)
                self.tt("pool" if state_only else "dve", H3, H3,
                        self.EA[:, 32 * c + 8 * g:32 * c + 8 * g + 8].unsqueeze(2).to_broadcast([128, 8, 64]), ALU.mult,
                        reads=[Bd, self.B_H[g]], writes=[self.B_H[g]])
                self.tt("dve", Hg, Hg, self.PS(dh, 0, 512), ALU.add, reads=[self.B_H[g]], writes=[self.B_H[g]], banks=self.bk(dh))
                if not state_only:
                    self.cp("act", self.Hbf[:, g * 512:(g + 1) * 512], Hg, reads=[self.B_H[g]], writes=[self.B_Hbf[g]])
        if state_only:
            return
        for g in range(4):
            i = self.rotn("ytmp")
            self.tt("dve", self.ytmp[i][0:bs, :].rearrange("p (h c) -> p h c", c=64),
                    self.PS(ysb + g, 0, 512, 0, bs).rearrange("p (h c) -> p h c", c=64),
                    E3[:, 8 * g:8 * g + 8, :].to_broadcast([bs, 8, 64]), ALU.mult,
                    reads=[Bd], writes=[self.B_ytmp[i]], banks=self.bk(ysb + g))
            yg = self.y[0:bs, g * 512:(g + 1) * 512]
            self.tt("dve", yg, yg, self.ytmp[i][0:bs, :], ALU.add, reads=[self.B_ytmp[i], self.B_y[g]], writes=[self.B_y[g]])
            self.tt("pool", yg, yg, self.zssd[0:bs, b, g * 512:(g + 1) * 512], ALU.mult, reads=[self.B_zssd[b], self.B_y[g]],
                    writes=[self.B_y[g]])
            self.act(self.ynorm[0:bs, g * 512:(g + 1) * 512], yg, AF.Square, reads=[self.B_y[g]], writes=[self.B_ynorm, self.B_ss],
                     accum_out=self.ss[0:bs, g:g + 1])
        self.bank_unpin(ysb, 4)
        self.act(self.ss[0:bs, 0:4], self.ss[0:bs, 0:4], AF.Ln, reads=[self.B_ss], writes=[self.B_ss], scale=1.0 / 512, bias=EPS)
        self.act(self.ss[0:bs, 0:4], self.ss[0:bs, 0:4], AF.Exp, reads=[self.B_ss], writes=[self.B_ss], scale=-0.5)
        for g in range(4):
            self.act(self.ynorm[0:bs, g * 512:(g + 1) * 512], self.y[0:bs, g * 512:(g + 1) * 512], AF.Copy,
                     reads=[self.B_y[g], self.B_ss], writes=[self.B_ynorm], scale=self.ss[0:bs, g:g + 1])
        yb = self.bank_get(2)
        for ct in range(16):
            self.tp(self.PSB(yb, 2, ct * 128, ct * 128 + bs), self.ynorm[0:bs, ct * 128:(ct + 1) * 128], self.identb[0:bs, 0:bs],
                    reads=[self.B_ynorm, self.B_cst], banks=self.bk(yb + ct // 8))
        self.tt("dve", self.yT[:, :, t0:t0 + bs], self.PSB(yb, 2, 0, 2048).rearrange("p (k t) -> p k t", t=128)[:, :, 0:bs],
                self.gssdT[:, :].unsqueeze(2).to_broadcast([128, 16, bs]), ALU.mult,
                reads=[self.B_par], writes=[self.B_yT[b]], banks=self.bk(yb, 2))

    def tile(self, mode, src_ap, row0, Tn, seq, slot, tile_idx=None, kv=False, outs=None, out_rows=None, with_c=False):
        par = self.rotn("tilepar")
        nbk = (Tn + 127) // 128
        bss = [min(128, Tn - 128 * b) for b in range(nbk)]
        full = mode == "main"
        self.S.tag = "norm"
        self.hT, self.B_hT = self.hT2[par], self.B_hT2[par]
        self.load_norm(src_ap, row0, Tn, seq, par=par)
        self.dump("hT", self.hT[:, :, 0:Tn], self.B_hT)
        self.S.tag = "qkv"
        if full:
            for j in range(2):
                wt, wbuf = self.wget("q%d" % j)
                for ctl in range(4):
                    self.qk_tile("q", wt, wbuf, ctl, 4 * j + ctl, Tn, slot)
        if full or kv:
            for j in range(2):
                wt, wbuf = self.wget("k%d" % j)
                for ctl in range(4):
                    self.qk_tile("k", wt, wbuf, ctl, 4 * j + ctl, Tn, slot)
            for j in range(2):
                wt, wbuf = self.wget("v%d" % j)
                for b in range(nbk):
                    bank = self.bank_get()
                    self.proj_tm(wt, wbuf, b, bss[b], 512, bank)
                    dst = self.Vr[0:bss[b], slot * 2 + b, j * 520:(j + 1) * 520].rearrange("p (h c) -> p h c", c=65)[:, :, 0:64]
                    self.cp("act", dst, self.PS(bank, 0, 512, 0, bss[b]).rearrange("p (h c) -> p h c", c=64),
                            writes=[self.B_V[slot][b]], banks=self.bk(bank))
        if full:
            self.dump("qT", self.qT[:, :, 0:Tn], self.B_qT)
            self.dump("kT", self.kT[:, :, slot * T:slot * T + Tn], self.B_kT[slot])
            for j in range(2):
                wt, wbuf = self.wget("za%d" % j)
                for b in range(nbk):
                    bank = self.bank_get()
                    self.proj_tm(wt, wbuf, b, bss[b], 512, bank)
                    self.act(self.zatt[0:bss[b], b, j * 512:(j + 1) * 512], self.PS(bank, 0, 512, 0, bss[b]), AF.Silu,
                             writes=[self.B_zatt[b]], banks=self.bk(bank))
            self.S.tag = "attn"
            self.attention(Tn, slot, tile_idx)
            self.S.tag = "attproj"
            self.dump("OgT", self.OgT[:, :, 0:Tn], [self.B_OgT])
            for j in range(2):
                wt, wbuf = self.wget("ga%d" % j)
                for c2 in range(2):
                    bank = self.bank_get()
                    for u in range(2):
                        self.proj_fm(wt, wbuf, 2 * c2 + u, Tn, bank, u * T)
                    for u in range(2):
                        dtile = 4 * j + 2 * c2 + u
                        self.act(self.sig[:, dtile, 0:Tn], self.PS(bank, u * T, u * T + Tn), AF.Sigmoid,
                                 writes=[self.B_sig[dtile]], banks=self.bk(bank))
            for j in range(2):
                wt, wbuf = self.wget("wa%d" % j)
                for c2 in range(2):
                    bank = self.bank_get()
                    for u in range(2):
                        for kc in range(8):
                            self.mm(self.PS(bank, u * T, u * T + Tn), wt[:, kc, (2 * c2 + u) * 128:(2 * c2 + u + 1) * 128],
                                    self.OgT[:, kc, 0:Tn], kc == 0, kc == 7, reads=[wbuf, self.B_OgT], banks=self.bk(bank))
                    for u in range(2):
                        dtile = 4 * j + 2 * c2 + u
                        self.tt("dve", self.pre[:, dtile, 0:Tn], self.PS(bank, u * T, u * T + Tn), self.sig[:, dtile, 0:Tn], ALU.mult,
                                reads=[self.B_sig[dtile]], writes=[self.B_pre[dtile]], banks=self.bk(bank))
        self.S.tag = "xbc"
        wt, wbuf = self.wget("dt")
        for b in range(nbk):
            bank = self.bank_get()
            self.proj_tm(wt, wbuf, b, bss[b], 32, bank)
            self.tt("dve", self.dtraw[0:bss[b], b, :], self.PS(bank, 0, 32, 0, bss[b]), self.rows[0:bss[b], 0, :], ALU.add,
                    reads=[self.B_par], writes=[self.B_dtraw[b]], banks=self.bk(bank))
        for j in range(6 if (full or with_c) else 5):
            wt, wbuf = self.wget("xb%d" % j)
            for c2 in range(2):
                bank = self.bank_get()
                for u in range(2):
                    self.proj_fm(wt, wbuf, 2 * c2 + u, Tn, bank, u * T)
                for u in range(2):
                    self.conv_tile(4 * j + 2 * c2 + u, bank, u * T, Tn)
        if full:
            for j in range(4):
                wt, wbuf = self.wget("zs%d" % j)
                for b in range(nbk):
                    bank = self.bank_get()
                    self.proj_tm(wt, wbuf, b, bss[b], 512, bank)
                    self.act(self.zssd[0:bss[b], b, j * 512:(j + 1) * 512], self.PS(bank, 0, 512, 0, bss[b]), AF.Silu,
                             writes=[self.B_zssd[b]], banks=self.bk(bank))
        self.S.tag = "ssd"
        for b in range(nbk):
            self.ssd_block(b, bss[b], not full)
        self.S.tag = "tail"
        if not full:
            return
        self.dump("yT", self.yT[:, :, 0:Tn], self.B_yT)
        for j in range(2):
            wt, wbuf = self.wget("gs%d" % j)
            for c2 in range(2):
                bank = self.bank_get()
                for u in range(2):
                    self.proj_fm(wt, wbuf, 2 * c2 + u, Tn, bank, u * T)
                for u in range(2):
                    dtile = 4 * j + 2 * c2 + u
                    self.act(self.sig[:, dtile, 0:Tn], self.PS(bank, u * T, u * T + Tn), AF.Sigmoid,
                             writes=[self.B_sig[dtile]], banks=self.bk(bank))
        for j in range(4):
            wt, wbuf = self.wget("ws%d" % j)
            bank = self.bank_get()
            for u in range(2):
                for kc in range(16):
                    self.mm(self.PS(bank, u * T, u * T + Tn), wt[:, kc, u * 128:(u + 1) * 128], self.yT[:, kc, 0:Tn],
                            kc == 0, kc == 15, reads=[wbuf] + self.B_yT, banks=self.bk(bank))
            for u in range(2):
                dtile = 2 * j + u
                i = self.rotn("tmpm", 1)
                self.tt("dve", self.tmpm[i][:, 0:Tn], self.PS(bank, u * T, u * T + Tn), self.sig[:, dtile, 0:Tn], ALU.mult,
                        reads=[self.B_sig[dtile]], writes=[self.B_tmpm[i]], banks=self.bk(bank))
                self.tt("pool", self.pre[:, dtile, 0:Tn], self.pre[:, dtile, 0:Tn], self.tmpm[i][:, 0:Tn], ALU.add,
                        reads=[self.B_tmpm[i], self.B_pre[dtile]], writes=[self.B_pre[dtile]])
        self.dump("merged", self.pre[:, :, 0:Tn], self.B_pre)
        for j in range(2):
            wt, wbuf = self.wget("wo%d" % j)
            for b in range(nbk):
                bs = bss[b]
                bank = self.bank_get()
                for kc in range(8):
                    self.mm(self.PS(bank, 0, 512, 0, bs), self.pre[:, kc, 128 * b:128 * b + bs], wt[:, kc, 0:512], kc == 0, kc == 7,
                            reads=[wbuf, self.B_pre[kc]], banks=self.bk(bank))
                xb_ = 2 * par + b
                oc = self.xt[0:bs, xb_, j * 512:(j + 1) * 512]
                pso = self.PS(bank, 0, 512, 0, bs)
                self.tt("dve", pso, pso, self.gate_tok[0:bs, j * 512:(j + 1) * 512], ALU.mult,
                        reads=[self.B_gate], banks=self.bk(bank))
                self.tt("dve", oc, oc, pso, ALU.add, reads=[self.B_xt[xb_]], writes=[self.B_xt[xb_]], banks=self.bk(bank))
        for b in range(nbk):
            bs = bss[b]
            xb_ = 2 * par + b
            self.dma("sp", outs[0].ap()[out_rows + 128 * b: out_rows + 128 * b + bs, :], self.xt[0:bs, xb_, :], self.d_xo[xb_],
                     reads=[self.B_xt[xb_]], is_output=True)

    def emit_kv_out(self, slot, Tn, nk, nv, row0):
        nbk = (Tn + 127) // 128
        for b in range(nbk):
            bs = min(128, Tn - 128 * b)
            bank = self.bank_get()
            for hp in range(8):
                self.tp(self.PSB(bank, 1, hp * 128, hp * 128 + 128, 0, bs), self.kT[:, hp, slot * T + 128 * b: slot * T + 128 * b + bs],
                        self.identb[:, :], reads=[self.B_kT[slot][hp], self.B_cst], banks=self.bk(bank))
            i = self.rotn("kvo")
            self.cp("dve", self.ost[i][0:bs, :], self.PSB(bank, 1, 0, 1024, 0, bs), writes=[self.B_ost[i]], banks=self.bk(bank))
            self.dma("sp", nk.ap()[row0 + 128 * b: row0 + 128 * b + bs, :], self.ost[i][0:bs, :], self.d_ost[i],
                     reads=[self.B_ost[i]], is_output=True)
            i = self.rotn("kvo")
            self.cp("act", self.ost[i][0:bs, :].rearrange("p (h c) -> p h c", c=64),
                    self.Vr[0:bs, slot * 2 + b, :].rearrange("p (h c) -> p h c", c=65)[:, :, 0:64],
                    reads=[self.B_V[slot][b]], writes=[self.B_ost[i]])
            self.dma("sp", nv.ap()[row0 + 128 * b: row0 + 128 * b + bs, :], self.ost[i][0:bs, :], self.d_ost[i],
                     reads=[self.B_ost[i]], is_output=True)

    def emit_conv_out(self, out):
        cst = self.cst
        for q3 in range(3):
            bank = self.bank_get(2)
            for u in range(8):
                t = 8 * q3 + u
                self.tp(self.psum[0:3, bank * 512 + u * 128: bank * 512 + u * 128 + 128], self.halo[:, t, :], cst[:, C_ID:C_ID + 128],
                        reads=[self.B_halo[t], self.B_cst], banks=self.bk(bank + u // 4))
            i = self.rotn("kvo")
            self.cp("dve", self.ost[i][0:3, :], self.psum[0:3, bank * 512: bank * 512 + 1024], writes=[self.B_ost[i]], banks=self.bk(bank, 2))
            self.dma("sp", out.ap()[:, q3 * 1024:(q3 + 1) * 1024], self.ost[i][0:3, :], self.d_ost[i], reads=[self.B_ost[i]], is_output=True)

    def emit_ssm_out(self, out):
        cst = self.cst
        for q4 in range(4):
            bank = self.bank_get()
            for u in range(4):
                t = 4 * q4 + u
                self.tp(self.PS(bank, u * 128, u * 128 + 128), self.H[:, t * 128:(t + 1) * 128], cst[:, C_ID:C_ID + 128],
                        reads=[self.B_H[t // 4], self.B_cst], banks=self.bk(bank))
            self.cp("dve", self.y[:, q4 * 512:(q4 + 1) * 512], self.PS(bank, 0, 512), writes=[self.B_y[q4]], banks=self.bk(bank))
        self.dma("sp", out.ap().rearrange("(t p) n -> p t n", p=128), self.y[:].rearrange("p (t n) -> p t n", n=128), self.d_sso,
                 reads=self.B_y, is_output=True)

    def zero_state(self):
        self.memset("pool", self.H[:], 0.0, writes=self.B_H)
        self.memset("pool", self.Hbf[:], 0.0, writes=self.B_Hbf)
        self.memset("pool", self.halo[:], 0.0, writes=self.B_halo)

    def sample_tile(self):
        I, O, cst = self.I, self.O, self.cst
        self.dma("sp", self.y[:].rearrange("p (t n) -> p t n", n=128), I["state_ssm"].ap().rearrange("(t p) n -> p t n", p=128),
                 self.d_ss, writes=self.B_y)
        for q4 in range(4):
            bank = self.bank_get()
            for u in range(4):
                t = 4 * q4 + u
                self.tp(self.PS(bank, u * 128, u * 128 + 128), self.y[:, t * 128:(t + 1) * 128], cst[:, C_ID:C_ID + 128],
                        reads=[self.B_y[t // 4], self.B_cst], banks=self.bk(bank))
            self.cp("dve", self.H[:, q4 * 512:(q4 + 1) * 512], self.PS(bank, 0, 512), writes=[self.B_H[q4]], banks=self.bk(bank))
            self.cp("act", self.Hbf[:, q4 * 512:(q4 + 1) * 512], self.H[:, q4 * 512:(q4 + 1) * 512], reads=[self.B_H[q4]],
                    writes=[self.B_Hbf[q4]])
        for j_ in range(3):
            self.dma("sp", self.halo[:, :, j_], I["state_conv"].ap()[j_, :].rearrange("(t p) -> p t", p=128), self.d_sc,
                     writes=self.B_halo, slow=True)
        self.S.seal(self.d_sc, list(self.B_halo))
        self.load_norm(I["cache_k"].ap(), 0, 256, 1, plain_kslot={0: (0, 0), 1: (0, 1)})
        self.load_norm(I["cache_k"].ap(), 256, 256, 1, plain_kslot={0: (1, 0), 1: (1, 1)})
        for blk in range(4):
            i = blk % 2
            self.dma("sp", self.ost[i][:, :], I["cache_v"].ap()[128 * blk:128 * blk + 128, :], self.d_ost[i], writes=[self.B_ost[i]])
            self.cp("dve", self.Vr[:, blk, :].rearrange("p (h c) -> p h c", c=65)[:, :, 0:64],
                    self.ost[i][:, :].rearrange("p (h c) -> p h c", c=64), reads=[self.B_ost[i]], writes=[self.B_V[blk // 2][blk % 2]])
        self.make_gate_tok(1)
        self.tile("main", I["x_s"].ap(), 0, TS, 1, 2, tile_idx=None, outs=[O["y_s"]], out_rows=0)
        self.emit_kv_out(2, TS, O["nk_s"], O["nv_s"], 0)
        self.emit_conv_out(O["nconv_s"])
        self.emit_ssm_out(O["nssm_s"])

    def state_pass(self):
        I = self.I
        self.zero_state()
        n = self.n_state
        xb_ = list(self.B_crawx[2:]) + list(self.B_caccx[2:])
        self.memset("pool", self.y[:, 2047:2048], 0.0, writes=list(self.B_y) + xb_)
        self.state_mode = True
        for i in range(n):
            kv = i >= n - 2
            slot = 1 if i == n - 2 else (2 if i == n - 1 else 0)
            self.tile("state", I["x_prev"].ap(), i * T, T, 0, slot, kv=kv, with_c=(i == n - 1))
        self.state_mode = False
        self.memset("pool", self.y[:, 2047:2048], 0.0, writes=list(self.B_y) + xb_)
        f = self.flg[:, 0:1]
        for g in range(4):
            Hg = self.H[:, g * 512:(g + 1) * 512]
            self.ts("pool", Hg, Hg, f, None, ALU.mult, reads=[self.B_par, self.B_H[g]], writes=[self.B_H[g]])
            self.cp("act", self.Hbf[:, g * 512:(g + 1) * 512], Hg, reads=[self.B_H[g]], writes=[self.B_Hbf[g]])
        self.ts("pool", self.halo[:], self.halo[:], f, None, ALU.mult, reads=[self.B_par] + list(self.B_halo), writes=self.B_halo)

    def main_pass(self):
        I, O = self.I, self.O
        self.make_gate_tok(0)
        n = self.n_main
        for t in range(n):
            self.tile("main", I["x_main"].ap(), t * T, T, 0, t % 3, tile_idx=t, outs=[O["y_main"]], out_rows=t * T)
        if n >= 2:
            self.emit_kv_out((n - 2) % 3, T, O["nk"], O["nv"], 0)
        self.emit_kv_out((n - 1) % 3, T, O["nk"], O["nv"], 256)
        self.emit_conv_out(O["nconv"])
        self.emit_ssm_out(O["nssm"])


def make_consts():
    c = np.zeros((128, NCONST), np.float32)
    s = np.arange(128)[:, None]
    l = np.arange(128)[None, :]
    same = (s // 64) == (l // 64)
    c[:, C_ID:C_ID + 128] = np.eye(128)
    c[:, C_J:C_J + 128] = np.eye(128)[::-1]
    c[:, C_TRI:C_TRI + 128] = (same & (s <= l))
    c[:, C_STRI:C_STRI + 128] = (same & (s > l))
    for ch in range(2):
        c[:, C_CSEL + 128 * ch:C_CSEL + 128 * ch + 128] = ((s // 64) == ch)
    c[:, C_TRIL:C_TRIL + 64] = ((s % 64) <= np.arange(64)[None, :])
    c[:, C_BONES:C_BONES + 128] = same
    return c


_CACHE = {}


def get_program(key=("full",), **kw):
    if key not in _CACHE:
        b = Builder(**{k: v for k, v in kw.items() if k == "dbg"})
        nc = b.build(**{k: v for k, v in kw.items() if k != "dbg"})
        _CACHE[key] = (nc, b)
    return _CACHE[key]


def make_in_maps(inp):
    f = lambda a: np.ascontiguousarray(np.asarray(a, dtype=np.float32))
    xp = f(inp["x_prompt"])
    consts = make_consts()
    shared = {
        "consts": consts,
        "norm_g": f(inp["norm_g"][0]), "w_ada": f(inp["w_ada"][0]), "b_ada": f(inp["b_ada"][0]), "w_in": f(inp["w_in"][0]),
        "q_norm_g": f(inp["q_norm_g"][0]), "k_norm_g": f(inp["k_norm_g"][0]), "rel_bias": f(inp["rel_bias"][0]),
        "w_att": f(inp["w_att_proj"][0]), "conv_w": f(inp["conv_w"][0]), "conv_b": f(inp["conv_b"][0]),
        "dt_bias": f(inp["dt_bias"][0]), "a_log": f(inp["a_log"][0]), "d_skip": f(inp["d_skip"][0]),
        "ssd_norm_g": f(inp["ssd_norm_g"][0]), "w_ssd": f(inp["w_ssd_proj"][0]), "w_out": f(inp["w_out"][0]),
    }
    maps = []
    for c in range(8):
        b, half = c // 2, c % 2
        flags = np.zeros((128, 2), np.float32)
        flags[:, 0] = float(half)
        flags[:, 1] = 0.0 if half else -30000.0
        m = dict(shared)
        m.update({
            "x_main": f(xp[b, half * 4096:(half + 1) * 4096]),
            "x_prev": f(xp[b, 0:4096]),
            "x_s": f(inp["x_sample"][c]),
            "c2": f(np.stack([np.asarray(inp["c_prompt"])[b], np.asarray(inp["c_sample"])[c]])),
            "cache_k": f(np.asarray(inp["cache_k"])[0, c].reshape(512, 1024)),
            "cache_v": f(np.asarray(inp["cache_v"])[0, c].reshape(512, 1024)),
            "state_conv": f(np.asarray(inp["state_conv"])[0, c]),
            "state_ssm": f(np.asarray(inp["state_ssm"])[0, c].reshape(2048, 128)),
            "flags": flags,
        })
        maps.append(m)
    return maps


def assemble(res):
    R = res
    y_prompt = np.zeros((4, 8192, 1024), np.float32)
    y_sample = np.zeros((8, 16, 1024), np.float32)
    nkp = np.zeros((1, 4, 512, 16, 64), np.float32)
    nvp = np.zeros((1, 4, 512, 16, 64), np.float32)
    ncp = np.zeros((1, 4, 3, 3072), np.float32)
    nhp = np.zeros((1, 4, 32, 64, 128), np.float32)
    nks = np.zeros((1, 8, 16, 16, 64), np.float32)
    nvs = np.zeros((1, 8, 16, 16, 64), np.float32)
    ncs = np.zeros((1, 8, 3, 3072), np.float32)
    nhs = np.zeros((1, 8, 32, 64, 128), np.float32)
    for c in range(8):
        b, half = c // 2, c % 2
        r = R[c]
        y_prompt[b, half * 4096:(half + 1) * 4096] = r["y_main"]
        y_sample[c] = r["y_s"]
        if half == 1:
            nkp[0, b] = r["nk"].reshape(512, 16, 64)
            nvp[0, b] = r["nv"].reshape(512, 16, 64)
            ncp[0, b] = r["nconv"]
            nhp[0, b] = r["nssm"].reshape(32, 64, 128)
        nks[0, c] = r["nk_s"].reshape(16, 16, 64)
        nvs[0, c] = r["nv_s"].reshape(16, 16, 64)
        ncs[0, c] = r["nconv_s"]
        nhs[0, c] = r["nssm_s"].reshape(32, 64, 128)
    return (y_prompt, y_sample, nkp, nvp, ncp, nhp, nks, nvs, ncs, nhs)


def kernel(**inputs):
    nc, _ = get_program()
    maps = make_in_maps(inputs)
    res = run_bass_kernel_spmd(nc, maps, core_ids=list(range(8)))
    return assemble(res.results)
```

```python
import numpy as np
from contextlib import ExitStack
import concourse.bass as bass
import concourse.mybir as mybir
from concourse.bass_utils import run_bass_kernel_spmd

F32 = mybir.dt.float32
BF = mybir.dt.bfloat16
AF = mybir.ActivationFunctionType
ALU = mybir.AluOpType
AX = mybir.AxisListType

D = 1024
KC = 8
T = 256
NT = 16
TS = 16
IN_DIM = 11296
EPS = 1e-6
C_ID, C_J, C_TRI, C_STRI, C_CSEL, C_TRIL, C_BONES = 0, 128, 256, 384, 512, 768, 832
NCONST = 960


class Buf:
    __slots__ = ("name", "w", "r", "acc")

    def __init__(self, name):
        self.name = name
        self.w = []
        self.r = []
        self.acc = {}


class DSem:
    __slots__ = ("key", "ops")

    def __init__(self, key):
        self.key = key
        self.ops = []


class _Op:
    __slots__ = ("idx", "eng", "fn", "preds", "succs", "cost", "dsem", "nbytes", "is_output", "seq", "cum",
                 "npred", "ready", "fin", "tag", "start", "blame", "gap", "aset")


class Sched:
    ENGS = ("pe", "act", "dve", "pool", "sp")
    XLAT = 250.0
    SLAT = 100.0
    STARVE = 8000.0

    def __init__(self, same_engine_sync=True, reorder=True):
        self.ops = []
        self.dsems = {}
        self.same_engine_sync = same_engine_sync
        self.reorder = reorder
        self.tag = ""

    def buf(self, name):
        return Buf(name)

    def bufs(self, name, n):
        return [Buf("%s%d" % (name, i)) for i in range(n)]

    def dsem(self, name):
        d = DSem("D_" + name)
        self.dsems[d.key] = d
        return d

    def _add(self, eng, fn, reads, writes, banks, cost, dsem=None, nbytes=0, is_output=False, aset=None):
        o = _Op()
        o.aset = aset
        o.idx = len(self.ops)
        o.eng, o.fn, o.cost, o.dsem, o.nbytes, o.is_output = eng, fn, cost, dsem, nbytes, is_output
        o.succs = []
        o.tag = self.tag
        o.blame = None
        p = set()
        for b in banks:
            p.update(b.acc.values())
            b.acc[eng] = o.idx
        for b in reads:
            p.update(b.w)
        for b in writes:
            p.update(b.w)
            p.update(b.r)
        p.discard(o.idx)
        o.preds = p
        for b in reads:
            b.r.append(o.idx)
        for b in writes:
            b.w = [o.idx]
            b.r = []
        self.ops.append(o)
        if dsem is not None:
            dsem.ops.append(o.idx)
        return o

    def op(self, engname, fn, reads=(), writes=(), banks=(), cost=100.0, aset=None):
        self._add(engname, fn, reads, writes, banks, cost, aset=aset)

    def dma(self, qname, fn, dsem, reads=(), writes=(), is_output=False, nbytes=0):
        self._add(qname, fn, reads, writes, (), 60.0, dsem=dsem, nbytes=nbytes, is_output=is_output)

    def seal(self, dsem, bufs):
        for b in bufs:
            b.w = list(dsem.ops)
            b.r = []

    def _order(self):
        ops = self.ops
        n = len(ops)
        for o in ops:
            o.npred = len(o.preds)
            o.ready = 0.0
            for p in o.preds:
                ops[p].succs.append(o.idx)
        if not self.reorder:
            return {e: [o.idx for o in ops if o.eng == e] for e in self.ENGS}
        ready = {e: [] for e in self.ENGS}
        for o in ops:
            if o.npred == 0:
                ready[o.eng].append(o.idx)
        free = {e: 0.0 for e in self.ENGS}
        order = {e: [] for e in self.ENGS}
        dma_pipe = 0.0
        cur_set = None
        self.n_switch = 0
        done = 0
        WIN = 4000
        oldest = 0
        sched = [False] * n
        while done < n:
            best = None
            while oldest < n and sched[oldest]:
                oldest += 1
            for e in self.ENGS:
                r = ready[e]
                if not r:
                    continue
                f = free[e]
                cand = None
                soon = None
                for i in r:
                    if i > oldest + WIN:
                        continue
                    o = ops[i]
                    if o.ready <= f:
                        if cand is None or i < cand:
                            cand = i
                    elif soon is None or o.ready < ops[soon].ready or (o.ready == ops[soon].ready and i < soon):
                        soon = i
                if e == "act" and cand is not None and cur_set is not None:
                    oa = ops[cand].aset
                    if oa is not None and oa != cur_set and f - ops[cand].ready < self.STARVE:
                        alt = None
                        for i in r:
                            if i > oldest + WIN:
                                continue
                            o2 = ops[i]
                            if o2.ready <= f and (o2.aset is None or o2.aset == cur_set) and (alt is None or i < alt):
                                alt = i
                        if alt is not None:
                            cand = alt
                pick = cand if cand is not None else soon
                if pick is None:
                    continue
                st = max(f, ops[pick].ready)
                if best is None or st < best[0] or (st == best[0] and pick < best[2]):
                    best = (st, e, pick)
            if best is None:
                WIN *= 2
                continue
            st, e, i = best
            o = ops[i]
            ready[e].remove(i)
            sched[i] = True
            o.start = st
            o.gap = st - free[e]
            if o.dsem is not None:
                t0 = max(st + o.cost, dma_pipe)
                dma_pipe = t0 + o.nbytes / 280.0
                o.fin = dma_pipe + 1800.0
                free[e] = st + o.cost
            else:
                sw = 0.0
                if e == "act" and o.aset is not None and o.aset != cur_set:
                    sw = 1300.0
                    cur_set = o.aset
                    self.n_switch += 1
                o.fin = st + o.cost + sw
                free[e] = o.fin
            order[e].append(i)
            done += 1
            for s_ in o.succs:
                so = ops[s_]
                if so.eng == e and o.dsem is None:
                    lat = 0.0 if e == "pe" else self.SLAT
                else:
                    lat = self.XLAT
                if o.fin + lat > so.ready:
                    so.ready = o.fin + lat
                    so.blame = o.idx
                so.npred -= 1
                if so.npred == 0:
                    ready[so.eng].append(s_)
        self.sim_ns = max(o.fin for o in ops)
        return order

    def finish(self):
        self.final_order = self._order()

    def emit(self, nc, stack):
        ops = self.ops
        order = self.final_order
        ekey = {e: "E_" + e for e in self.ENGS}
        sems = {}
        for e in self.ENGS:
            sems[ekey[e]] = stack.enter_context(nc.semaphore("s_" + e))
        for k in self.dsems:
            sems[k] = stack.enter_context(nc.semaphore("s_" + k))
        cnt = {e: 0 for e in self.ENGS}
        dcnt = {k: 0 for k in self.dsems}
        for e in self.ENGS:
            for i in order[e]:
                o = ops[i]
                if o.dsem is not None:
                    dcnt[o.dsem.key] += 16
                    o.cum = dcnt[o.dsem.key]
                    o.seq = None
                else:
                    cnt[e] += 1
                    o.seq = cnt[e]
        progs = {}
        out_ev = {}
        for e in self.ENGS:
            waited = {}
            prog = []
            for i in order[e]:
                o = ops[i]
                need = {}
                for p in o.preds:
                    po = ops[p]
                    if po.dsem is not None:
                        k, v = po.dsem.key, po.cum
                    else:
                        if po.eng == e and (e == "pe" or e in NO_SELF_SYNC or not self.same_engine_sync):
                            continue
                        k, v = ekey[po.eng], po.seq
                    if need.get(k, 0) < v:
                        need[k] = v
                waits = []
                for k, v in need.items():
                    if waited.get(k, 0) >= v:
                        continue
                    waited[k] = v
                    waits.append((k, v))
                if o.dsem is not None:
                    prog.append((waits, o.fn, (o.dsem.key, 16)))
                    if o.is_output:
                        out_ev[o.dsem.key] = max(out_ev.get(o.dsem.key, 0), o.cum)
                else:
                    prog.append((waits, o.fn, (ekey[e], 1)))
            progs[e] = (prog, waited)
        prog, waited = progs["sp"]
        waits = [(k, v) for k, v in out_ev.items() if waited.get(k, 0) < v]
        for e in self.ENGS:
            if e != "sp" and cnt[e] > 0:
                waits.append((ekey[e], cnt[e]))
        prog.append((waits, None, None))

        def replay(prog):
            def run(eng):
                for waits, fn, inc in prog:
                    for s, v in waits:
                        eng.wait_ge(sems[s], v)
                    if fn is not None:
                        fn(eng).then_inc(sems[inc[0]], inc[1])
            return run

        with nc.Block() as block:
            block.tensor(replay(progs["pe"][0]))
            block.scalar(replay(progs["act"][0]))
            block.vector(replay(progs["dve"][0]))
            block.gpsimd(replay(progs["pool"][0]))
            block.sync(replay(progs["sp"][0]))


WB = {}
for _i in range(2):
    WB["q%d" % _i] = ("w_in", 8, 0 + 512 * _i, 512)
    WB["k%d" % _i] = ("w_in", 8, 1024 + 512 * _i, 512)
    WB["v%d" % _i] = ("w_in", 8, 2048 + 512 * _i, 512)
    WB["za%d" % _i] = ("w_in", 8, 3072 + 512 * _i, 512)
    WB["ga%d" % _i] = ("w_in", 8, 9248 + 512 * _i, 512)
    WB["gs%d" % _i] = ("w_in", 8, 10272 + 512 * _i, 512)
    WB["wa%d" % _i] = ("w_att", 8, 512 * _i, 512)
    WB["wo%d" % _i] = ("w_out", 8, 512 * _i, 512)
for _i in range(4):
    WB["zs%d" % _i] = ("w_in", 8, 4096 + 512 * _i, 512)
    WB["ws%d" % _i] = ("w_ssd", 16, 256 * _i, 256)
for _i in range(6):
    WB["xb%d" % _i] = ("w_in", 8, 6144 + 512 * _i, 512)
WB["dt"] = ("w_in", 8, 9216, 32)

for _i in range(12):
    WB["ad%d" % _i] = ("w_ada", 8, 256 * _i, 256)
ADA_BLOCKS = ["ad%d" % i for i in range(12)]
MAIN_BLOCKS = (["q0", "q1", "k0", "k1", "v0", "v1", "za0", "za1", "ga0", "ga1", "wa0", "wa1", "dt"]
               + ["xb%d" % i for i in range(6)] + ["zs%d" % i for i in range(4)]
               + ["gs0", "gs1"] + ["ws%d" % i for i in range(4)] + ["wo0", "wo1"])
STATE_BLOCKS = ["dt"] + ["xb%d" % i for i in range(5)]
STATE_KV_BLOCKS = ["k0", "k1", "v0", "v1"] + STATE_BLOCKS
STATE_LAST_BLOCKS = STATE_KV_BLOCKS + ["xb5"]
NW = 3
SAME_ENGINE_SYNC = True
NO_SELF_SYNC = ()


class Builder:
    def __init__(self, dbg=()):
        self.dbg = set(dbg)
        self.dbg_out = {}
        self.nc = bass.Bass("TRN2", target_bir_lowering=False)
        self.S = Sched(same_engine_sync=SAME_ENGINE_SYNC)
        self.st = ExitStack()

    def sb(self, name, shape, dt):
        return self.st.enter_context(self.nc.sbuf_tensor(name, shape, dt))

    def din(self, name, shape, dt=F32):
        return self.nc.dram_tensor(name, shape, dt, kind="ExternalInput")

    def dout(self, name, shape, dt=F32):
        return self.nc.dram_tensor(name, shape, dt, kind="ExternalOutput")

    def bank_get(self, k=1):
        for _ in range(32):
            s = self.bnext
            if s + k > 8:
                s = 0
            if all((s + i) not in self.bpinned for i in range(k)):
                self.bnext = (s + k) % 8
                return s
            self.bnext = (s + 1) % 8
        raise RuntimeError("no psum banks")

    def bank_pin(self, k):
        s = self.bank_get(k)
        self.bpinned |= set(range(s, s + k))
        return s

    def bank_unpin(self, s, k):
        self.bpinned -= set(range(s, s + k))

    def PS(self, bank, c0, c1, r0=0, r1=128):
        return self.psum[r0:r1, bank * 512 + c0: bank * 512 + c1]

    def PSB(self, bank, nb, c0, c1, r0=0, r1=128):
        return self.psum[:, bank * 512:(bank + nb) * 512].bitcast(BF)[r0:r1, c0:c1]

    def bk(self, bank, n=1):
        return [self.pb[bank + i] for i in range(n)]

    @staticmethod
    def _fd(ap):
        n = 1
        for d in ap.shape[1:]:
            n *= int(d)
        return n

    def _ecost(self, eng, out, ins=()):
        n = self._fd(out)
        if eng == "pool":
            return 200.0 + 1.7 * n
        if eng == "act":
            return 150.0 + 0.75 * n
        allbf = out.dtype == BF and all(getattr(a, "dtype", None) == BF for a in ins)
        return 70.0 + (0.6 if allbf else 1.3) * n

    def mm(self, out, lhsT, rhs, start, stop, reads, banks, writes=()):
        n_ = self._fd(out)
        if lhsT.dtype == F32:
            cost = 131.0 if n_ <= 64 else 4 * (55.0 + 0.27 * n_)
        elif n_ <= 256:
            cost = 55.0 + 0.27 * max(n_, 64)
        else:
            cost = 124.0 + (n_ - 256) * 0.78
        self.S.op("pe", lambda e: e.matmul(out, lhsT=lhsT, rhs=rhs, start=start, stop=stop), reads=reads, writes=writes, banks=banks,
                  cost=cost)

    def tp(self, out, in_, ident, reads, banks):
        cost = max(64, self._fd(in_)) * (2 if in_.dtype == F32 else 1) / 2.4 + 16
        self.S.op("pe", lambda e: e.transpose(out, in_, ident), reads=reads, banks=banks, cost=cost)

    def act(self, out, in_, func, reads=(), writes=(), banks=(), **kw):
        cost = self._ecost("act", out) + (100.0 if "accum_out" in kw else 0.0)
        aset = {AF.Silu: "silu", AF.Sigmoid: "sig", AF.Exp: "exp", AF.Ln: "exp"}.get(func)
        self.S.op("act", lambda e: e.activation(out=out, in_=in_, func=func, **kw), reads=reads, writes=writes, banks=banks, cost=cost,
                  aset=aset)

    def tt(self, eng, out, in0, in1, op, reads=(), writes=(), banks=()):
        self.S.op(eng, lambda e: e.tensor_tensor(out=out, in0=in0, in1=in1, op=op), reads=reads, writes=writes, banks=banks,
                  cost=self._ecost(eng, out, (in0, in1)))

    def ts(self, eng, out, in0, s1, s2, op0, op1=None, reads=(), writes=(), banks=()):
        cost = self._ecost(eng, out, (in0,))
        if op1 is None:
            self.S.op(eng, lambda e: e.tensor_scalar(out=out, in0=in0, scalar1=s1, scalar2=None, op0=op0),
                      reads=reads, writes=writes, banks=banks, cost=cost)
        else:
            self.S.op(eng, lambda e: e.tensor_scalar(out=out, in0=in0, scalar1=s1, scalar2=s2, op0=op0, op1=op1),
                      reads=reads, writes=writes, banks=banks, cost=cost)

    def stt(self, out, in0, scalar, in1, op0, op1, reads=(), writes=(), banks=()):
        self.S.op("dve", lambda e: e.scalar_tensor_tensor(out=out, in0=in0, scalar=scalar, in1=in1, op0=op0, op1=op1),
                  reads=reads, writes=writes, banks=banks, cost=70.0 + 1.7 * self._fd(out))

    def cp(self, eng, out, in_, reads=(), writes=(), banks=()):
        cost = self._ecost(eng, out, (in_,))
        if eng == "act":
            self.S.op("act", lambda e: e.copy(out=out, in_=in_), reads=reads, writes=writes, banks=banks, cost=cost)
        else:
            self.S.op(eng, lambda e: e.tensor_copy(out=out, in_=in_), reads=reads, writes=writes, banks=banks, cost=cost)

    def memset(self, eng, ap, val, writes):
        self.S.op(eng, lambda e: e.memset(ap, val), writes=writes, cost=100.0 + 0.5 * self._fd(ap))

    def dma(self, q, out, in_, dsem, reads=(), writes=(), is_output=False, slow=False):
        nbytes = (2 if (out.dtype == BF and in_.dtype == BF) else 4) * int(out.shape[0]) * self._fd(out)
        if slow:
            self.S.dma(q, lambda e: e.dma_start(out=out, in_=in_, allow_slow_non_contiguous=True), dsem,
                       reads=reads, writes=writes, is_output=is_output, nbytes=nbytes * 8)
        else:
            self.S.dma(q, lambda e: e.dma_start(out=out, in_=in_), dsem, reads=reads, writes=writes, is_output=is_output,
                       nbytes=nbytes)

    def dump(self, name, ap, reads):
        if name not in self.dbg:
            return
        t = self.dout("dbg_" + name, list(ap.shape))
        self.dbg_out["dbg_" + name] = list(ap.shape)
        self.dma("sp", t.ap(), ap, self.d_out, reads=reads, is_output=True)

    def wget(self, bid):
        i = self.wpos
        assert self.wseq[i] == bid, (i, self.wseq[i], bid)
        while self.wissued < min(len(self.wseq), i + NW):
            j = self.wissued
            b = self.wseq[j]
            wname, kc, c0, n = WB[b]
            slot = j % NW
            if wname == "w_ada":
                src = self.I[wname].ap()[:, c0:c0 + n].rearrange("(kc p) n -> p kc n", p=128)
                dst = self.wring[slot][:, :].bitcast(F32)[:, 0:kc * n].rearrange("p (kc n) -> p kc n", kc=kc)
                self.dma("sp", dst, src, self.d_w[slot], writes=[self.B_w[slot]])
            else:
                src = self.wscr[wname].ap()[:, c0:c0 + n].rearrange("(kc p) n -> p kc n", p=128)
                dst = self.wring[slot][:, 0:kc * n].rearrange("p (kc n) -> p kc n", kc=kc)
                self.dma("sp", dst, src, self.d_w[slot], reads=[self.scrbuf[b]], writes=[self.B_w[slot]])
            self.wissued += 1
        self.wpos += 1
        slot = i % NW
        wname, kc, c0, n = WB[bid]
        if wname == "w_ada":
            return self.wring[slot][:, :].bitcast(F32)[:, 0:kc * n].rearrange("p (kc n) -> p kc n", kc=kc), self.B_w[slot]
        return self.wring[slot][:, 0:kc * n].rearrange("p (kc n) -> p kc n", kc=kc), self.B_w[slot]

    def build(self, n_state=NT, n_main=NT, do_sample=True):
        nc, S = self.nc, self.S
        self.n_state, self.n_main, self.do_sample = n_state, n_main, do_sample
        I = {}
        for name, shape in [("x_main", [NT * T, D]), ("x_prev", [NT * T, D]), ("x_s", [TS, D]), ("c2", [2, D]),
                            ("cache_k", [512, D]), ("cache_v", [512, D]), ("state_conv", [3, 3072]),
                            ("state_ssm", [2048, 128]), ("flags", [128, 2]), ("consts", [128, NCONST]),
                            ("norm_g", [D]), ("w_ada", [D, 3 * D]), ("b_ada", [3 * D]), ("w_in", [D, IN_DIM]),
                            ("q_norm_g", [64]), ("k_norm_g", [64]), ("rel_bias", [16, 513]), ("w_att", [D, D]),
                            ("conv_w", [4, 3072]), ("conv_b", [3072]), ("dt_bias", [32]), ("a_log", [32]),
                            ("d_skip", [32]), ("ssd_norm_g", [2048]), ("w_ssd", [2048, D]), ("w_out", [D, D])]:
            I[name] = self.din(name, shape)
        self.I = I
        O = {}
        for name, shape in [("y_main", [NT * T, D]), ("y_s", [TS, D]), ("nk", [512, D]), ("nv", [512, D]),
                            ("nconv", [3, 3072]), ("nssm", [2048, 128]), ("nk_s", [TS, D]), ("nv_s", [TS, D]),
                            ("nconv_s", [3, 3072]), ("nssm_s", [2048, 128])]:
            O[name] = self.dout(name, shape)
        self.O = O
        self.wscr = {}
        for wname, shape in [("w_in", [D, IN_DIM]), ("w_att", [D, D]), ("w_ssd", [2048, D]), ("w_out", [D, D])]:
            self.wscr[wname] = nc.dram_tensor(wname + "_b", shape, BF, kind="Internal")
        self.ext = nc.dram_tensor("ext_bias", [16, 1024], F32, kind="Internal")

        self.psum = self.st.enter_context(nc.psum_tensor("psum", [128, 4096], F32))
        self.pb = S.bufs("bank", 8)
        self.bnext = 0
        self.bpinned = set()

        self.d_w = [S.dsem("w%d" % i) for i in range(NW)]
        self.d_cv = S.dsem("cv")
        self.d_par = S.dsem("par")
        self.d_x = [S.dsem("x%d" % i) for i in range(4)]
        self.d_out = S.dsem("out")
        self.d_misc = S.dsem("misc")
        self.d_bc = S.dsem("bc")
        self.d_sc = S.dsem("sc")
        self.d_ss = S.dsem("ss")
        self.d_ost = [S.dsem("ost0")] * 2
        self.d_xo = [S.dsem("xo%d" % i) for i in range(4)]
        self.d_cvo = S.dsem("cvo")
        self.d_sso = S.dsem("sso")
        self.d_cvs = [S.dsem("cv%d" % i) for i in range(4)]
        self.B_cvslot = S.bufs("cvslot", 4)

        sb = self.sb
        self.cst = sb("cst", [128, NCONST], F32)
        self.identb = sb("identb", [128, 128], BF)
        self.strib = sb("strib", [128, 128], BF)
        self.bonesb = sb("bonesb", [128, 128], BF)
        self.wring = [sb("wring%d" % i, [128, 4096], BF) for i in range(NW)]
        self.xt = sb("xt", [128, 4, D], F32)
        self.xn = [sb("xn%d" % i, [128, D], BF) for i in range(2)]
        self.hT2 = [sb("hT%d" % i, [128, 8, T], BF) for i in range(2)]
        self.hT = self.hT2[0]
        self.qT = sb("qT", [128, 8, T], BF)
        self.kT = sb("kT", [128, 8, 3 * T], BF)
        self.Vr = sb("Vr", [128, 6, 1040], BF)
        self.zatt = sb("zatt", [128, 2, D], BF)
        self.zssd = sb("zssd", [128, 2, 2048], BF)
        self.sq = [sb("sq%d" % i, [128, T], BF) for i in range(2)]
        self.rs = [sb("rs%d" % i, [128, T], F32) for i in range(2)]
        self.PT = [sb("PT%d" % i, [128, 640], BF) for i in range(2)]
        self.tab = sb("tab", [128, 16, 384], BF)
        self.bconst = sb("bconst", [128, 16], F32)
        self.negb = sb("negb", [128, 16], F32)
        self.rec = sb("rec", [128, 16], F32)
        self.Osb = sb("Osb", [128, D], BF)
        self.OgT = sb("OgT", [128, 8, T], BF)
        self.sig = sb("sig", [128, 8, T], BF)
        self.pre = sb("pre", [128, 8, T], BF)
        self.tmpm = [sb("tmpm%d" % i, [128, T], BF) for i in range(1)]
        self.craw = [sb("craw%d" % i, [128, T + 3], F32) for i in range(2)]
        self.cacc = [sb("cacc%d" % i, [128, T], F32) for i in range(2)]
        self.xsT = sb("xsT", [128, 16, T], BF)
        self.BT = sb("BT", [128, 4, T], BF)
        self.CT = sb("CT", [128, 4, T], BF)
        self.halo = sb("halo", [128, 24, 3], F32)
        self.dtraw = sb("dtraw", [128, 2, 32], F32)
        self.dtt = sb("dtt", [128, 32], F32)
        self.dt = sb("dt", [128, 32], F32)
        self.dta = sb("dta", [128, 32], F32)
        self.EW = sb("EW", [128, 64], F32)
        self.EA = sb("EA", [128, 64], F32)
        self.R = sb("R", [128, 1024], BF)
        self.dec = sb("dec", [128, 2048], BF)
        self.cbm = sb("cbm", [128, 256], BF)
        self.xdt = sb("xdt", [128, 2048], BF)
        self.xsD = sb("xsD", [128, 2048], BF)
        self.xdtw = self.xsD
        self.Btok = sb("Btok", [128, 512], BF)
        self.y = sb("y", [128, 2048], F32)
        self.ytmp = [sb("ytmp%d" % i, [128, 512], F32) for i in range(2)]
        self.ynorm = sb("ynorm", [128, 2048], BF)
        self.ss = sb("ss", [128, 8], F32)
        self.yT = sb("yT", [128, 16, T], BF)
        self.H = sb("H", [128, 2048], F32)
        self.Hbf = sb("Hbf", [128, 2048], BF)
        self.ost = [sb("ost0", [128, D], F32)] * 2
        self.gate_tok = sb("gate_tok", [128, D], F32)
        self.normgT = sb("normgT", [128, 8], F32)
        self.badaT = sb("badaT", [128, 24], F32)
        self.cT = sb("cT", [128, 2, 8], F32)
        self.siluT = sb("siluT", [128, 8, 2], F32)
        self.modT = sb("modT", [128, 24, 2], F32)
        self.gmodT = sb("gmodT", [128, 8, 2], F32)
        self.gqk = sb("gqk", [128, 2], F32)
        self.convwT = sb("convwT", [128, 4, 24], F32)
        self.convbT = sb("convbT", [128, 24], F32)
        self.gssdT = sb("gssdT", [128, 16], F32)
        self.rows = sb("rows", [128, 3, 32], F32)
        self.arow = sb("arow", [128, 32], F32)
        self.flg = sb("flg", [128, 2], F32)
        self.Atile = self.ost[0][:, 0:128]
        print("sbuf bytes remaining:", nc.sbuf_bytes_remaining)

        nb = S.buf
        self.B_cst = nb("cst")
        self.B_par = nb("par")
        self.B_w = S.bufs("w", NW)
        self.B_xt = S.bufs("xt", 4)
        self.B_xn = S.bufs("xn", 2)
        self.B_hT2 = [S.bufs("hT%d_" % i, 2) for i in range(2)]
        self.B_hT = self.B_hT2[0]
        self.B_qT = S.bufs("qT", 8)
        self.B_kT = [S.bufs("kT%d_" % s, 8) for s in range(3)]
        self.B_V = [S.bufs("V%d_" % s, 2) for s in range(3)]
        self.B_zatt = S.bufs("zatt", 2)
        self.B_zssd = S.bufs("zssd", 2)
        self.B_sq = S.bufs("sq", 2)
        self.B_rs = S.bufs("rs", 2)
        self.B_PT = S.bufs("PT", 2)
        self.B_tab = nb("tab")
        self.B_rowsep = S.bufs("rowsep", 4)
        self.B_tabw = nb("tabw")
        self.B_rec = nb("rec")
        self.B_Osb = nb("Osb")
        self.B_OgT = nb("OgT")
        self.B_sig = S.bufs("sig", 8)
        self.B_pre = S.bufs("pre", 8)
        self.B_tmpm = S.bufs("tmpm", 2)
        self.B_craw = S.bufs("craw", 2)
        self.B_cacc = S.bufs("cacc", 2)
        self.B_xsT = S.bufs("xsT", 16)
        self.B_BT = S.bufs("BT", 4)
        self.B_CT = S.bufs("CT", 4)
        self.B_halo = S.bufs("halo", 24)
        self.B_dtraw = S.bufs("dtraw", 2)
        self.B_dts = nb("dts")
        self.B_R = nb("R")
        self.B_dec = S.bufs("dec", 2)
        self.B_cbm = nb("cbm")
        self.B_xdt = nb("xdt")
        self.B_xsD = nb("xsD")
        self.B_xdtw = self.B_xsD
        self.B_Btok = nb("Btok")
        self.B_y = S.bufs("y", 4)
        self.B_ytmp = S.bufs("ytmp", 2)
        self.B_ynorm = nb("ynorm")
        self.B_ss = nb("ss")
        self.B_yT = S.bufs("yT", 2)
        self.B_H = S.bufs("H", 4)
        self.B_Hbf = S.bufs("Hbf", 4)
        self.B_ost = [S.buf("ost")] * 2
        self.B_gate = nb("gate")
        self.B_mod = nb("mod")
        self.B_A = self.B_ost[0]
        self.scrbuf = {b: nb("scr_" + b) for b in WB}
        self.B_ext = nb("ext")
        self.rot = {}
        self.state_mode = False
        self.crawx = list(self.craw) + [self.y[:, k * 260:k * 260 + T + 3] for k in range(3)]
        self.caccx = list(self.cacc) + [self.y[:, 1024 + k * 256:1024 + k * 256 + T] for k in range(3)]
        self.B_crawx = list(self.B_craw) + S.bufs("crawx", 3)
        self.B_caccx = list(self.B_cacc) + S.bufs("caccx", 3)

        seq = list(ADA_BLOCKS)
        if do_sample:
            seq += MAIN_BLOCKS
        for i in range(n_state):
            seq += STATE_LAST_BLOCKS if i == n_state - 1 else (STATE_KV_BLOCKS if i == n_state - 2 else STATE_BLOCKS)
        for i in range(n_main):
            seq += MAIN_BLOCKS
        self.wseq, self.wpos, self.wissued = seq, 0, 0

        self.setup()
        if do_sample:
            self.sample_tile()
        self.state_pass()
        self.main_pass()
        assert self.wpos == len(self.wseq), (self.wpos, len(self.wseq))
        S.finish()
        S.emit(nc, self.st)
        self.st.close()
        return nc

    def rotn(self, key, n=2):
        i = self.rot.get(key, 0)
        self.rot[key] = (i + 1) % n
        return i

    def setup(self):
        nc, S, I = self.nc, self.S, self.I
        cst = self.cst
        self.dma("sp", cst[:], I["consts"].ap(), self.d_par, writes=[self.B_cst])
        order = MAIN_BLOCKS
        for bi, b in enumerate(order):
            wname, kc, c0, n = WB[b]
            self.dma("pool", self.wscr[wname].ap()[:, c0:c0 + n], I[wname].ap()[:, c0:c0 + n], self.d_cvs[bi % 4],
                     writes=[self.scrbuf[b], self.B_cvslot[bi % 4]])
        self.dma("sp", self.y[0:16, 0:513], I["rel_bias"].ap(), self.d_misc, writes=[self.B_y[0], self.B_y[1]])
        self.cp("dve", self.y[0:16, 513:1024], self.y[0:16, 512:513].to_broadcast([16, 511]), reads=[self.B_y[0], self.B_y[1]],
                writes=[self.B_y[0], self.B_y[1]])
        self.dma("sp", self.ext.ap(), self.y[0:16, 0:1024], self.d_misc, reads=[self.B_y[0], self.B_y[1]], writes=[self.B_ext])
        P = self.d_par
        dm = lambda out, in_: self.dma("sp", out, in_, P, writes=[self.B_par], slow=True)
        dm(self.normgT[:], I["norm_g"].ap().rearrange("(kc p) -> p kc", p=128))
        dm(self.badaT[:], I["b_ada"].ap().rearrange("(t p) -> p t", p=128))
        for s_ in range(2):
            dm(self.cT[:, s_, :], I["c2"].ap()[s_, :].rearrange("(kc p) -> p kc", p=128))
        for half in range(2):
            dm(self.gqk[half * 64:(half + 1) * 64, 0:1], I["q_norm_g"].ap().rearrange("(p o) -> p o", o=1))
            dm(self.gqk[half * 64:(half + 1) * 64, 1:2], I["k_norm_g"].ap().rearrange("(p o) -> p o", o=1))
        for j_ in range(4):
            dm(self.convwT[:, j_, :], I["conv_w"].ap()[j_, :].rearrange("(t p) -> p t", p=128))
        dm(self.convbT[:], I["conv_b"].ap().rearrange("(t p) -> p t", p=128))
        dm(self.gssdT[:], I["ssd_norm_g"].ap().rearrange("(t p) -> p t", p=128))
        dm(self.rows[:, 0, :], I["dt_bias"].ap().partition_broadcast(128))
        dm(self.rows[:, 1, :], I["a_log"].ap().partition_broadcast(128))
        dm(self.rows[:, 2, :], I["d_skip"].ap().partition_broadcast(128))
        dm(self.flg[:], I["flags"].ap())
        S.seal(P, [self.B_cst, self.B_par])
        Bc, Bp = self.B_cst, self.B_par
        self.cp("dve", self.identb[:], cst[:, C_ID:C_ID + 128], reads=[Bc], writes=[Bc])
        self.cp("dve", self.strib[:], cst[:, C_STRI:C_STRI + 128], reads=[Bc], writes=[Bc])
        self.cp("dve", self.bonesb[:], cst[:, C_BONES:C_BONES + 128], reads=[Bc], writes=[Bc])
        self.cp("dve", cst[:, C_BONES:C_BONES + 128], cst[:, C_STRI:C_STRI + 128], reads=[Bc], writes=[Bc])
        self.tt("dve", cst[:, C_BONES:C_BONES + 64], cst[:, C_BONES:C_BONES + 64], cst[:, C_CSEL + 128:C_CSEL + 192], ALU.add,
                reads=[Bc], writes=[Bc])
        self.act(self.arow[:], self.rows[:, 1, :], AF.Exp, reads=[Bp], writes=[Bp])
        self.ts("dve", self.arow[:], self.arow[:], -1.0, None, ALU.mult, reads=[Bp], writes=[Bp])
        self.memset("pool", self.Vr[:].rearrange("p b (h c) -> p b h c", c=65)[:, :, :, 64:65], 1.0,
                    writes=[b for s in self.B_V for b in s])
        self.act(self.siluT[:], self.cT[:].rearrange("p s k -> p k s"), AF.Silu, reads=[Bp], writes=[Bp])
        bank = self.bank_get()
        for bi, bid in enumerate(ADA_BLOCKS):
            wt, wbuf = self.wget(bid)
            for ct in range(2):
                t = bi * 2 + ct
                for kc in range(8):
                    self.mm(self.PS(bank, 2 * t, 2 * t + 2), wt[:, kc, ct * 128:(ct + 1) * 128], self.siluT[:, kc, :],
                            kc == 0, kc == 7, reads=[wbuf, Bp], banks=self.bk(bank))
        self.tt("dve", self.modT[:], self.PS(bank, 0, 48).rearrange("p (t s) -> p t s", s=2),
                self.badaT[:].unsqueeze(2).to_broadcast([128, 24, 2]), ALU.add,
                reads=[Bp], writes=[self.B_mod], banks=self.bk(bank))
        self.ts("dve", self.gmodT[:], self.modT[:, 8:16, :], 1.0, None, ALU.add, reads=[self.B_mod], writes=[self.B_mod])
        self.tt("dve", self.gmodT[:], self.gmodT[:], self.normgT[:].unsqueeze(2).to_broadcast([128, 8, 2]), ALU.mult,
                reads=[Bp, self.B_mod], writes=[self.B_mod])
        self.dump("modT", self.modT[:], [self.B_mod])
        srcc = bass.AP(self.ext, 600, [[0, 128], [1024, 16], [1, 1]])
        self.dma("sp", self.bconst[:, :].unsqueeze(2), srcc, self.d_bc, reads=[self.B_ext], writes=[self.B_tab], slow=True)
        self.ts("dve", self.negb[:], self.bconst[:], -1.0, None, ALU.mult, reads=[self.B_tab], writes=[self.B_tab])
        hk = self.y
        for h in range(16):
            src = bass.AP(self.ext, h * 1024 + 129, [[1, 128], [1, 384]])
            self.dma("sp", hk[:, 0:384], src, self.d_misc, reads=[self.B_ext], writes=[self.B_y[0]], slow=True)
            bank = self.bank_get()
            self.mm(self.PS(bank, 0, 384), cst[:, C_J:C_J + 128], hk[:, 0:384], True, True,
                    reads=[Bc, self.B_y[0]], banks=self.bk(bank))
            self.act(self.tab[:, h, :], self.PS(bank, 0, 384), AF.Exp, reads=[self.B_tab], writes=[self.B_tabw], banks=self.bk(bank),
                     bias=self.negb[:, h:h + 1])
        self.memset("pool", self.tab[64:128, :, 0:64], 0.0, writes=[self.B_tabw])
        self.S.op("pool", lambda e: e.memset(self.negb[:, 0:1], 0.0), reads=[self.B_tabw], writes=[self.B_tab], cost=100.0)
        self.dump("tab", self.tab[:, 0, :], [self.B_tab])

    def make_gate_tok(self, seq):
        cst = self.cst
        bank = self.bank_get(2)
        for kc in range(8):
            self.act(self.Atile, cst[:, 0:128], AF.Identity, reads=[self.B_mod, self.B_cst], writes=[self.B_A],
                     scale=0.0, bias=self.modT[:, 16 + kc, seq:seq + 1])
            self.mm(self.psum[:, bank * 512 + kc * 128: bank * 512 + (kc + 1) * 128], self.Atile, cst[:, C_ID:C_ID + 128],
                    True, True, reads=[self.B_A, self.B_cst], banks=self.bk(bank + kc // 4))
        self.cp("dve", self.gate_tok[:], self.psum[:, bank * 512: bank * 512 + 1024], writes=[self.B_gate],
                banks=self.bk(bank, 2))

    def load_norm(self, src_ap, row0, Tn, seq, plain_kslot=None, par=0):
        cst = self.cst
        nbk = (Tn + 127) // 128
        for b in range(nbk):
            bs = min(128, Tn - 128 * b)
            xb_ = 2 * par + b
            self.dma("sp", self.xt[0:bs, xb_, :], src_ap[row0 + 128 * b: row0 + 128 * b + bs, :], self.d_x[xb_],
                     writes=[self.B_xt[xb_]])
            i = self.rotn("xn")
            xn, Bxn = self.xn[i], self.B_xn[i]
            if plain_kslot is None:
                self.act(xn[0:bs, :], self.xt[0:bs, xb_, :], AF.Square, reads=[self.B_xt[xb_]], writes=[Bxn, self.B_ss],
                         accum_out=self.ss[0:bs, 4:5])
                self.act(self.ss[0:bs, 5:6], self.ss[0:bs, 4:5], AF.Ln, reads=[self.B_ss], writes=[self.B_ss],
                         scale=1.0 / D, bias=EPS)
                self.act(self.ss[0:bs, 5:6], self.ss[0:bs, 5:6], AF.Exp, reads=[self.B_ss], writes=[self.B_ss], scale=-0.5)
                self.act(xn[0:bs, :], self.xt[0:bs, xb_, :], AF.Copy, reads=[self.B_xt[xb_], self.B_ss], writes=[Bxn],
                         scale=self.ss[0:bs, 5:6])
            else:
                self.cp("act", xn[0:bs, :], self.xt[0:bs, xb_, :], reads=[self.B_xt[xb_]], writes=[Bxn])
            bank = self.bank_get()
            for kc in range(8):
                self.tp(self.PSB(bank, 1, kc * 128, kc * 128 + bs), xn[0:bs, kc * 128:(kc + 1) * 128],
                        self.identb[0:bs, 0:bs], reads=[Bxn, self.B_cst], banks=self.bk(bank))
            if plain_kslot is None:
                for kc in range(8):
                    self.ts("dve", self.hT[:, kc, 128 * b:128 * b + bs], self.PSB(bank, 1, kc * 128, kc * 128 + bs),
                            self.gmodT[:, kc, seq:seq + 1], self.modT[:, kc, seq:seq + 1], ALU.mult, ALU.add,
                            reads=[self.B_mod], writes=[self.B_hT[b]], banks=self.bk(bank))
            else:
                slot, half = plain_kslot[b]
                self.cp("dve", self.kT[:, :, slot * T + half * 128: slot * T + half * 128 + bs],
                        self.PSB(bank, 1, 0, 1024).rearrange("p (k t) -> p k t", t=128)[:, :, 0:bs],
                        writes=self.B_kT[slot], banks=self.bk(bank))

    def proj_fm(self, wt, wbuf, ct, Tn, bank, c0):
        for kc in range(8):
            self.mm(self.PS(bank, c0, c0 + Tn), wt[:, kc, ct * 128:(ct + 1) * 128], self.hT[:, kc, 0:Tn],
                    kc == 0, kc == 7, reads=[wbuf] + self.B_hT, banks=self.bk(bank))

    def proj_tm(self, wt, wbuf, b, bs, ncols, bank):
        for kc in range(8):
            self.mm(self.PS(bank, 0, ncols, 0, bs), self.hT[:, kc, 128 * b:128 * b + bs], wt[:, kc, 0:ncols],
                    kc == 0, kc == 7, reads=[wbuf, self.B_hT[b]], banks=self.bk(bank))

    def qk_tile(self, which, wt, wbuf, ctl, hp, Tn, slot):
        bank = self.bank_get()
        self.proj_fm(wt, wbuf, ctl, Tn, bank, 0)
        ps = self.PS(bank, 0, Tn)
        i = self.rotn("sq")
        sq, rs = self.sq[i], self.rs[i]
        self.act(sq[:, 0:Tn], ps, AF.Square, writes=[self.B_sq[i]], banks=self.bk(bank))
        bank2 = self.bank_get()
        self.mm(self.PS(bank2, 0, Tn), self.bonesb[:], sq[:, 0:Tn], True, True, reads=[self.B_sq[i], self.B_cst],
                banks=self.bk(bank2))
        self.act(rs[:, 0:Tn], self.PS(bank2, 0, Tn), AF.Ln, writes=[self.B_rs[i]], banks=self.bk(bank2),
                 scale=1.0 / 64, bias=EPS)
        self.act(rs[:, 0:Tn], rs[:, 0:Tn], AF.Exp, reads=[self.B_rs[i]], writes=[self.B_rs[i]], scale=-0.5)
        if which == "q":
            dst, Bd, g = self.qT[:, hp, 0:Tn], self.B_qT[hp], self.gqk[:, 0:1]
        else:
            dst, Bd, g = self.kT[:, hp, slot * T: slot * T + Tn], self.B_kT[slot][hp], self.gqk[:, 1:2]
        self.stt(dst, ps, g, rs[:, 0:Tn], ALU.mult, ALU.mult, reads=[self.B_rs[i], self.B_par], writes=[Bd],
                 banks=self.bk(bank))

    def keyblock(self, g, slot, bss):
        if g >= 0:
            return slot * T + g * 128, slot * 2 + g, (slot, g), bss[g]
        j = g + 4
        hs = (slot - 2) % 3 if j < 2 else (slot - 1) % 3
        half = j % 2
        return hs * T + half * 128, hs * 2 + half, (hs, half), 128

    def attention(self, Tn, slot, tile_idx):
        nbk = (Tn + 127) // 128
        bss = [min(128, Tn - 128 * b) for b in range(nbk)]
        for p in range(nbk):
            qn = bss[p]
            ob = self.bank_pin(3)
            for h in range(16):
                hp, base = h // 2, 64 * (h % 2)
                sbk = self.bank_get(2)
                pi = self.rotn("PT")
                PT, BPT = self.PT[pi], self.B_PT[pi]
                kinfo = []
                for r in range(5):
                    g = p - r
                    kcol, vblk, vsh, kn = self.keyblock(g, slot, bss)
                    kslot = kcol // T
                    kinfo.append((kcol, vblk, vsh, kn, kslot))
                    self.mm(self.psum[0:kn, sbk * 512 + r * 128: sbk * 512 + r * 128 + qn],
                            self.kT[base:base + 64, hp, kcol:kcol + kn], self.qT[base:base + 64, hp, p * 128:p * 128 + qn],
                            True, True, reads=[self.B_kT[kslot][hp], self.B_qT[hp]], banks=self.bk(sbk + r // 4))
                runs = []
                for r in range(5):
                    kn = kinfo[r][3]
                    masked = (tile_idx is not None) and (tile_idx * 2 + p - r < 0)
                    key = (kn, masked)
                    if runs and runs[-1][0] == key:
                        runs[-1][2] = r + 1
                    else:
                        runs.append([key, r, r + 1])
                for (kn, masked), r0, r1 in runs:
                    src = self.psum[0:kn, sbk * 512 + r0 * 128: sbk * 512 + r1 * 128].rearrange("p (r i) -> p r i", i=128)[:, :, 0:qn]
                    dst = PT[0:kn, r0 * 128:r1 * 128].rearrange("p (r i) -> p r i", i=128)[:, :, 0:qn]
                    if masked:
                        self.act(dst, src, AF.Exp, reads=[self.B_par], writes=[BPT], banks=self.bk(sbk, 2), scale=0.125,
                                 bias=self.flg[0:kn, 1:2])
                    else:
                        self.act(dst, src, AF.Exp, reads=[self.B_tab], writes=[BPT], banks=self.bk(sbk, 2), scale=0.125,
                                 bias=self.bconst[0:kn, h:h + 1])
                    r1t = min(r1, 3)
                    if r0 < r1t:
                        dstt = PT[0:kn, r0 * 128:r1t * 128].rearrange("p (r i) -> p r i", i=128)[:, :, 0:qn]
                        tb = self.tab[0:kn, h, r0 * 128:r1t * 128].rearrange("p (r i) -> p r i", i=128)[:, :, 0:qn]
                        self.tt("dve", dstt, dstt, tb, ALU.mult, reads=[self.B_tab, BPT], writes=[BPT])
                if qn > 64 and kinfo[4][3] == 128:
                    self.memset("pool", PT[0:64, 4 * 128 + 64: 4 * 128 + qn], 0.0, writes=[BPT])
                obank = ob + h // 7
                oc = (h % 7) * 65
                for r in range(5):
                    kcol, vblk, vsh, kn, kslot = kinfo[r]
                    self.mm(self.psum[0:qn, obank * 512 + oc: obank * 512 + oc + 65], PT[0:kn, r * 128:r * 128 + qn],
                            self.Vr[0:kn, vblk, h * 65:(h + 1) * 65], r == 0, r == 4,
                            reads=[BPT, self.B_V[vsh[0]][vsh[1]]], banks=self.bk(obank))
            for bi, (h0, nh) in enumerate([(0, 7), (7, 7), (14, 2)]):
                o3 = self.psum[0:qn, (ob + bi) * 512: (ob + bi) * 512 + nh * 65].rearrange("p (h c) -> p h c", c=65)
                rec_o, rec_i = self.rec[0:qn, h0:h0 + nh].unsqueeze(2), o3[:, :, 64:65]
                self.S.op("dve", lambda e, rec_o=rec_o, rec_i=rec_i: e.reciprocal(out=rec_o, in_=rec_i),
                          writes=[self.B_rec], banks=self.bk(ob + bi), cost=200.0)
                self.tt("dve", self.Osb[0:qn, h0 * 64:(h0 + nh) * 64].rearrange("p (h c) -> p h c", c=64), o3[:, :, 0:64],
                        self.rec[0:qn, h0:h0 + nh].unsqueeze(2).to_broadcast([qn, nh, 64]), ALU.mult,
                        reads=[self.B_rec], writes=[self.B_Osb], banks=self.bk(ob + bi))
            self.bank_unpin(ob, 3)
            self.tt("pool", self.Osb[0:qn, :], self.Osb[0:qn, :], self.zatt[0:qn, p, :], ALU.mult,
                    reads=[self.B_Osb, self.B_zatt[p]], writes=[self.B_Osb])
            bank = self.bank_get()
            for kc in range(8):
                self.tp(self.PSB(bank, 1, kc * 128, kc * 128 + qn), self.Osb[0:qn, kc * 128:(kc + 1) * 128],
                        self.identb[0:qn, 0:qn], reads=[self.B_Osb, self.B_cst], banks=self.bk(bank))
            self.cp("act", self.OgT[:, :, p * 128:p * 128 + qn],
                    self.PSB(bank, 1, 0, 1024).rearrange("p (k t) -> p k t", t=128)[:, :, 0:qn],
                    writes=[self.B_OgT], banks=self.bk(bank))

    def conv_tile(self, ct, bank, c0, Tn):
        if self.state_mode:
            i = self.rotn("convx", 5)
            raw, acc = self.crawx[i], self.caccx[i]
            Br, Ba = self.B_crawx[i], self.B_caccx[i]
        else:
            i = self.rotn("conv")
            raw, acc = self.craw[i], self.cacc[i]
            Br, Ba = self.B_craw[i], self.B_cacc[i]
        self.cp("pool", raw[:, 0:3], self.halo[:, ct, :], reads=[self.B_halo[ct]], writes=[Br])
        self.cp("act", raw[:, 3:3 + Tn], self.PS(bank, c0, c0 + Tn), writes=[Br], banks=self.bk(bank))
        self.cp("pool", self.halo[:, ct, :], raw[:, Tn:Tn + 3], reads=[Br], writes=[self.B_halo[ct]])
        self.act(acc[:, 0:Tn], raw[:, 0:Tn], AF.Identity, reads=[Br, self.B_par], writes=[Ba],
                 scale=self.convwT[:, 0, ct:ct + 1], bias=self.convbT[:, ct:ct + 1])
        for j in range(1, 4):
            self.stt(acc[:, 0:Tn], raw[:, j:j + Tn], self.convwT[:, j, ct:ct + 1], acc[:, 0:Tn], ALU.mult, ALU.add,
                     reads=[Br, Ba, self.B_par], writes=[Ba])
        if ct < 16:
            dst, Bd = self.xsT[:, ct, 0:Tn], self.B_xsT[ct]
        elif ct < 20:
            dst, Bd = self.BT[:, ct - 16, 0:Tn], self.B_BT[ct - 16]
        else:
            dst, Bd = self.CT[:, ct - 20, 0:Tn], self.B_CT[ct - 20]
        self.act(dst, acc[:, 0:Tn], AF.Silu, reads=[Ba], writes=[Bd])

    def ssd_block(self, b, bs, state_only):
        cst = self.cst
        nch = (bs + 63) // 64
        Lc = min(64, bs)
        t0 = 128 * b
        Bd = self.B_dts
        x = self.dtraw[0:bs, b, :]
        self.act(self.dtt[0:bs, :], x, AF.Abs, reads=[self.B_dtraw[b]], writes=[Bd])
        self.act(self.dtt[0:bs, :], self.dtt[0:bs, :], AF.Exp, reads=[Bd], writes=[Bd], scale=-1.0)
        self.act(self.dtt[0:bs, :], self.dtt[0:bs, :], AF.Ln, reads=[Bd], writes=[Bd], bias=1.0)
        self.stt(self.dt[0:bs, :], x, 0.0, self.dtt[0:bs, :], ALU.max, ALU.add, reads=[self.B_dtraw[b], Bd], writes=[Bd])
        self.tt("dve", self.dta[0:bs, :], self.dt[0:bs, :], self.arow[0:bs, :], ALU.mult, reads=[Bd, self.B_par], writes=[Bd])
        bk = self.bank_get()
        self.mm(self.PS(bk, 0, 32, 0, bs), cst[0:bs, C_TRI:C_TRI + bs], self.dta[0:bs, :], True, True, reads=[Bd, self.B_cst], banks=self.bk(bk))
        mrg = state_only and bs == 128
        if mrg:
            self.mm(self.PS(bk, 32, 64, 0, bs), cst[0:bs, C_BONES:C_BONES + bs], self.dta[0:bs, :], True, True, reads=[Bd, self.B_cst], banks=self.bk(bk))
            for c in range(2):
                self.mm(self.PS(bk, 64, 96), cst[0:bs, C_CSEL + 128 * c:C_CSEL + 128 * c + 128], self.dta[0:bs, :],
                        c == 0, c == 1, reads=[Bd, self.B_cst], banks=self.bk(bk))
        else:
            self.mm(self.PS(bk, 32, 64, 0, bs), cst[0:bs, C_STRI:C_STRI + bs], self.dta[0:bs, :], True, True, reads=[Bd, self.B_cst], banks=self.bk(bk))
            for c in range(nch):
                self.mm(self.PS(bk, 64 + 32 * c, 96 + 32 * c), cst[0:bs, C_CSEL + 128 * c:C_CSEL + 128 * c + 128], self.dta[0:bs, :],
                        True, True, reads=[Bd, self.B_cst], banks=self.bk(bk))
        self.act(self.EW[0:bs, :], self.PS(bk, 0, 64, 0, bs), AF.Exp, writes=[Bd], banks=self.bk(bk))
        nea = 1 if mrg else nch
        self.act(self.EA[:, 0:32 * nea], self.PS(bk, 64, 64 + 32 * nea), AF.Exp, writes=[Bd], banks=self.bk(bk))
        E3 = self.EW[0:bs, 0:32].unsqueeze(2)
        W3 = self.EW[0:bs, 32:64].unsqueeze(2)
        dt3 = self.dt[0:bs, :].unsqueeze(2)
        xb = self.bank_get(2)
        for ct in range(16):
            self.tp(self.PSB(xb, 2, ct * 128, ct * 128 + 128, 0, bs), self.xsT[:, ct, t0:t0 + bs], self.identb[:, :],
                    reads=[self.B_xsT[ct], self.B_cst], banks=self.bk(xb + ct // 8))
        xs3 = self.PSB(xb, 2, 0, 2048, 0, bs).rearrange("p (h c) -> p h c", c=64)
        self.tt("dve", self.xdt[0:bs, :].rearrange("p (h c) -> p h c", c=64), xs3, dt3.to_broadcast([bs, 32, 64]), ALU.mult,
                reads=[Bd], writes=[self.B_xdt], banks=self.bk(xb, 2))
        if not state_only:
            self.tt("dve", self.xsD[0:bs, :].rearrange("p (h c) -> p h c", c=64), xs3,
                    self.rows[0:bs, 2, :].unsqueeze(2).to_broadcast([bs, 32, 64]), ALU.mult,
                    reads=[self.B_par], writes=[self.B_xsD], banks=self.bk(xb, 2))
        bb = self.bank_get()
        for g in range(4):
            self.tp(self.PSB(bb, 1, g * 128, g * 128 + 128, 0, bs), self.BT[:, g, t0:t0 + bs], self.identb[:, :],
                    reads=[self.B_BT[g], self.B_cst], banks=self.bk(bb))
        self.cp("act", self.Btok[0:bs, :], self.PSB(bb, 1, 0, 512, 0, bs), writes=[self.B_Btok], banks=self.bk(bb))

        if not state_only:
            cb = self.bank_get()
            for g in range(4):
                for c in range(nch):
                    cs = t0 + 64 * c
                    self.mm(self.PS(cb, g * 64, g * 64 + Lc, 64 * c, 64 * c + Lc), self.BT[:, g, cs:cs + Lc], self.CT[:, g, cs:cs + Lc],
                            True, True, reads=[self.B_BT[g], self.B_CT[g]], banks=self.bk(cb))
            self.tt("dve", self.cbm[0:bs, :].rearrange("p (g l) -> p g l", l=64)[:, :, 0:Lc],
                    self.PS(cb, 0, 256, 0, bs).rearrange("p (g l) -> p g l", l=64)[:, :, 0:Lc],
                    cst[0:bs, C_TRIL:C_TRIL + Lc].unsqueeze(1).to_broadcast([bs, 4, Lc]), ALU.mult,
                    reads=[self.B_cst], writes=[self.B_cbm], banks=self.bk(cb))
            for hh in range(2):
                R3 = self.R[0:bs, 0:16 * Lc].rearrange("p (h l) -> p h l", l=Lc)
                self.tt("dve", R3, cst[0:bs, C_TRIL:C_TRIL + Lc].unsqueeze(1).to_broadcast([bs, 16, Lc]),
                        self.dta[0:bs, 16 * hh:16 * hh + 16].unsqueeze(2).to_broadcast([bs, 16, Lc]), ALU.mult,
                        reads=[Bd, self.B_cst], writes=[self.B_R])
                sg = self.bank_get(2)
                ncol = 16 * Lc
                for j in range((ncol + 511) // 512):
                    cw = min(512, ncol - 512 * j)
                    self.mm(self.PS(sg + j, 0, cw, 0, bs), self.strib[0:bs, 0:bs], self.R[0:bs, 512 * j:512 * j + cw], True, True,
                            reads=[self.B_R, self.B_cst], banks=self.bk(sg + j))
                d3 = self.dec[0:bs, hh * 1024: hh * 1024 + ncol]
                self.act(d3, self.psum[0:bs, sg * 512: sg * 512 + ncol], AF.Exp, writes=[self.B_dec[hh]], banks=self.bk(sg, 2))
                d4 = d3.rearrange("p (g e l) -> p g e l", g=2, l=Lc)
                c4 = self.cbm[0:bs, :].rearrange("p (g l) -> p g l", l=64)[:, 2 * hh:2 * hh + 2, 0:Lc].unsqueeze(2).to_broadcast([bs, 2, 8, Lc])
                self.tt("dve", d4, d4, c4, ALU.mult, reads=[self.B_cbm, self.B_dec[hh]], writes=[self.B_dec[hh]])

        ysb = None
        if not state_only:
            ysb = self.bank_pin(4)
            for g in range(4):
                yi = self.bank_get()
                hh = g // 2
                for c in range(nch):
                    r0 = 64 * c
                    for e in range(8):
                        h = 8 * g + e
                        hl = h - 16 * hh
                        self.mm(self.PS(yi, e * 64, e * 64 + 64, r0, r0 + Lc),
                                self.dec[r0:r0 + Lc, hh * 1024 + hl * Lc: hh * 1024 + hl * Lc + Lc],
                                self.xdt[r0:r0 + Lc, h * 64:(h + 1) * 64], True, True,
                                reads=[self.B_dec[hh], self.B_xdt, self.B_rowsep[g]], banks=self.bk(yi))
                    if c == 0:
                        self.mm(self.PS(ysb + g, 0, 512, 0, Lc), self.CT[:, g, t0:t0 + Lc], self.Hbf[:, g * 512:(g + 1) * 512],
                                True, True, reads=[self.B_CT[g], self.B_Hbf[g]], banks=self.bk(ysb + g), writes=[self.B_rowsep[g]])
                self.tt("dve", self.y[0:bs, g * 512:(g + 1) * 512], self.PS(yi, 0, 512, 0, bs), self.xsD[0:bs, g * 512:(g + 1) * 512],
                        ALU.add, reads=[self.B_xsD], writes=[self.B_y[g]], banks=self.bk(yi))
        self.tt("pool", self.xdtw[0:bs, :].rearrange("p (h c) -> p h c", c=64), self.xdt[0:bs, :].rearrange("p (h c) -> p h c", c=64),
                W3.to_broadcast([bs, 32, 64]), ALU.mult, reads=[Bd, self.B_xdt], writes=[self.B_xdtw])
        for c in range(1 if mrg else nch):
            r0 = 64 * c
            if mrg:
                Lc = 128
            for g in range(4):
                if (not state_only) and c > 0:
                    self.mm(self.PS(ysb + g, 0, 512, r0, r0 + Lc), self.CT[:, g, t0 + r0:t0 + r0 + Lc], self.Hbf[:, g * 512:(g + 1) * 512],
                            True, True, reads=[self.B_CT[g], self.B_Hbf[g]], banks=self.bk(ysb + g))
                dh = self.bank_get()
                self.mm(self.PS(dh, 0, 512), self.Btok[r0:r0 + Lc, g * 128:(g + 1) * 128], self.xdtw[r0:r0 + Lc, g * 512:(g + 1) * 512],
                        True, True, reads=[self.B_Btok, self.B_xdtw], banks=self.bk(dh))
                Hg = self.H[:, g * 512:(g + 1) * 512]
                H3 = Hg.rearrange("p (h c) -> p h c", c=64)
                self.tt("pool" if state_only else "dve", H3, H3,
                        self.EA[:, 32 * c + 8 * g:32 * c + 8 * g + 8].unsqueeze(2).to_broadcast([128, 8, 64]), ALU.mult,
                        reads=[Bd, self.B_H[g]], writes=[self.B_H[g]])
                self.tt("dve", Hg, Hg, self.PS(dh, 0, 512), ALU.add, reads=[self.B_H[g]], writes=[self.B_H[g]], banks=self.bk(dh))
                if not state_only:
                    self.cp("act", self.Hbf[:, g * 512:(g + 1) * 512], Hg, reads=[self.B_H[g]], writes=[self.B_Hbf[g]])
        if state_only:
            return
        for g in range(4):
            i = self.rotn("ytmp")
            self.tt("dve", self.ytmp[i][0:bs, :].rearrange("p (h c) -> p h c", c=64),
                    self.PS(ysb + g, 0, 512, 0, bs).rearrange("p (h c) -> p h c", c=64),
                    E3[:, 8 * g:8 * g + 8, :].to_broadcast([bs, 8, 64]), ALU.mult,
                    reads=[Bd], writes=[self.B_ytmp[i]], banks=self.bk(ysb + g))
            yg = self.y[0:bs, g * 512:(g + 1) * 512]
            self.tt("dve", yg, yg, self.ytmp[i][0:bs, :], ALU.add, reads=[self.B_ytmp[i], self.B_y[g]], writes=[self.B_y[g]])
            self.tt("pool", yg, yg, self.zssd[0:bs, b, g * 512:(g + 1) * 512], ALU.mult, reads=[self.B_zssd[b], self.B_y[g]],
                    writes=[self.B_y[g]])
            self.act(self.ynorm[0:bs, g * 512:(g + 1) * 512], yg, AF.Square, reads=[self.B_y[g]], writes=[self.B_ynorm, self.B_ss],
                     accum_out=self.ss[0:bs, g:g + 1])
        self.bank_unpin(ysb, 4)
        self.act(self.ss[0:bs, 0:4], self.ss[0:bs, 0:4], AF.Ln, reads=[self.B_ss], writes=[self.B_ss], scale=1.0 / 512, bias=EPS)
        self.act(self.ss[0:bs, 0:4], self.ss[0:bs, 0:4], AF.Exp, reads=[self.B_ss], writes=[self.B_ss], scale=-0.5)
        for g in range(4):
            self.act(self.ynorm[0:bs, g * 512:(g + 1) * 512], self.y[0:bs, g * 512:(g + 1) * 512], AF.Copy,
                     reads=[self.B_y[g], self.B_ss], writes=[self.B_ynorm], scale=self.ss[0:bs, g:g + 1])
        yb = self.bank_get(2)
        for ct in range(16):
            self.tp(self.PSB(yb, 2, ct * 128, ct * 128 + bs), self.ynorm[0:bs, ct * 128:(ct + 1) * 128], self.identb[0:bs, 0:bs],
                    reads=[self.B_ynorm, self.B_cst], banks=self.bk(yb + ct // 8))
        self.tt("dve", self.yT[:, :, t0:t0 + bs], self.PSB(yb, 2, 0, 2048).rearrange("p (k t) -> p k t", t=128)[:, :, 0:bs],
                self.gssdT[:, :].unsqueeze(2).to_broadcast([128, 16, bs]), ALU.mult,
                reads=[self.B_par], writes=[self.B_yT[b]], banks=self.bk(yb, 2))

    def tile(self, mode, src_ap, row0, Tn, seq, slot, tile_idx=None, kv=False, outs=None, out_rows=None, with_c=False):
        par = self.rotn("tilepar")
        nbk = (Tn + 127) // 128
        bss = [min(128, Tn - 128 * b) for b in range(nbk)]
        full = mode == "main"
        self.S.tag = "norm"
        self.hT, self.B_hT = self.hT2[par], self.B_hT2[par]
        self.load_norm(src_ap, row0, Tn, seq, par=par)
        self.dump("hT", self.hT[:, :, 0:Tn], self.B_hT)
        self.S.tag = "qkv"
        if full:
            for j in range(2):
                wt, wbuf = self.wget("q%d" % j)
                for ctl in range(4):
                    self.qk_tile("q", wt, wbuf, ctl, 4 * j + ctl, Tn, slot)
        if full or kv:
            for j in range(2):
                wt, wbuf = self.wget("k%d" % j)
                for ctl in range(4):
                    self.qk_tile("k", wt, wbuf, ctl, 4 * j + ctl, Tn, slot)
            for j in range(2):
                wt, wbuf = self.wget("v%d" % j)
                for b in range(nbk):
                    bank = self.bank_get()
                    self.proj_tm(wt, wbuf, b, bss[b], 512, bank)
                    dst = self.Vr[0:bss[b], slot * 2 + b, j * 520:(j + 1) * 520].rearrange("p (h c) -> p h c", c=65)[:, :, 0:64]
                    self.cp("act", dst, self.PS(bank, 0, 512, 0, bss[b]).rearrange("p (h c) -> p h c", c=64),
                            writes=[self.B_V[slot][b]], banks=self.bk(bank))
        if full:
            self.dump("qT", self.qT[:, :, 0:Tn], self.B_qT)
            self.dump("kT", self.kT[:, :, slot * T:slot * T + Tn], self.B_kT[slot])
            for j in range(2):
                wt, wbuf = self.wget("za%d" % j)
                for b in range(nbk):
                    bank = self.bank_get()
                    self.proj_tm(wt, wbuf, b, bss[b], 512, bank)
                    self.act(self.zatt[0:bss[b], b, j * 512:(j + 1) * 512], self.PS(bank, 0, 512, 0, bss[b]), AF.Silu,
                             writes=[self.B_zatt[b]], banks=self.bk(bank))
            self.S.tag = "attn"
            self.attention(Tn, slot, tile_idx)
            self.S.tag = "attproj"
            self.dump("OgT", self.OgT[:, :, 0:Tn], [self.B_OgT])
            for j in range(2):
                wt, wbuf = self.wget("ga%d" % j)
                for c2 in range(2):
                    bank = self.bank_get()
                    for u in range(2):
                        self.proj_fm(wt, wbuf, 2 * c2 + u, Tn, bank, u * T)
                    for u in range(2):
                        dtile = 4 * j + 2 * c2 + u
                        self.act(self.sig[:, dtile, 0:Tn], self.PS(bank, u * T, u * T + Tn), AF.Sigmoid,
                                 writes=[self.B_sig[dtile]], banks=self.bk(bank))
            for j in range(2):
                wt, wbuf = self.wget("wa%d" % j)
                for c2 in range(2):
                    bank = self.bank_get()
                    for u in range(2):
                        for kc in range(8):
                            self.mm(self.PS(bank, u * T, u * T + Tn), wt[:, kc, (2 * c2 + u) * 128:(2 * c2 + u + 1) * 128],
                                    self.OgT[:, kc, 0:Tn], kc == 0, kc == 7, reads=[wbuf, self.B_OgT], banks=self.bk(bank))
                    for u in range(2):
                        dtile = 4 * j + 2 * c2 + u
                        self.tt("dve", self.pre[:, dtile, 0:Tn], self.PS(bank, u * T, u * T + Tn), self.sig[:, dtile, 0:Tn], ALU.mult,
                                reads=[self.B_sig[dtile]], writes=[self.B_pre[dtile]], banks=self.bk(bank))
        self.S.tag = "xbc"
        wt, wbuf = self.wget("dt")
        for b in range(nbk):
            bank = self.bank_get()
            self.proj_tm(wt, wbuf, b, bss[b], 32, bank)
            self.tt("dve", self.dtraw[0:bss[b], b, :], self.PS(bank, 0, 32, 0, bss[b]), self.rows[0:bss[b], 0, :], ALU.add,
                    reads=[self.B_par], writes=[self.B_dtraw[b]], banks=self.bk(bank))
        for j in range(6 if (full or with_c) else 5):
            wt, wbuf = self.wget("xb%d" % j)
            for c2 in range(2):
                bank = self.bank_get()
                for u in range(2):
                    self.proj_fm(wt, wbuf, 2 * c2 + u, Tn, bank, u * T)
                for u in range(2):
                    self.conv_tile(4 * j + 2 * c2 + u, bank, u * T, Tn)
        if full:
            for j in range(4):
                wt, wbuf = self.wget("zs%d" % j)
                for b in range(nbk):
                    bank = self.bank_get()
                    self.proj_tm(wt, wbuf, b, bss[b], 512, bank)
                    self.act(self.zssd[0:bss[b], b, j * 512:(j + 1) * 512], self.PS(bank, 0, 512, 0, bss[b]), AF.Silu,
                             writes=[self.B_zssd[b]], banks=self.bk(bank))
        self.S.tag = "ssd"
        for b in range(nbk):
            self.ssd_block(b, bss[b], not full)
        self.S.tag = "tail"
        if not full:
            return
        self.dump("yT", self.yT[:, :, 0:Tn], self.B_yT)
        for j in range(2):
            wt, wbuf = self.wget("gs%d" % j)
            for c2 in range(2):
                bank = self.bank_get()
                for u in range(2):
                    self.proj_fm(wt, wbuf, 2 * c2 + u, Tn, bank, u * T)
                for u in range(2):
                    dtile = 4 * j + 2 * c2 + u
                    self.act(self.sig[:, dtile, 0:Tn], self.PS(bank, u * T, u * T + Tn), AF.Sigmoid,
                             writes=[self.B_sig[dtile]], banks=self.bk(bank))
        for j in range(4):
            wt, wbuf = self.wget("ws%d" % j)
            bank = self.bank_get()
            for u in range(2):
                for kc in range(16):
                    self.mm(self.PS(bank, u * T, u * T + Tn), wt[:, kc, u * 128:(u + 1) * 128], self.yT[:, kc, 0:Tn],
                            kc == 0, kc == 15, reads=[wbuf] + self.B_yT, banks=self.bk(bank))
            for u in range(2):
                dtile = 2 * j + u
                i = self.rotn("tmpm", 1)
                self.tt("dve", self.tmpm[i][:, 0:Tn], self.PS(bank, u * T, u * T + Tn), self.sig[:, dtile, 0:Tn], ALU.mult,
                        reads=[self.B_sig[dtile]], writes=[self.B_tmpm[i]], banks=self.bk(bank))
                self.tt("pool", self.pre[:, dtile, 0:Tn], self.pre[:, dtile, 0:Tn], self.tmpm[i][:, 0:Tn], ALU.add,
                        reads=[self.B_tmpm[i], self.B_pre[dtile]], writes=[self.B_pre[dtile]])
        self.dump("merged", self.pre[:, :, 0:Tn], self.B_pre)
        for j in range(2):
            wt, wbuf = self.wget("wo%d" % j)
            for b in range(nbk):
                bs = bss[b]
                bank = self.bank_get()
                for kc in range(8):
                    self.mm(self.PS(bank, 0, 512, 0, bs), self.pre[:, kc, 128 * b:128 * b + bs], wt[:, kc, 0:512], kc == 0, kc == 7,
                            reads=[wbuf, self.B_pre[kc]], banks=self.bk(bank))
                xb_ = 2 * par + b
                oc = self.xt[0:bs, xb_, j * 512:(j + 1) * 512]
                pso = self.PS(bank, 0, 512, 0, bs)
                self.tt("dve", pso, pso, self.gate_tok[0:bs, j * 512:(j + 1) * 512], ALU.mult,
                        reads=[self.B_gate], banks=self.bk(bank))
                self.tt("dve", oc, oc, pso, ALU.add, reads=[self.B_xt[xb_]], writes=[self.B_xt[xb_]], banks=self.bk(bank))
        for b in range(nbk):
            bs = bss[b]
            xb_ = 2 * par + b
            self.dma("sp", outs[0].ap()[out_rows + 128 * b: out_rows + 128 * b + bs, :], self.xt[0:bs, xb_, :], self.d_xo[xb_],
                     reads=[self.B_xt[xb_]], is_output=True)

    def emit_kv_out(self, slot, Tn, nk, nv, row0):
        nbk = (Tn + 127) // 128
        for b in range(nbk):
            bs = min(128, Tn - 128 * b)
            bank = self.bank_get()
            for hp in range(8):
                self.tp(self.PSB(bank, 1, hp * 128, hp * 128 + 128, 0, bs), self.kT[:, hp, slot * T + 128 * b: slot * T + 128 * b + bs],
                        self.identb[:, :], reads=[self.B_kT[slot][hp], self.B_cst], banks=self.bk(bank))
            i = self.rotn("kvo")
            self.cp("dve", self.ost[i][0:bs, :], self.PSB(bank, 1, 0, 1024, 0, bs), writes=[self.B_ost[i]], banks=self.bk(bank))
            self.dma("sp", nk.ap()[row0 + 128 * b: row0 + 128 * b + bs, :], self.ost[i][0:bs, :], self.d_ost[i],
                     reads=[self.B_ost[i]], is_output=True)
            i = self.rotn("kvo")
            self.cp("act", self.ost[i][0:bs, :].rearrange("p (h c) -> p h c", c=64),
                    self.Vr[0:bs, slot * 2 + b, :].rearrange("p (h c) -> p h c", c=65)[:, :, 0:64],
                    reads=[self.B_V[slot][b]], writes=[self.B_ost[i]])
            self.dma("sp", nv.ap()[row0 + 128 * b: row0 + 128 * b + bs, :], self.ost[i][0:bs, :], self.d_ost[i],
                     reads=[self.B_ost[i]], is_output=True)

    def emit_conv_out(self, out):
        cst = self.cst
        for q3 in range(3):
            bank = self.bank_get(2)
            for u in range(8):
                t = 8 * q3 + u
                self.tp(self.psum[0:3, bank * 512 + u * 128: bank * 512 + u * 128 + 128], self.halo[:, t, :], cst[:, C_ID:C_ID + 128],
                        reads=[self.B_halo[t], self.B_cst], banks=self.bk(bank + u // 4))
            i = self.rotn("kvo")
            self.cp("dve", self.ost[i][0:3, :], self.psum[0:3, bank * 512: bank * 512 + 1024], writes=[self.B_ost[i]], banks=self.bk(bank, 2))
            self.dma("sp", out.ap()[:, q3 * 1024:(q3 + 1) * 1024], self.ost[i][0:3, :], self.d_ost[i], reads=[self.B_ost[i]], is_output=True)

    def emit_ssm_out(self, out):
        cst = self.cst
        for q4 in range(4):
            bank = self.bank_get()
            for u in range(4):
                t = 4 * q4 + u
                self.tp(self.PS(bank, u * 128, u * 128 + 128), self.H[:, t * 128:(t + 1) * 128], cst[:, C_ID:C_ID + 128],
                        reads=[self.B_H[t // 4], self.B_cst], banks=self.bk(bank))
            self.cp("dve", self.y[:, q4 * 512:(q4 + 1) * 512], self.PS(bank, 0, 512), writes=[self.B_y[q4]], banks=self.bk(bank))
        self.dma("sp", out.ap().rearrange("(t p) n -> p t n", p=128), self.y[:].rearrange("p (t n) -> p t n", n=128), self.d_sso,
                 reads=self.B_y, is_output=True)

    def zero_state(self):
        self.memset("pool", self.H[:], 0.0, writes=self.B_H)
        self.memset("pool", self.Hbf[:], 0.0, writes=self.B_Hbf)
        self.memset("pool", self.halo[:], 0.0, writes=self.B_halo)

    def sample_tile(self):
        I, O, cst = self.I, self.O, self.cst
        self.dma("sp", self.y[:].rearrange("p (t n) -> p t n", n=128), I["state_ssm"].ap().rearrange("(t p) n -> p t n", p=128),
                 self.d_ss, writes=self.B_y)
        for q4 in range(4):
            bank = self.bank_get()
            for u in range(4):
                t = 4 * q4 + u
                self.tp(self.PS(bank, u * 128, u * 128 + 128), self.y[:, t * 128:(t + 1) * 128], cst[:, C_ID:C_ID + 128],
                        reads=[self.B_y[t // 4], self.B_cst], banks=self.bk(bank))
            self.cp("dve", self.H[:, q4 * 512:(q4 + 1) * 512], self.PS(bank, 0, 512), writes=[self.B_H[q4]], banks=self.bk(bank))
            self.cp("act", self.Hbf[:, q4 * 512:(q4 + 1) * 512], self.H[:, q4 * 512:(q4 + 1) * 512], reads=[self.B_H[q4]],
                    writes=[self.B_Hbf[q4]])
        for j_ in range(3):
            self.dma("sp", self.halo[:, :, j_], I["state_conv"].ap()[j_, :].rearrange("(t p) -> p t", p=128), self.d_sc,
                     writes=self.B_halo, slow=True)
        self.S.seal(self.d_sc, list(self.B_halo))
        self.load_norm(I["cache_k"].ap(), 0, 256, 1, plain_kslot={0: (0, 0), 1: (0, 1)})
        self.load_norm(I["cache_k"].ap(), 256, 256, 1, plain_kslot={0: (1, 0), 1: (1, 1)})
        for blk in range(4):
            i = blk % 2
            self.dma("sp", self.ost[i][:, :], I["cache_v"].ap()[128 * blk:128 * blk + 128, :], self.d_ost[i], writes=[self.B_ost[i]])
            self.cp("dve", self.Vr[:, blk, :].rearrange("p (h c) -> p h c", c=65)[:, :, 0:64],
                    self.ost[i][:, :].rearrange("p (h c) -> p h c", c=64), reads=[self.B_ost[i]], writes=[self.B_V[blk // 2][blk % 2]])
        self.make_gate_tok(1)
        self.tile("main", I["x_s"].ap(), 0, TS, 1, 2, tile_idx=None, outs=[O["y_s"]], out_rows=0)
        self.emit_kv_out(2, TS, O["nk_s"], O["nv_s"], 0)
        self.emit_conv_out(O["nconv_s"])
        self.emit_ssm_out(O["nssm_s"])

    def state_pass(self):
        I = self.I
        self.zero_state()
        n = self.n_state
        xb_ = list(self.B_crawx[2:]) + list(self.B_caccx[2:])
        self.memset("pool", self.y[:, 2047:2048], 0.0, writes=list(self.B_y) + xb_)
        self.state_mode = True
        for i in range(n):
            kv = i >= n - 2
            slot = 1 if i == n - 2 else (2 if i == n - 1 else 0)
            self.tile("state", I["x_prev"].ap(), i * T, T, 0, slot, kv=kv, with_c=(i == n - 1))
        self.state_mode = False
        self.memset("pool", self.y[:, 2047:2048], 0.0, writes=list(self.B_y) + xb_)
        f = self.flg[:, 0:1]
        for g in range(4):
            Hg = self.H[:, g * 512:(g + 1) * 512]
            self.ts("pool", Hg, Hg, f, None, ALU.mult, reads=[self.B_par, self.B_H[g]], writes=[self.B_H[g]])
            self.cp("act", self.Hbf[:, g * 512:(g + 1) * 512], Hg, reads=[self.B_H[g]], writes=[self.B_Hbf[g]])
        self.ts("pool", self.halo[:], self.halo[:], f, None, ALU.mult, reads=[self.B_par] + list(self.B_halo), writes=self.B_halo)

    def main_pass(self):
        I, O = self.I, self.O
        self.make_gate_tok(0)
        n = self.n_main
        for t in range(n):
            self.tile("main", I["x_main"].ap(), t * T, T, 0, t % 3, tile_idx=t, outs=[O["y_main"]], out_rows=t * T)
        if n >= 2:
            self.emit_kv_out((n - 2) % 3, T, O["nk"], O["nv"], 0)
        self.emit_kv_out((n - 1) % 3, T, O["nk"], O["nv"], 256)
        self.emit_conv_out(O["nconv"])
        self.emit_ssm_out(O["nssm"])


def make_consts():
    c = np.zeros((128, NCONST), np.float32)
    s = np.arange(128)[:, None]
    l = np.arange(128)[None, :]
    same = (s // 64) == (l // 64)
    c[:, C_ID:C_ID + 128] = np.eye(128)
    c[:, C_J:C_J + 128] = np.eye(128)[::-1]
    c[:, C_TRI:C_TRI + 128] = (same & (s <= l))
    c[:, C_STRI:C_STRI + 128] = (same & (s > l))
    for ch in range(2):
        c[:, C_CSEL + 128 * ch:C_CSEL + 128 * ch + 128] = ((s // 64) == ch)
    c[:, C_TRIL:C_TRIL + 64] = ((s % 64) <= np.arange(64)[None, :])
    c[:, C_BONES:C_BONES + 128] = same
    return c


_CACHE = {}


def get_program(key=("full",), **kw):
    if key not in _CACHE:
        b = Builder(**{k: v for k, v in kw.items() if k == "dbg"})
        nc = b.build(**{k: v for k, v in kw.items() if k != "dbg"})
        _CACHE[key] = (nc, b)
    return _CACHE[key]


def make_in_maps(inp):
    f = lambda a: np.ascontiguousarray(np.asarray(a, dtype=np.float32))
    xp = f(inp["x_prompt"])
    consts = make_consts()
    shared = {
        "consts": consts,
        "norm_g": f(inp["norm_g"][0]), "w_ada": f(inp["w_ada"][0]), "b_ada": f(inp["b_ada"][0]), "w_in": f(inp["w_in"][0]),
        "q_norm_g": f(inp["q_norm_g"][0]), "k_norm_g": f(inp["k_norm_g"][0]), "rel_bias": f(inp["rel_bias"][0]),
        "w_att": f(inp["w_att_proj"][0]), "conv_w": f(inp["conv_w"][0]), "conv_b": f(inp["conv_b"][0]),
        "dt_bias": f(inp["dt_bias"][0]), "a_log": f(inp["a_log"][0]), "d_skip": f(inp["d_skip"][0]),
        "ssd_norm_g": f(inp["ssd_norm_g"][0]), "w_ssd": f(inp["w_ssd_proj"][0]), "w_out": f(inp["w_out"][0]),
    }
    maps = []
    for c in range(8):
        b, half = c // 2, c % 2
        flags = np.zeros((128, 2), np.float32)
        flags[:, 0] = float(half)
        flags[:, 1] = 0.0 if half else -30000.0
        m = dict(shared)
        m.update({
            "x_main": f(xp[b, half * 4096:(half + 1) * 4096]),
            "x_prev": f(xp[b, 0:4096]),
            "x_s": f(inp["x_sample"][c]),
            "c2": f(np.stack([np.asarray(inp["c_prompt"])[b], np.asarray(inp["c_sample"])[c]])),
            "cache_k": f(np.asarray(inp["cache_k"])[0, c].reshape(512, 1024)),
            "cache_v": f(np.asarray(inp["cache_v"])[0, c].reshape(512, 1024)),
            "state_conv": f(np.asarray(inp["state_conv"])[0, c]),
            "state_ssm": f(np.asarray(inp["state_ssm"])[0, c].reshape(2048, 128)),
            "flags": flags,
        })
        maps.append(m)
    return maps


def assemble(res):
    R = res
    y_prompt = np.zeros((4, 8192, 1024), np.float32)
    y_sample = np.zeros((8, 16, 1024), np.float32)
    nkp = np.zeros((1, 4, 512, 16, 64), np.float32)
    nvp = np.zeros((1, 4, 512, 16, 64), np.float32)
    ncp = np.zeros((1, 4, 3, 3072), np.float32)
    nhp = np.zeros((1, 4, 32, 64, 128), np.float32)
    nks = np.zeros((1, 8, 16, 16, 64), np.float32)
    nvs = np.zeros((1, 8, 16, 16, 64), np.float32)
    ncs = np.zeros((1, 8, 3, 3072), np.float32)
    nhs = np.zeros((1, 8, 32, 64, 128), np.float32)
    for c in range(8):
        b, half = c // 2, c % 2
        r = R[c]
        y_prompt[b, half * 4096:(half + 1) * 4096] = r["y_main"]
        y_sample[c] = r["y_s"]
        if half == 1:
            nkp[0, b] = r["nk"].reshape(512, 16, 64)
            nvp[0, b] = r["nv"].reshape(512, 16, 64)
            ncp[0, b] = r["nconv"]
            nhp[0, b] = r["nssm"].reshape(32, 64, 128)
        nks[0, c] = r["nk_s"].reshape(16, 16, 64)
        nvs[0, c] = r["nv_s"].reshape(16, 16, 64)
        ncs[0, c] = r["nconv_s"]
        nhs[0, c] = r["nssm_s"].reshape(32, 64, 128)
    return (y_prompt, y_sample, nkp, nvp, ncp, nhp, nks, nvs, ncs, nhs)


def kernel(**inputs):
    nc, _ = get_program()
    maps = make_in_maps(inputs)
    res = run_bass_kernel_spmd(nc, maps, core_ids=list(range(8)))
    return assemble(res.results)
```

```python
import numpy as np
from contextlib import ExitStack
import concourse.bass as bass
import concourse.mybir as mybir
from concourse.bass_utils import run_bass_kernel_spmd

F32 = mybir.dt.float32
BF = mybir.dt.bfloat16
AF = mybir.ActivationFunctionType
ALU = mybir.AluOpType
AX = mybir.AxisListType

D = 1024
KC = 8
T = 256
NT = 16
TS = 16
IN_DIM = 11296
EPS = 1e-6
C_ID, C_J, C_TRI, C_STRI, C_CSEL, C_TRIL, C_BONES = 0, 128, 256, 384, 512, 768, 832
NCONST = 960


class Buf:
    __slots__ = ("name", "w", "r", "acc")

    def __init__(self, name):
        self.name = name
        self.w = []
        self.r = []
        self.acc = {}


class DSem:
    __slots__ = ("key", "ops")

    def __init__(self, key):
        self.key = key
        self.ops = []


class _Op:
    __slots__ = ("idx", "eng", "fn", "preds", "succs", "cost", "dsem", "nbytes", "is_output", "seq", "cum",
                 "npred", "ready", "fin", "tag", "start", "blame", "gap", "aset")


class Sched:
    ENGS = ("pe", "act", "dve", "pool", "sp")
    XLAT = 250.0
    SLAT = 100.0
    STARVE = 8000.0

    def __init__(self, same_engine_sync=True, reorder=True):
        self.ops = []
        self.dsems = {}
        self.same_engine_sync = same_engine_sync
        self.reorder = reorder
        self.tag = ""

    def buf(self, name):
        return Buf(name)

    def bufs(self, name, n):
        return [Buf("%s%d" % (name, i)) for i in range(n)]

    def dsem(self, name):
        d = DSem("D_" + name)
        self.dsems[d.key] = d
        return d

    def _add(self, eng, fn, reads, writes, banks, cost, dsem=None, nbytes=0, is_output=False, aset=None):
        o = _Op()
        o.aset = aset
        o.idx = len(self.ops)
        o.eng, o.fn, o.cost, o.dsem, o.nbytes, o.is_output = eng, fn, cost, dsem, nbytes, is_output
        o.succs = []
        o.tag = self.tag
        o.blame = None
        p = set()
        for b in banks:
            p.update(b.acc.values())
            b.acc[eng] = o.idx
        for b in reads:
            p.update(b.w)
        for b in writes:
            p.update(b.w)
            p.update(b.r)
        p.discard(o.idx)
        o.preds = p
        for b in reads:
            b.r.append(o.idx)
        for b in writes:
            b.w = [o.idx]
            b.r = []
        self.ops.append(o)
        if dsem is not None:
            dsem.ops.append(o.idx)
        return o

    def op(self, engname, fn, reads=(), writes=(), banks=(), cost=100.0, aset=None):
        self._add(engname, fn, reads, writes, banks, cost, aset=aset)

    def dma(self, qname, fn, dsem, reads=(), writes=(), is_output=False, nbytes=0):
        self._add(qname, fn, reads, writes, (), 60.0, dsem=dsem, nbytes=nbytes, is_output=is_output)

    def seal(self, dsem, bufs):
        for b in bufs:
            b.w = list(dsem.ops)
            b.r = []

    def _order(self):
        ops = self.ops
        n = len(ops)
        for o in ops:
            o.npred = len(o.preds)
            o.ready = 0.0
            for p in o.preds:
                ops[p].succs.append(o.idx)
        if not self.reorder:
            return {e: [o.idx for o in ops if o.eng == e] for e in self.ENGS}
        ready = {e: [] for e in self.ENGS}
        for o in ops:
            if o.npred == 0:
                ready[o.eng].append(o.idx)
        free = {e: 0.0 for e in self.ENGS}
        order = {e: [] for e in self.ENGS}
        dma_pipe = 0.0
        cur_set = None
        self.n_switch = 0
        done = 0
        WIN = 4000
        oldest = 0
        sched = [False] * n
        while done < n:
            best = None
            while oldest < n and sched[oldest]:
                oldest += 1
            for e in self.ENGS:
                r = ready[e]
                if not r:
                    continue
                f = free[e]
                cand = None
                soon = None
                for i in r:
                    if i > oldest + WIN:
                        continue
                    o = ops[i]
                    if o.ready <= f:
                        if cand is None or i < cand:
                            cand = i
                    elif soon is None or o.ready < ops[soon].ready or (o.ready == ops[soon].ready and i < soon):
                        soon = i
                if e == "act" and cand is not None and cur_set is not None:
                    oa = ops[cand].aset
                    if oa is not None and oa != cur_set and f - ops[cand].ready < self.STARVE:
                        alt = None
                        for i in r:
                            if i > oldest + WIN:
                                continue
                            o2 = ops[i]
                            if o2.ready <= f and (o2.aset is None or o2.aset == cur_set) and (alt is None or i < alt):
                                alt = i
                        if alt is not None:
                            cand = alt
                pick = cand if cand is not None else soon
                if pick is None:
                    continue
                st = max(f, ops[pick].ready)
                if best is None or st < best[0] or (st == best[0] and pick < best[2]):
                    best = (st, e, pick)
            if best is None:
                WIN *= 2
                continue
            st, e, i = best
            o = ops[i]
            ready[e].remove(i)
            sched[i] = True
            o.start = st
            o.gap = st - free[e]
            if o.dsem is not None:
                t0 = max(st + o.cost, dma_pipe)
                dma_pipe = t0 + o.nbytes / 280.0
                o.fin = dma_pipe + 1800.0
                free[e] = st + o.cost
            else:
                sw = 0.0
                if e == "act" and o.aset is not None and o.aset != cur_set:
                    sw = 1300.0
                    cur_set = o.aset
                    self.n_switch += 1
                o.fin = st + o.cost + sw
                free[e] = o.fin
            order[e].append(i)
            done += 1
            for s_ in o.succs:
                so = ops[s_]
                if so.eng == e and o.dsem is None:
                    lat = 0.0 if e == "pe" else self.SLAT
                else:
                    lat = self.XLAT
                if o.fin + lat > so.ready:
                    so.ready = o.fin + lat
                    so.blame = o.idx
                so.npred -= 1
                if so.npred == 0:
                    ready[so.eng].append(s_)
        self.sim_ns = max(o.fin for o in ops)
        return order

    def finish(self):
        self.final_order = self._order()

    def emit(self, nc, stack):
        ops = self.ops
        order = self.final_order
        ekey = {e: "E_" + e for e in self.ENGS}
        sems = {}
        for e in self.ENGS:
            sems[ekey[e]] = stack.enter_context(nc.semaphore("s_" + e))
        for k in self.dsems:
            sems[k] = stack.enter_context(nc.semaphore("s_" + k))
        cnt = {e: 0 for e in self.ENGS}
        dcnt = {k: 0 for k in self.dsems}
        for e in self.ENGS:
            for i in order[e]:
                o = ops[i]
                if o.dsem is not None:
                    dcnt[o.dsem.key] += 16
                    o.cum = dcnt[o.dsem.key]
                    o.seq = None
                else:
                    cnt[e] += 1
                    o.seq = cnt[e]
        progs = {}
        out_ev = {}
        for e in self.ENGS:
            waited = {}
            prog = []
            for i in order[e]:
                o = ops[i]
                need = {}
                for p in o.preds:
                    po = ops[p]
                    if po.dsem is not None:
                        k, v = po.dsem.key, po.cum
                    else:
                        if po.eng == e and (e == "pe" or e in NO_SELF_SYNC or not self.same_engine_sync):
                            continue
                        k, v = ekey[po.eng], po.seq
                    if need.get(k, 0) < v:
                        need[k] = v
                waits = []
                for k, v in need.items():
                    if waited.get(k, 0) >= v:
                        continue
                    waited[k] = v
                    waits.append((k, v))
                if o.dsem is not None:
                    prog.append((waits, o.fn, (o.dsem.key, 16)))
                    if o.is_output:
                        out_ev[o.dsem.key] = max(out_ev.get(o.dsem.key, 0), o.cum)
                else:
                    prog.append((waits, o.fn, (ekey[e], 1)))
            progs[e] = (prog, waited)
        prog, waited = progs["sp"]
        waits = [(k, v) for k, v in out_ev.items() if waited.get(k, 0) < v]
        for e in self.ENGS:
            if e != "sp" and cnt[e] > 0:
                waits.append((ekey[e], cnt[e]))
        prog.append((waits, None, None))

        def replay(prog):
            def run(eng):
                for waits, fn, inc in prog:
                    for s, v in waits:
                        eng.wait_ge(sems[s], v)
                    if fn is not None:
                        fn(eng).then_inc(sems[inc[0]], inc[1])
            return run

        with nc.Block() as block:
            block.tensor(replay(progs["pe"][0]))
            block.scalar(replay(progs["act"][0]))
            block.vector(replay(progs["dve"][0]))
            block.gpsimd(replay(progs["pool"][0]))
            block.sync(replay(progs["sp"][0]))


WB = {}
for _i in range(2):
    WB["q%d" % _i] = ("w_in", 8, 0 + 512 * _i, 512)
    WB["k%d" % _i] = ("w_in", 8, 1024 + 512 * _i, 512)
    WB["v%d" % _i] = ("w_in", 8, 2048 + 512 * _i, 512)
    WB["za%d" % _i] = ("w_in", 8, 3072 + 512 * _i, 512)
    WB["ga%d" % _i] = ("w_in", 8, 9248 + 512 * _i, 512)
    WB["gs%d" % _i] = ("w_in", 8, 10272 + 512 * _i, 512)
    WB["wa%d" % _i] = ("w_att", 8, 512 * _i, 512)
    WB["wo%d" % _i] = ("w_out", 8, 512 * _i, 512)
for _i in range(4):
    WB["zs%d" % _i] = ("w_in", 8, 4096 + 512 * _i, 512)
    WB["ws%d" % _i] = ("w_ssd", 16, 256 * _i, 256)
for _i in range(6):
    WB["xb%d" % _i] = ("w_in", 8, 6144 + 512 * _i, 512)
WB["dt"] = ("w_in", 8, 9216, 32)

for _i in range(12):
    WB["ad%d" % _i] = ("w_ada", 8, 256 * _i, 256)
ADA_BLOCKS = ["ad%d" % i for i in range(12)]
MAIN_BLOCKS = (["q0", "q1", "k0", "k1", "v0", "v1", "za0", "za1", "ga0", "ga1", "wa0", "wa1", "dt"]
               + ["xb%d" % i for i in range(6)] + ["zs%d" % i for i in range(4)]
               + ["gs0", "gs1"] + ["ws%d" % i for i in range(4)] + ["wo0", "wo1"])
STATE_BLOCKS = ["dt"] + ["xb%d" % i for i in range(5)]
STATE_KV_BLOCKS = ["k0", "k1", "v0", "v1"] + STATE_BLOCKS
STATE_LAST_BLOCKS = STATE_KV_BLOCKS + ["xb5"]
NW = 3
SAME_ENGINE_SYNC = True
NO_SELF_SYNC = ()


class Builder:
    def __init__(self, dbg=()):
        self.dbg = set(dbg)
        self.dbg_out = {}
        self.nc = bass.Bass("TRN2", target_bir_lowering=False)
        self.S = Sched(same_engine_sync=SAME_ENGINE_SYNC)
        self.st = ExitStack()

    def sb(self, name, shape, dt):
        return self.st.enter_context(self.nc.sbuf_tensor(name, shape, dt))

    def din(self, name, shape, dt=F32):
        return self.nc.dram_tensor(name, shape, dt, kind="ExternalInput")

    def dout(self, name, shape, dt=F32):
        return self.nc.dram_tensor(name, shape, dt, kind="ExternalOutput")

    def bank_get(self, k=1):
        for _ in range(32):
            s = self.bnext
            if s + k > 8:
                s = 0
            if all((s + i) not in self.bpinned for i in range(k)):
                self.bnext = (s + k) % 8
                return s
            self.bnext = (s + 1) % 8
        raise RuntimeError("no psum banks")

    def bank_pin(self, k):
        s = self.bank_get(k)
        self.bpinned |= set(range(s, s + k))
        return s

    def bank_unpin(self, s, k):
        self.bpinned -= set(range(s, s + k))

    def PS(self, bank, c0, c1, r0=0, r1=128):
        return self.psum[r0:r1, bank * 512 + c0: bank * 512 + c1]

    def PSB(self, bank, nb, c0, c1, r0=0, r1=128):
        return self.psum[:, bank * 512:(bank + nb) * 512].bitcast(BF)[r0:r1, c0:c1]

    def bk(self, bank, n=1):
        return [self.pb[bank + i] for i in range(n)]

    @staticmethod
    def _fd(ap):
        n = 1
        for d in ap.shape[1:]:
            n *= int(d)
        return n

    def _ecost(self, eng, out, ins=()):
        n = self._fd(out)
        if eng == "pool":
            return 200.0 + 1.7 * n
        if eng == "act":
            return 150.0 + 0.75 * n
        allbf = out.dtype == BF and all(getattr(a, "dtype", None) == BF for a in ins)
        return 70.0 + (0.6 if allbf else 1.3) * n

    def mm(self, out, lhsT, rhs, start, stop, reads, banks, writes=()):
        n_ = self._fd(out)
        if lhsT.dtype == F32:
            cost = 131.0 if n_ <= 64 else 4 * (55.0 + 0.27 * n_)
        elif n_ <= 256:
            cost = 55.0 + 0.27 * max(n_, 64)
        else:
            cost = 124.0 + (n_ - 256) * 0.78
        self.S.op("pe", lambda e: e.matmul(out, lhsT=lhsT, rhs=rhs, start=start, stop=stop), reads=reads, writes=writes, banks=banks,
                  cost=cost)

    def tp(self, out, in_, ident, reads, banks):
        cost = max(64, self._fd(in_)) * (2 if in_.dtype == F32 else 1) / 2.4 + 16
        self.S.op("pe", lambda e: e.transpose(out, in_, ident), reads=reads, banks=banks, cost=cost)

    def act(self, out, in_, func, reads=(), writes=(), banks=(), **kw):
        cost = self._ecost("act", out) + (100.0 if "accum_out" in kw else 0.0)
        aset = {AF.Silu: "silu", AF.Sigmoid: "sig", AF.Exp: "exp", AF.Ln: "exp"}.get(func)
        self.S.op("act", lambda e: e.activation(out=out, in_=in_, func=func, **kw), reads=reads, writes=writes, banks=banks, cost=cost,
                  aset=aset)

    def tt(self, eng, out, in0, in1, op, reads=(), writes=(), banks=()):
        self.S.op(eng, lambda e: e.tensor_tensor(out=out, in0=in0, in1=in1, op=op), reads=reads, writes=writes, banks=banks,
                  cost=self._ecost(eng, out, (in0, in1)))

    def ts(self, eng, out, in0, s1, s2, op0, op1=None, reads=(), writes=(), banks=()):
        cost = self._ecost(eng, out, (in0,))
        if op1 is None:
            self.S.op(eng, lambda e: e.tensor_scalar(out=out, in0=in0, scalar1=s1, scalar2=None, op0=op0),
                      reads=reads, writes=writes, banks=banks, cost=cost)
        else:
            self.S.op(eng, lambda e: e.tensor_scalar(out=out, in0=in0, scalar1=s1, scalar2=s2, op0=op0, op1=op1),
                      reads=reads, writes=writes, banks=banks, cost=cost)

    def stt(self, out, in0, scalar, in1, op0, op1, reads=(), writes=(), banks=()):
        self.S.op("dve", lambda e: e.scalar_tensor_tensor(out=out, in0=in0, scalar=scalar, in1=in1, op0=op0, op1=op1),
                  reads=reads, writes=writes, banks=banks, cost=70.0 + 1.7 * self._fd(out))

    def cp(self, eng, out, in_, reads=(), writes=(), banks=()):
        cost = self._ecost(eng, out, (in_,))
        if eng == "act":
            self.S.op("act", lambda e: e.copy(out=out, in_=in_), reads=reads, writes=writes, banks=banks, cost=cost)
        else:
            self.S.op(eng, lambda e: e.tensor_copy(out=out, in_=in_), reads=reads, writes=writes, banks=banks, cost=cost)

    def memset(self, eng, ap, val, writes):
        self.S.op(eng, lambda e: e.memset(ap, val), writes=writes, cost=100.0 + 0.5 * self._fd(ap))

    def dma(self, q, out, in_, dsem, reads=(), writes=(), is_output=False, slow=False):
        nbytes = (2 if (out.dtype == BF and in_.dtype == BF) else 4) * int(out.shape[0]) * self._fd(out)
        if slow:
            self.S.dma(q, lambda e: e.dma_start(out=out, in_=in_, allow_slow_non_contiguous=True), dsem,
                       reads=reads, writes=writes, is_output=is_output, nbytes=nbytes * 8)
        else:
            self.S.dma(q, lambda e: e.dma_start(out=out, in_=in_), dsem, reads=reads, writes=writes, is_output=is_output,
                       nbytes=nbytes)

    def dump(self, name, ap, reads):
        if name not in self.dbg:
            return
        t = self.dout("dbg_" + name, list(ap.shape))
        self.dbg_out["dbg_" + name] = list(ap.shape)
        self.dma("sp", t.ap(), ap, self.d_out, reads=reads, is_output=True)

    def wget(self, bid):
        i = self.wpos
        assert self.wseq[i] == bid, (i, self.wseq[i], bid)
        while self.wissued < min(len(self.wseq), i + NW):
            j = self.wissued
            b = self.wseq[j]
            wname, kc, c0, n = WB[b]
            slot = j % NW
            if wname == "w_ada":
                src = self.I[wname].ap()[:, c0:c0 + n].rearrange("(kc p) n -> p kc n", p=128)
                dst = self.wring[slot][:, :].bitcast(F32)[:, 0:kc * n].rearrange("p (kc n) -> p kc n", kc=kc)
                self.dma("sp", dst, src, self.d_w[slot], writes=[self.B_w[slot]])
            else:
                src = self.wscr[wname].ap()[:, c0:c0 + n].rearrange("(kc p) n -> p kc n", p=128)
                dst = self.wring[slot][:, 0:kc * n].rearrange("p (kc n) -> p kc n", kc=kc)
                self.dma("sp", dst, src, self.d_w[slot], reads=[self.scrbuf[b]], writes=[self.B_w[slot]])
            self.wissued += 1
        self.wpos += 1
        slot = i % NW
        wname, kc, c0, n = WB[bid]
        if wname == "w_ada":
            return self.wring[slot][:, :].bitcast(F32)[:, 0:kc * n].rearrange("p (kc n) -> p kc n", kc=kc), self.B_w[slot]
        return self.wring[slot][:, 0:kc * n].rearrange("p (kc n) -> p kc n", kc=kc), self.B_w[slot]

    def build(self, n_state=NT, n_main=NT, do_sample=True):
        nc, S = self.nc, self.S
        self.n_state, self.n_main, self.do_sample = n_state, n_main, do_sample
        I = {}
        for name, shape in [("x_main", [NT * T, D]), ("x_prev", [NT * T, D]), ("x_s", [TS, D]), ("c2", [2, D]),
                            ("cache_k", [512, D]), ("cache_v", [512, D]), ("state_conv", [3, 3072]),
                            ("state_ssm", [2048, 128]), ("flags", [128, 2]), ("consts", [128, NCONST]),
                            ("norm_g", [D]), ("w_ada", [D, 3 * D]), ("b_ada", [3 * D]), ("w_in", [D, IN_DIM]),
                            ("q_norm_g", [64]), ("k_norm_g", [64]), ("rel_bias", [16, 513]), ("w_att", [D, D]),
                            ("conv_w", [4, 3072]), ("conv_b", [3072]), ("dt_bias", [32]), ("a_log", [32]),
                            ("d_skip", [32]), ("ssd_norm_g", [2048]), ("w_ssd", [2048, D]), ("w_out", [D, D])]:
            I[name] = self.din(name, shape)
        self.I = I
        O = {}
        for name, shape in [("y_main", [NT * T, D]), ("y_s", [TS, D]), ("nk", [512, D]), ("nv", [512, D]),
                            ("nconv", [3, 3072]), ("nssm", [2048, 128]), ("nk_s", [TS, D]), ("nv_s", [TS, D]),
                            ("nconv_s", [3, 3072]), ("nssm_s", [2048, 128])]:
            O[name] = self.dout(name, shape)
        self.O = O
        self.wscr = {}
        for wname, shape in [("w_in", [D, IN_DIM]), ("w_att", [D, D]), ("w_ssd", [2048, D]), ("w_out", [D, D])]:
            self.wscr[wname] = nc.dram_tensor(wname + "_b", shape, BF, kind="Internal")
        self.ext = nc.dram_tensor("ext_bias", [16, 1024], F32, kind="Internal")

        self.psum = self.st.enter_context(nc.psum_tensor("psum", [128, 4096], F32))
        self.pb = S.bufs("bank", 8)
        self.bnext = 0
        self.bpinned = set()

        self.d_w = [S.dsem("w%d" % i) for i in range(NW)]
        self.d_cv = S.dsem("cv")
        self.d_par = S.dsem("par")
        self.d_x = [S.dsem("x%d" % i) for i in range(4)]
        self.d_out = S.dsem("out")
        self.d_misc = S.dsem("misc")
        self.d_bc = S.dsem("bc")
        self.d_sc = S.dsem("sc")
        self.d_ss = S.dsem("ss")
        self.d_ost = [S.dsem("ost0")] * 2
        self.d_xo = [S.dsem("xo%d" % i) for i in range(4)]
        self.d_cvo = S.dsem("cvo")
        self.d_sso = S.dsem("sso")
        self.d_cvs = [S.dsem("cv%d" % i) for i in range(4)]
        self.B_cvslot = S.bufs("cvslot", 4)

        sb = self.sb
        self.cst = sb("cst", [128, NCONST], F32)
        self.identb = sb("identb", [128, 128], BF)
        self.strib = sb("strib", [128, 128], BF)
        self.bonesb = sb("bonesb", [128, 128], BF)
        self.wring = [sb("wring%d" % i, [128, 4096], BF) for i in range(NW)]
        self.xt = sb("xt", [128, 4, D], F32)
        self.xn = [sb("xn%d" % i, [128, D], BF) for i in range(2)]
        self.hT2 = [sb("hT%d" % i, [128, 8, T], BF) for i in range(2)]
        self.hT = self.hT2[0]
        self.qT = sb("qT", [128, 8, T], BF)
        self.kT = sb("kT", [128, 8, 3 * T], BF)
        self.Vr = sb("Vr", [128, 6, 1040], BF)
        self.zatt = sb("zatt", [128, 2, D], BF)
        self.zssd = sb("zssd", [128, 2, 2048], BF)
        self.sq = [sb("sq%d" % i, [128, T], BF) for i in range(2)]
        self.rs = [sb("rs%d" % i, [128, T], F32) for i in range(2)]
        self.PT = [sb("PT%d" % i, [128, 640], BF) for i in range(2)]
        self.tab = sb("tab", [128, 16, 384], BF)
        self.bconst = sb("bconst", [128, 16], F32)
        self.negb = sb("negb", [128, 16], F32)
        self.rec = sb("rec", [128, 16], F32)
        self.Osb = sb("Osb", [128, D], BF)
        self.OgT = sb("OgT", [128, 8, T], BF)
        self.sig = sb("sig", [128, 8, T], BF)
        self.pre = sb("pre", [128, 8, T], BF)
        self.tmpm = [sb("tmpm%d" % i, [128, T], BF) for i in range(1)]
        self.craw = [sb("craw%d" % i, [128, T + 3], F32) for i in range(2)]
        self.cacc = [sb("cacc%d" % i, [128, T], F32) for i in range(2)]
        self.xsT = sb("xsT", [128, 16, T], BF)
        self.BT = sb("BT", [128, 4, T], BF)
        self.CT = sb("CT", [128, 4, T], BF)
        self.halo = sb("halo", [128, 24, 3], F32)
        self.dtraw = sb("dtraw", [128, 2, 32], F32)
        self.dtt = sb("dtt", [128, 32], F32)
        self.dt = sb("dt", [128, 32], F32)
        self.dta = sb("dta", [128, 32], F32)
        self.EW = sb("EW", [128, 64], F32)
        self.EA = sb("EA", [128, 64], F32)
        self.R = sb("R", [128, 1024], BF)
        self.dec = sb("dec", [128, 2048], BF)
        self.cbm = sb("cbm", [128, 256], BF)
        self.xdt = sb("xdt", [128, 2048], BF)
        self.xsD = sb("xsD", [128, 2048], BF)
        self.xdtw = self.xsD
        self.Btok = sb("Btok", [128, 512], BF)
        self.y = sb("y", [128, 2048], F32)
        self.ytmp = [sb("ytmp%d" % i, [128, 512], F32) for i in range(2)]
        self.ynorm = sb("ynorm", [128, 2048], BF)
        self.ss = sb("ss", [128, 8], F32)
        self.yT = sb("yT", [128, 16, T], BF)
        self.H = sb("H", [128, 2048], F32)
        self.Hbf = sb("Hbf", [128, 2048], BF)
        self.ost = [sb("ost0", [128, D], F32)] * 2
        self.gate_tok = sb("gate_tok", [128, D], F32)
        self.normgT = sb("normgT", [128, 8], F32)
        self.badaT = sb("badaT", [128, 24], F32)
        self.cT = sb("cT", [128, 2, 8], F32)
        self.siluT = sb("siluT", [128, 8, 2], F32)
        self.modT = sb("modT", [128, 24, 2], F32)
        self.gmodT = sb("gmodT", [128, 8, 2], F32)
        self.gqk = sb("gqk", [128, 2], F32)
        self.convwT = sb("convwT", [128, 4, 24], F32)
        self.convbT = sb("convbT", [128, 24], F32)
        self.gssdT = sb("gssdT", [128, 16], F32)
        self.rows = sb("rows", [128, 3, 32], F32)
        self.arow = sb("arow", [128, 32], F32)
        self.flg = sb("flg", [128, 2], F32)
        self.Atile = self.ost[0][:, 0:128]
        print("sbuf bytes remaining:", nc.sbuf_bytes_remaining)

        nb = S.buf
        self.B_cst = nb("cst")
        self.B_par = nb("par")
        self.B_w = S.bufs("w", NW)
        self.B_xt = S.bufs("xt", 4)
        self.B_xn = S.bufs("xn", 2)
        self.B_hT2 = [S.bufs("hT%d_" % i, 2) for i in range(2)]
        self.B_hT = self.B_hT2[0]
        self.B_qT = S.bufs("qT", 8)
        self.B_kT = [S.bufs("kT%d_" % s, 8) for s in range(3)]
        self.B_V = [S.bufs("V%d_" % s, 2) for s in range(3)]
        self.B_zatt = S.bufs("zatt", 2)
        self.B_zssd = S.bufs("zssd", 2)
        self.B_sq = S.bufs("sq", 2)
        self.B_rs = S.bufs("rs", 2)
        self.B_PT = S.bufs("PT", 2)
        self.B_tab = nb("tab")
        self.B_rowsep = S.bufs("rowsep", 4)
        self.B_tabw = nb("tabw")
        self.B_rec = nb("rec")
        self.B_Osb = nb("Osb")
        self.B_OgT = nb("OgT")
        self.B_sig = S.bufs("sig", 8)
        self.B_pre = S.bufs("pre", 8)
        self.B_tmpm = S.bufs("tmpm", 2)
        self.B_craw = S.bufs("craw", 2)
        self.B_cacc = S.bufs("cacc", 2)
        self.B_xsT = S.bufs("xsT", 16)
        self.B_BT = S.bufs("BT", 4)
        self.B_CT = S.bufs("CT", 4)
        self.B_halo = S.bufs("halo", 24)
        self.B_dtraw = S.bufs("dtraw", 2)
        self.B_dts = nb("dts")
        self.B_R = nb("R")
        self.B_dec = S.bufs("dec", 2)
        self.B_cbm = nb("cbm")
        self.B_xdt = nb("xdt")
        self.B_xsD = nb("xsD")
        self.B_xdtw = self.B_xsD
        self.B_Btok = nb("Btok")
        self.B_y = S.bufs("y", 4)
        self.B_ytmp = S.bufs("ytmp", 2)
        self.B_ynorm = nb("ynorm")
        self.B_ss = nb("ss")
        self.B_yT = S.bufs("yT", 2)
        self.B_H = S.bufs("H", 4)
        self.B_Hbf = S.bufs("Hbf", 4)
        self.B_ost = [S.buf("ost")] * 2
        self.B_gate = nb("gate")
        self.B_mod = nb("mod")
        self.B_A = self.B_ost[0]
        self.scrbuf = {b: nb("scr_" + b) for b in WB}
        self.B_ext = nb("ext")
        self.rot = {}
        self.state_mode = False
        self.crawx = list(self.craw) + [self.y[:, k * 260:k * 260 + T + 3] for k in range(3)]
        self.caccx = list(self.cacc) + [self.y[:, 1024 + k * 256:1024 + k * 256 + T] for k in range(3)]
        self.B_crawx = list(self.B_craw) + S.bufs("crawx", 3)
        self.B_caccx = list(self.B_cacc) + S.bufs("caccx", 3)

        seq = list(ADA_BLOCKS)
        if do_sample:
            seq += MAIN_BLOCKS
        for i in range(n_state):
            seq += STATE_LAST_BLOCKS if i == n_state - 1 else (STATE_KV_BLOCKS if i == n_state - 2 else STATE_BLOCKS)
        for i in range(n_main):
            seq += MAIN_BLOCKS
        self.wseq, self.wpos, self.wissued = seq, 0, 0

        self.setup()
        if do_sample:
            self.sample_tile()
        self.state_pass()
        self.main_pass()
        assert self.wpos == len(self.wseq), (self.wpos, len(self.wseq))
        S.finish()
        S.emit(nc, self.st)
        self.st.close()
        return nc

    def rotn(self, key, n=2):
        i = self.rot.get(key, 0)
        self.rot[key] = (i + 1) % n
        return i

    def setup(self):
        nc, S, I = self.nc, self.S, self.I
        cst = self.cst
        self.dma("sp", cst[:], I["consts"].ap(), self.d_par, writes=[self.B_cst])
        order = MAIN_BLOCKS
        for bi, b in enumerate(order):
            wname, kc, c0, n = WB[b]
            self.dma("pool", self.wscr[wname].ap()[:, c0:c0 + n], I[wname].ap()[:, c0:c0 + n], self.d_cvs[bi % 4],
                     writes=[self.scrbuf[b], self.B_cvslot[bi % 4]])
        self.dma("sp", self.y[0:16, 0:513], I["rel_bias"].ap(), self.d_misc, writes=[self.B_y[0], self.B_y[1]])
        self.cp("dve", self.y[0:16, 513:1024], self.y[0:16, 512:513].to_broadcast([16, 511]), reads=[self.B_y[0], self.B_y[1]],
                writes=[self.B_y[0], self.B_y[1]])
        self.dma("sp", self.ext.ap(), self.y[0:16, 0:1024], self.d_misc, reads=[self.B_y[0], self.B_y[1]], writes=[self.B_ext])
        P = self.d_par
        dm = lambda out, in_: self.dma("sp", out, in_, P, writes=[self.B_par], slow=True)
        dm(self.normgT[:], I["norm_g"].ap().rearrange("(kc p) -> p kc", p=128))
        dm(self.badaT[:], I["b_ada"].ap().rearrange("(t p) -> p t", p=128))
        for s_ in range(2):
            dm(self.cT[:, s_, :], I["c2"].ap()[s_, :].rearrange("(kc p) -> p kc", p=128))
        for half in range(2):
            dm(self.gqk[half * 64:(half + 1) * 64, 0:1], I["q_norm_g"].ap().rearrange("(p o) -> p o", o=1))
            dm(self.gqk[half * 64:(half + 1) * 64, 1:2], I["k_norm_g"].ap().rearrange("(p o) -> p o", o=1))
        for j_ in range(4):
            dm(self.convwT[:, j_, :], I["conv_w"].ap()[j_, :].rearrange("(t p) -> p t", p=128))
        dm(self.convbT[:], I["conv_b"].ap().rearrange("(t p) -> p t", p=128))
        dm(self.gssdT[:], I["ssd_norm_g"].ap().rearrange("(t p) -> p t", p=128))
        dm(self.rows[:, 0, :], I["dt_bias"].ap().partition_broadcast(128))
        dm(self.rows[:, 1, :], I["a_log"].ap().partition_broadcast(128))
        dm(self.rows[:, 2, :], I["d_skip"].ap().partition_broadcast(128))
        dm(self.flg[:], I["flags"].ap())
        S.seal(P, [self.B_cst, self.B_par])
        Bc, Bp = self.B_cst, self.B_par
        self.cp("dve", self.identb[:], cst[:, C_ID:C_ID + 128], reads=[Bc], writes=[Bc])
        self.cp("dve", self.strib[:], cst[:, C_STRI:C_STRI + 128], reads=[Bc], writes=[Bc])
        self.cp("dve", self.bonesb[:], cst[:, C_BONES:C_BONES + 128], reads=[Bc], writes=[Bc])
        self.cp("dve", cst[:, C_BONES:C_BONES + 128], cst[:, C_STRI:C_STRI + 128], reads=[Bc], writes=[Bc])
        self.tt("dve", cst[:, C_BONES:C_BONES + 64], cst[:, C_BONES:C_BONES + 64], cst[:, C_CSEL + 128:C_CSEL + 192], ALU.add,
                reads=[Bc], writes=[Bc])
        self.act(self.arow[:], self.rows[:, 1, :], AF.Exp, reads=[Bp], writes=[Bp])
        self.ts("dve", self.arow[:], self.arow[:], -1.0, None, ALU.mult, reads=[Bp], writes=[Bp])
        self.memset("pool", self.Vr[:].rearrange("p b (h c) -> p b h c", c=65)[:, :, :, 64:65], 1.0,
                    writes=[b for s in self.B_V for b in s])
        self.act(self.siluT[:], self.cT[:].rearrange("p s k -> p k s"), AF.Silu, reads=[Bp], writes=[Bp])
        bank = self.bank_get()
        for bi, bid in enumerate(ADA_BLOCKS):
            wt, wbuf = self.wget(bid)
            for ct in range(2):
                t = bi * 2 + ct
                for kc in range(8):
                    self.mm(self.PS(bank, 2 * t, 2 * t + 2), wt[:, kc, ct * 128:(ct + 1) * 128], self.siluT[:, kc, :],
                            kc == 0, kc == 7, reads=[wbuf, Bp], banks=self.bk(bank))
        self.tt("dve", self.modT[:], self.PS(bank, 0, 48).rearrange("p (t s) -> p t s", s=2),
                self.badaT[:].unsqueeze(2).to_broadcast([128, 24, 2]), ALU.add,
                reads=[Bp], writes=[self.B_mod], banks=self.bk(bank))
        self.ts("dve", self.gmodT[:], self.modT[:, 8:16, :], 1.0, None, ALU.add, reads=[self.B_mod], writes=[self.B_mod])
        self.tt("dve", self.gmodT[:], self.gmodT[:], self.normgT[:].unsqueeze(2).to_broadcast([128, 8, 2]), ALU.mult,
                reads=[Bp, self.B_mod], writes=[self.B_mod])
        self.dump("modT", self.modT[:], [self.B_mod])
        srcc = bass.AP(self.ext, 600, [[0, 128], [1024, 16], [1, 1]])
        self.dma("sp", self.bconst[:, :].unsqueeze(2), srcc, self.d_bc, reads=[self.B_ext], writes=[self.B_tab], slow=True)
        self.ts("dve", self.negb[:], self.bconst[:], -1.0, None, ALU.mult, reads=[self.B_tab], writes=[self.B_tab])
        hk = self.y
        for h in range(16):
            src = bass.AP(self.ext, h * 1024 + 129, [[1, 128], [1, 384]])
            self.dma("sp", hk[:, 0:384], src, self.d_misc, reads=[self.B_ext], writes=[self.B_y[0]], slow=True)
            bank = self.bank_get()
            self.mm(self.PS(bank, 0, 384), cst[:, C_J:C_J + 128], hk[:, 0:384], True, True,
                    reads=[Bc, self.B_y[0]], banks=self.bk(bank))
            self.act(self.tab[:, h, :], self.PS(bank, 0, 384), AF.Exp, reads=[self.B_tab], writes=[self.B_tabw], banks=self.bk(bank),
                     bias=self.negb[:, h:h + 1])
        self.memset("pool", self.tab[64:128, :, 0:64], 0.0, writes=[self.B_tabw])
        self.S.op("pool", lambda e: e.memset(self.negb[:, 0:1], 0.0), reads=[self.B_tabw], writes=[self.B_tab], cost=100.0)
        self.dump("tab", self.tab[:, 0, :], [self.B_tab])

    def make_gate_tok(self, seq):
        cst = self.cst
        bank = self.bank_get(2)
        for kc in range(8):
            self.act(self.Atile, cst[:, 0:128], AF.Identity, reads=[self.B_mod, self.B_cst], writes=[self.B_A],
                     scale=0.0, bias=self.modT[:, 16 + kc, seq:seq + 1])
            self.mm(self.psum[:, bank * 512 + kc * 128: bank * 512 + (kc + 1) * 128], self.Atile, cst[:, C_ID:C_ID + 128],
                    True, True, reads=[self.B_A, self.B_cst], banks=self.bk(bank + kc // 4))
        self.cp("dve", self.gate_tok[:], self.psum[:, bank * 512: bank * 512 + 1024], writes=[self.B_gate],
                banks=self.bk(bank, 2))

    def load_norm(self, src_ap, row0, Tn, seq, plain_kslot=None, par=0):
        cst = self.cst
        nbk = (Tn + 127) // 128
        for b in range(nbk):
            bs = min(128, Tn - 128 * b)
            xb_ = 2 * par + b
            self.dma("sp", self.xt[0:bs, xb_, :], src_ap[row0 + 128 * b: row0 + 128 * b + bs, :], self.d_x[xb_],
                     writes=[self.B_xt[xb_]])
            i = self.rotn("xn")
            xn, Bxn = self.xn[i], self.B_xn[i]
            if plain_kslot is None:
                self.act(xn[0:bs, :], self.xt[0:bs, xb_, :], AF.Square, reads=[self.B_xt[xb_]], writes=[Bxn, self.B_ss],
                         accum_out=self.ss[0:bs, 4:5])
                self.act(self.ss[0:bs, 5:6], self.ss[0:bs, 4:5], AF.Ln, reads=[self.B_ss], writes=[self.B_ss],
                         scale=1.0 / D, bias=EPS)
                self.act(self.ss[0:bs, 5:6], self.ss[0:bs, 5:6], AF.Exp, reads=[self.B_ss], writes=[self.B_ss], scale=-0.5)
                self.act(xn[0:bs, :], self.xt[0:bs, xb_, :], AF.Copy, reads=[self.B_xt[xb_], self.B_ss], writes=[Bxn],
                         scale=self.ss[0:bs, 5:6])
            else:
                self.cp("act", xn[0:bs, :], self.xt[0:bs, xb_, :], reads=[self.B_xt[xb_]], writes=[Bxn])
            bank = self.bank_get()
            for kc in range(8):
                self.tp(self.PSB(bank, 1, kc * 128, kc * 128 + bs), xn[0:bs, kc * 128:(kc + 1) * 128],
                        self.identb[0:bs, 0:bs], reads=[Bxn, self.B_cst], banks=self.bk(bank))
            if plain_kslot is None:
                for kc in range(8):
                    self.ts("dve", self.hT[:, kc, 128 * b:128 * b + bs], self.PSB(bank, 1, kc * 128, kc * 128 + bs),
                            self.gmodT[:, kc, seq:seq + 1], self.modT[:, kc, seq:seq + 1], ALU.mult, ALU.add,
                            reads=[self.B_mod], writes=[self.B_hT[b]], banks=self.bk(bank))
            else:
                slot, half = plain_kslot[b]
                self.cp("dve", self.kT[:, :, slot * T + half * 128: slot * T + half * 128 + bs],
                        self.PSB(bank, 1, 0, 1024).rearrange("p (k t) -> p k t", t=128)[:, :, 0:bs],
                        writes=self.B_kT[slot], banks=self.bk(bank))

    def proj_fm(self, wt, wbuf, ct, Tn, bank, c0):
        for kc in range(8):
            self.mm(self.PS(bank, c0, c0 + Tn), wt[:, kc, ct * 128:(ct + 1) * 128], self.hT[:, kc, 0:Tn],
                    kc == 0, kc == 7, reads=[wbuf] + self.B_hT, banks=self.bk(bank))

    def proj_tm(self, wt, wbuf, b, bs, ncols, bank):
        for kc in range(8):
            self.mm(self.PS(bank, 0, ncols, 0, bs), self.hT[:, kc, 128 * b:128 * b + bs], wt[:, kc, 0:ncols],
                    kc == 0, kc == 7, reads=[wbuf, self.B_hT[b]], banks=self.bk(bank))

    def qk_tile(self, which, wt, wbuf, ctl, hp, Tn, slot):
        bank = self.bank_get()
        self.proj_fm(wt, wbuf, ctl, Tn, bank, 0)
        ps = self.PS(bank, 0, Tn)
        i = self.rotn("sq")
        sq, rs = self.sq[i], self.rs[i]
        self.act(sq[:, 0:Tn], ps, AF.Square, writes=[self.B_sq[i]], banks=self.bk(bank))
        bank2 = self.bank_get()
        self.mm(self.PS(bank2, 0, Tn), self.bonesb[:], sq[:, 0:Tn], True, True, reads=[self.B_sq[i], self.B_cst],
                banks=self.bk(bank2))
        self.act(rs[:, 0:Tn], self.PS(bank2, 0, Tn), AF.Ln, writes=[self.B_rs[i]], banks=self.bk(bank2),
                 scale=1.0 / 64, bias=EPS)
        self.act(rs[:, 0:Tn], rs[:, 0:Tn], AF.Exp, reads=[self.B_rs[i]], writes=[self.B_rs[i]], scale=-0.5)
        if which == "q":
            dst, Bd, g = self.qT[:, hp, 0:Tn], self.B_qT[hp], self.gqk[:, 0:1]
        else:
            dst, Bd, g = self.kT[:, hp, slot * T: slot * T + Tn], self.B_kT[slot][hp], self.gqk[:, 1:2]
        self.stt(dst, ps, g, rs[:, 0:Tn], ALU.mult, ALU.mult, reads=[self.B_rs[i], self.B_par], writes=[Bd],
                 banks=self.bk(bank))

    def keyblock(self, g, slot, bss):
        if g >= 0:
            return slot * T + g * 128, slot * 2 + g, (slot, g), bss[g]
        j = g + 4
        hs = (slot - 2) % 3 if j < 2 else (slot - 1) % 3
        half = j % 2
        return hs * T + half * 128, hs * 2 + half, (hs, half), 128

    def attention(self, Tn, slot, tile_idx):
        nbk = (Tn + 127) // 128
        bss = [min(128, Tn - 128 * b) for b in range(nbk)]
        for p in range(nbk):
            qn = bss[p]
            ob = self.bank_pin(3)
            for h in range(16):
                hp, base = h // 2, 64 * (h % 2)
                sbk = self.bank_get(2)
                pi = self.rotn("PT")
                PT, BPT = self.PT[pi], self.B_PT[pi]
                kinfo = []
                for r in range(5):
                    g = p - r
                    kcol, vblk, vsh, kn = self.keyblock(g, slot, bss)
                    kslot = kcol // T
                    kinfo.append((kcol, vblk, vsh, kn, kslot))
                    self.mm(self.psum[0:kn, sbk * 512 + r * 128: sbk * 512 + r * 128 + qn],
                            self.kT[base:base + 64, hp, kcol:kcol + kn], self.qT[base:base + 64, hp, p * 128:p * 128 + qn],
                            True, True, reads=[self.B_kT[kslot][hp], self.B_qT[hp]], banks=self.bk(sbk + r // 4))
                runs = []
                for r in range(5):
                    kn = kinfo[r][3]
                    masked = (tile_idx is not None) and (tile_idx * 2 + p - r < 0)
                    key = (kn, masked)
                    if runs and runs[-1][0] == key:
                        runs[-1][2] = r + 1
                    else:
                        runs.append([key, r, r + 1])
                for (kn, masked), r0, r1 in runs:
                    src = self.psum[0:kn, sbk * 512 + r0 * 128: sbk * 512 + r1 * 128].rearrange("p (r i) -> p r i", i=128)[:, :, 0:qn]
                    dst = PT[0:kn, r0 * 128:r1 * 128].rearrange("p (r i) -> p r i", i=128)[:, :, 0:qn]
                    if masked:
                        self.act(dst, src, AF.Exp, reads=[self.B_par], writes=[BPT], banks=self.bk(sbk, 2), scale=0.125,
                                 bias=self.flg[0:kn, 1:2])
                    else:
                        self.act(dst, src, AF.Exp, reads=[self.B_tab], writes=[BPT], banks=self.bk(sbk, 2), scale=0.125,
                                 bias=self.bconst[0:kn, h:h + 1])
                    r1t = min(r1, 3)
                    if r0 < r1t:
                        dstt = PT[0:kn, r0 * 128:r1t * 128].rearrange("p (r i) -> p r i", i=128)[:, :, 0:qn]
                        tb = self.tab[0:kn, h, r0 * 128:r1t * 128].rearrange("p (r i) -> p r i", i=128)[:, :, 0:qn]
                        self.tt("dve", dstt, dstt, tb, ALU.mult, reads=[self.B_tab, BPT], writes=[BPT])
                if qn > 64 and kinfo[4][3] == 128:
                    self.memset("pool", PT[0:64, 4 * 128 + 64: 4 * 128 + qn], 0.0, writes=[BPT])
                obank = ob + h // 7
                oc = (h % 7) * 65
                for r in range(5):
                    kcol, vblk, vsh, kn, kslot = kinfo[r]
                    self.mm(self.psum[0:qn, obank * 512 + oc: obank * 512 + oc + 65], PT[0:kn, r * 128:r * 128 + qn],
                            self.Vr[0:kn, vblk, h * 65:(h + 1) * 65], r == 0, r == 4,
                            reads=[BPT, self.B_V[vsh[0]][vsh[1]]], banks=self.bk(obank))
            for bi, (h0, nh) in enumerate([(0, 7), (7, 7), (14, 2)]):
                o3 = self.psum[0:qn, (ob + bi) * 512: (ob + bi) * 512 + nh * 65].rearrange("p (h c) -> p h c", c=65)
                rec_o, rec_i = self.rec[0:qn, h0:h0 + nh].unsqueeze(2), o3[:, :, 64:65]
                self.S.op("dve", lambda e, rec_o=rec_o, rec_i=rec_i: e.reciprocal(out=rec_o, in_=rec_i),
                          writes=[self.B_rec], banks=self.bk(ob + bi), cost=200.0)
                self.tt("dve", self.Osb[0:qn, h0 * 64:(h0 + nh) * 64].rearrange("p (h c) -> p h c", c=64), o3[:, :, 0:64],
                        self.rec[0:qn, h0:h0 + nh].unsqueeze(2).to_broadcast([qn, nh, 64]), ALU.mult,
                        reads=[self.B_rec], writes=[self.B_Osb], banks=self.bk(ob + bi))
            self.bank_unpin(ob, 3)
            self.tt("dve", self.Osb[0:qn, :], self.Osb[0:qn, :], self.zatt[0:qn, p, :], ALU.mult,
                    reads=[self.B_Osb, self.B_zatt[p]], writes=[self.B_Osb])
            bank = self.bank_get()
            for kc in range(8):
                self.tp(self.PSB(bank, 1, kc * 128, kc * 128 + qn), self.Osb[0:qn, kc * 128:(kc + 1) * 128],
                        self.identb[0:qn, 0:qn], reads=[self.B_Osb, self.B_cst], banks=self.bk(bank))
            self.cp("act", self.OgT[:, :, p * 128:p * 128 + qn],
                    self.PSB(bank, 1, 0, 1024).rearrange("p (k t) -> p k t", t=128)[:, :, 0:qn],
                    writes=[self.B_OgT], banks=self.bk(bank))

    def conv_tile(self, ct, bank, c0, Tn):
        if self.state_mode:
            i = self.rotn("convx", 5)
            raw, acc = self.crawx[i], self.caccx[i]
            Br, Ba = self.B_crawx[i], self.B_caccx[i]
        else:
            i = self.rotn("conv")
            raw, acc = self.craw[i], self.cacc[i]
            Br, Ba = self.B_craw[i], self.B_cacc[i]
        self.cp("pool", raw[:, 0:3], self.halo[:, ct, :], reads=[self.B_halo[ct]], writes=[Br])
        self.cp("act", raw[:, 3:3 + Tn], self.PS(bank, c0, c0 + Tn), writes=[Br], banks=self.bk(bank))
        self.cp("pool", self.halo[:, ct, :], raw[:, Tn:Tn + 3], reads=[Br], writes=[self.B_halo[ct]])
        self.act(acc[:, 0:Tn], raw[:, 0:Tn], AF.Identity, reads=[Br, self.B_par], writes=[Ba],
                 scale=self.convwT[:, 0, ct:ct + 1], bias=self.convbT[:, ct:ct + 1])
        for j in range(1, 4):
            self.stt(acc[:, 0:Tn], raw[:, j:j + Tn], self.convwT[:, j, ct:ct + 1], acc[:, 0:Tn], ALU.mult, ALU.add,
                     reads=[Br, Ba, self.B_par], writes=[Ba])
        if ct < 16:
            dst, Bd = self.xsT[:, ct, 0:Tn], self.B_xsT[ct]
        elif ct < 20:
            dst, Bd = self.BT[:, ct - 16, 0:Tn], self.B_BT[ct - 16]
        else:
            dst, Bd = self.CT[:, ct - 20, 0:Tn], self.B_CT[ct - 20]
        self.act(dst, acc[:, 0:Tn], AF.Silu, reads=[Ba], writes=[Bd])

    def ssd_block(self, b, bs, state_only):
        cst = self.cst
        nch = (bs + 63) // 64
        Lc = min(64, bs)
        t0 = 128 * b
        Bd = self.B_dts
        x = self.dtraw[0:bs, b, :]
        self.act(self.dtt[0:bs, :], x, AF.Abs, reads=[self.B_dtraw[b]], writes=[Bd])
        self.act(self.dtt[0:bs, :], self.dtt[0:bs, :], AF.Exp, reads=[Bd], writes=[Bd], scale=-1.0)
        self.act(self.dtt[0:bs, :], self.dtt[0:bs, :], AF.Ln, reads=[Bd], writes=[Bd], bias=1.0)
        self.stt(self.dt[0:bs, :], x, 0.0, self.dtt[0:bs, :], ALU.max, ALU.add, reads=[self.B_dtraw[b], Bd], writes=[Bd])
        self.tt("dve", self.dta[0:bs, :], self.dt[0:bs, :], self.arow[0:bs, :], ALU.mult, reads=[Bd, self.B_par], writes=[Bd])
        bk = self.bank_get()
        self.mm(self.PS(bk, 0, 32, 0, bs), cst[0:bs, C_TRI:C_TRI + bs], self.dta[0:bs, :], True, True, reads=[Bd, self.B_cst], banks=self.bk(bk))
        mrg = state_only and bs == 128
        if mrg:
            self.mm(self.PS(bk, 32, 64, 0, bs), cst[0:bs, C_BONES:C_BONES + bs], self.dta[0:bs, :], True, True, reads=[Bd, self.B_cst], banks=self.bk(bk))
            for c in range(2):
                self.mm(self.PS(bk, 64, 96), cst[0:bs, C_CSEL + 128 * c:C_CSEL + 128 * c + 128], self.dta[0:bs, :],
                        c == 0, c == 1, reads=[Bd, self.B_cst], banks=self.bk(bk))
        else:
            self.mm(self.PS(bk, 32, 64, 0, bs), cst[0:bs, C_STRI:C_STRI + bs], self.dta[0:bs, :], True, True, reads=[Bd, self.B_cst], banks=self.bk(bk))
            for c in range(nch):
                self.mm(self.PS(bk, 64 + 32 * c, 96 + 32 * c), cst[0:bs, C_CSEL + 128 * c:C_CSEL + 128 * c + 128], self.dta[0:bs, :],
                        True, True, reads=[Bd, self.B_cst], banks=self.bk(bk))
        self.act(self.EW[0:bs, :], self.PS(bk, 0, 64, 0, bs), AF.Exp, writes=[Bd], banks=self.bk(bk))
        nea = 1 if mrg else nch
        self.act(self.EA[:, 0:32 * nea], self.PS(bk, 64, 64 + 32 * nea), AF.Exp, writes=[Bd], banks=self.bk(bk))
        E3 = self.EW[0:bs, 0:32].unsqueeze(2)
        W3 = self.EW[0:bs, 32:64].unsqueeze(2)
        dt3 = self.dt[0:bs, :].unsqueeze(2)
        xb = self.bank_get(2)
        for ct in range(16):
            self.tp(self.PSB(xb, 2, ct * 128, ct * 128 + 128, 0, bs), self.xsT[:, ct, t0:t0 + bs], self.identb[:, :],
                    reads=[self.B_xsT[ct], self.B_cst], banks=self.bk(xb + ct // 8))
        xs3 = self.PSB(xb, 2, 0, 2048, 0, bs).rearrange("p (h c) -> p h c", c=64)
        self.tt("dve", self.xdt[0:bs, :].rearrange("p (h c) -> p h c", c=64), xs3, dt3.to_broadcast([bs, 32, 64]), ALU.mult,
                reads=[Bd], writes=[self.B_xdt], banks=self.bk(xb, 2))
        if not state_only:
            self.tt("dve", self.xsD[0:bs, :].rearrange("p (h c) -> p h c", c=64), xs3,
                    self.rows[0:bs, 2, :].unsqueeze(2).to_broadcast([bs, 32, 64]), ALU.mult,
                    reads=[self.B_par], writes=[self.B_xsD], banks=self.bk(xb, 2))
        bb = self.bank_get()
        for g in range(4):
            self.tp(self.PSB(bb, 1, g * 128, g * 128 + 128, 0, bs), self.BT[:, g, t0:t0 + bs], self.identb[:, :],
                    reads=[self.B_BT[g], self.B_cst], banks=self.bk(bb))
        self.cp("act", self.Btok[0:bs, :], self.PSB(bb, 1, 0, 512, 0, bs), writes=[self.B_Btok], banks=self.bk(bb))

        if not state_only:
            cb = self.bank_get()
            for g in range(4):
                for c in range(nch):
                    cs = t0 + 64 * c
                    self.mm(self.PS(cb, g * 64, g * 64 + Lc, 64 * c, 64 * c + Lc), self.BT[:, g, cs:cs + Lc], self.CT[:, g, cs:cs + Lc],
                            True, True, reads=[self.B_BT[g], self.B_CT[g]], banks=self.bk(cb))
            self.tt("dve", self.cbm[0:bs, :].rearrange("p (g l) -> p g l", l=64)[:, :, 0:Lc],
                    self.PS(cb, 0, 256, 0, bs).rearrange("p (g l) -> p g l", l=64)[:, :, 0:Lc],
                    cst[0:bs, C_TRIL:C_TRIL + Lc].unsqueeze(1).to_broadcast([bs, 4, Lc]), ALU.mult,
                    reads=[self.B_cst], writes=[self.B_cbm], banks=self.bk(cb))
            for hh in range(2):
                R3 = self.R[0:bs, 0:16 * Lc].rearrange("p (h l) -> p h l", l=Lc)
                self.tt("dve", R3, cst[0:bs, C_TRIL:C_TRIL + Lc].unsqueeze(1).to_broadcast([bs, 16, Lc]),
                        self.dta[0:bs, 16 * hh:16 * hh + 16].unsqueeze(2).to_broadcast([bs, 16, Lc]), ALU.mult,
                        reads=[Bd, self.B_cst], writes=[self.B_R])
                sg = self.bank_get(2)
                ncol = 16 * Lc
                for j in range((ncol + 511) // 512):
                    cw = min(512, ncol - 512 * j)
                    self.mm(self.PS(sg + j, 0, cw, 0, bs), self.strib[0:bs, 0:bs], self.R[0:bs, 512 * j:512 * j + cw], True, True,
                            reads=[self.B_R, self.B_cst], banks=self.bk(sg + j))
                d3 = self.dec[0:bs, hh * 1024: hh * 1024 + ncol]
                self.act(d3, self.psum[0:bs, sg * 512: sg * 512 + ncol], AF.Exp, writes=[self.B_dec[hh]], banks=self.bk(sg, 2))
                d4 = d3.rearrange("p (g e l) -> p g e l", g=2, l=Lc)
                c4 = self.cbm[0:bs, :].rearrange("p (g l) -> p g l", l=64)[:, 2 * hh:2 * hh + 2, 0:Lc].unsqueeze(2).to_broadcast([bs, 2, 8, Lc])
                self.tt("dve", d4, d4, c4, ALU.mult, reads=[self.B_cbm, self.B_dec[hh]], writes=[self.B_dec[hh]])

        ysb = None
        if not state_only:
            ysb = self.bank_pin(4)
            for g in range(4):
                yi = self.bank_get()
                hh = g // 2
                for c in range(nch):
                    r0 = 64 * c
                    for e in range(8):
                        h = 8 * g + e
                        hl = h - 16 * hh
                        self.mm(self.PS(yi, e * 64, e * 64 + 64, r0, r0 + Lc),
                                self.dec[r0:r0 + Lc, hh * 1024 + hl * Lc: hh * 1024 + hl * Lc + Lc],
                                self.xdt[r0:r0 + Lc, h * 64:(h + 1) * 64], True, True,
                                reads=[self.B_dec[hh], self.B_xdt, self.B_rowsep[g]], banks=self.bk(yi))
                    if c == 0:
                        self.mm(self.PS(ysb + g, 0, 512, 0, Lc), self.CT[:, g, t0:t0 + Lc], self.Hbf[:, g * 512:(g + 1) * 512],
                                True, True, reads=[self.B_CT[g], self.B_Hbf[g]], banks=self.bk(ysb + g), writes=[self.B_rowsep[g]])
                self.tt("dve", self.y[0:bs, g * 512:(g + 1) * 512], self.PS(yi, 0, 512, 0, bs), self.xsD[0:bs, g * 512:(g + 1) * 512],
                        ALU.add, reads=[self.B_xsD], writes=[self.B_y[g]], banks=self.bk(yi))
        self.tt("pool" if state_only else "dve", self.xdtw[0:bs, :].rearrange("p (h c) -> p h c", c=64), self.xdt[0:bs, :].rearrange("p (h c) -> p h c", c=64),
                W3.to_broadcast([bs, 32, 64]), ALU.mult, reads=[Bd, self.B_xdt], writes=[self.B_xdtw])
        for c in range(1 if mrg else nch):
            r0 = 64 * c
            if mrg:
                Lc = 128
            for g in range(4):
                if (not state_only) and c > 0:
                    self.mm(self.PS(ysb + g, 0, 512, r0, r0 + Lc), self.CT[:, g, t0 + r0:t0 + r0 + Lc], self.Hbf[:, g * 512:(g + 1) * 512],
                            True, True, reads=[self.B_CT[g], self.B_Hbf[g]], banks=self.bk(ysb + g))
                dh = self.bank_get()
                self.mm(self.PS(dh, 0, 512), self.Btok[r0:r0 + Lc, g * 128:(g + 1) * 128], self.xdtw[r0:r0 + Lc, g * 512:(g + 1) * 512],
                        True, True, reads=[self.B_Btok, self.B_xdtw], banks=self.bk(dh))
                Hg = self.H[:, g * 512:(g + 1) * 512]
                H3 = Hg.rearrange("p (h c) -> p h c", c=64)
                self.tt("pool" if state_only else "dve", H3, H3,
                        self.EA[:, 32 * c + 8 * g:32 * c + 8 * g + 8].unsqueeze(2).to_broadcast([128, 8, 64]), ALU.mult,
                        reads=[Bd, self.B_H[g]], writes=[self.B_H[g]])
                self.tt("dve", Hg, Hg, self.PS(dh, 0, 512), ALU.add, reads=[self.B_H[g]], writes=[self.B_H[g]], banks=self.bk(dh))
                if not state_only:
                    self.cp("act", self.Hbf[:, g * 512:(g + 1) * 512], Hg, reads=[self.B_H[g]], writes=[self.B_Hbf[g]])
        if state_only:
            return
        for g in range(4):
            i = self.rotn("ytmp")
            self.tt("dve", self.ytmp[i][0:bs, :].rearrange("p (h c) -> p h c", c=64),
                    self.PS(ysb + g, 0, 512, 0, bs).rearrange("p (h c) -> p h c", c=64),
                    E3[:, 8 * g:8 * g + 8, :].to_broadcast([bs, 8, 64]), ALU.mult,
                    reads=[Bd], writes=[self.B_ytmp[i]], banks=self.bk(ysb + g))
            yg = self.y[0:bs, g * 512:(g + 1) * 512]
            self.tt("dve", yg, yg, self.ytmp[i][0:bs, :], ALU.add, reads=[self.B_ytmp[i], self.B_y[g]], writes=[self.B_y[g]])
            self.tt("pool", yg, yg, self.zssd[0:bs, b, g * 512:(g + 1) * 512], ALU.mult, reads=[self.B_zssd[b], self.B_y[g]],
                    writes=[self.B_y[g]])
            self.act(self.ynorm[0:bs, g * 512:(g + 1) * 512], yg, AF.Square, reads=[self.B_y[g]], writes=[self.B_ynorm, self.B_ss],
                     accum_out=self.ss[0:bs, g:g + 1])
        self.bank_unpin(ysb, 4)
        self.act(self.ss[0:bs, 0:4], self.ss[0:bs, 0:4], AF.Ln, reads=[self.B_ss], writes=[self.B_ss], scale=1.0 / 512, bias=EPS)
        self.act(self.ss[0:bs, 0:4], self.ss[0:bs, 0:4], AF.Exp, reads=[self.B_ss], writes=[self.B_ss], scale=-0.5)
        for g in range(4):
            self.act(self.ynorm[0:bs, g * 512:(g + 1) * 512], self.y[0:bs, g * 512:(g + 1) * 512], AF.Copy,
                     reads=[self.B_y[g], self.B_ss], writes=[self.B_ynorm], scale=self.ss[0:bs, g:g + 1])
        yb = self.bank_get(2)
        for ct in range(16):
            self.tp(self.PSB(yb, 2, ct * 128, ct * 128 + bs), self.ynorm[0:bs, ct * 128:(ct + 1) * 128], self.identb[0:bs, 0:bs],
                    reads=[self.B_ynorm, self.B_cst], banks=self.bk(yb + ct // 8))
        self.tt("dve", self.yT[:, :, t0:t0 + bs], self.PSB(yb, 2, 0, 2048).rearrange("p (k t) -> p k t", t=128)[:, :, 0:bs],
                self.gssdT[:, :].unsqueeze(2).to_broadcast([128, 16, bs]), ALU.mult,
                reads=[self.B_par], writes=[self.B_yT[b]], banks=self.bk(yb, 2))

    def tile(self, mode, src_ap, row0, Tn, seq, slot, tile_idx=None, kv=False, outs=None, out_rows=None, with_c=False):
        par = self.rotn("tilepar")
        nbk = (Tn + 127) // 128
        bss = [min(128, Tn - 128 * b) for b in range(nbk)]
        full = mode == "main"
        self.S.tag = "norm"
        self.hT, self.B_hT = self.hT2[par], self.B_hT2[par]
        self.load_norm(src_ap, row0, Tn, seq, par=par)
        self.dump("hT", self.hT[:, :, 0:Tn], self.B_hT)
        self.S.tag = "qkv"
        if full:
            for j in range(2):
                wt, wbuf = self.wget("q%d" % j)
                for ctl in range(4):
                    self.qk_tile("q", wt, wbuf, ctl, 4 * j + ctl, Tn, slot)
        if full or kv:
            for j in range(2):
                wt, wbuf = self.wget("k%d" % j)
                for ctl in range(4):
                    self.qk_tile("k", wt, wbuf, ctl, 4 * j + ctl, Tn, slot)
            for j in range(2):
                wt, wbuf = self.wget("v%d" % j)
                for b in range(nbk):
                    bank = self.bank_get()
                    self.proj_tm(wt, wbuf, b, bss[b], 512, bank)
                    dst = self.Vr[0:bss[b], slot * 2 + b, j * 520:(j + 1) * 520].rearrange("p (h c) -> p h c", c=65)[:, :, 0:64]
                    self.cp("act", dst, self.PS(bank, 0, 512, 0, bss[b]).rearrange("p (h c) -> p h c", c=64),
                            writes=[self.B_V[slot][b]], banks=self.bk(bank))
        if full:
            self.dump("qT", self.qT[:, :, 0:Tn], self.B_qT)
            self.dump("kT", self.kT[:, :, slot * T:slot * T + Tn], self.B_kT[slot])
            for j in range(2):
                wt, wbuf = self.wget("za%d" % j)
                for b in range(nbk):
                    bank = self.bank_get()
                    self.proj_tm(wt, wbuf, b, bss[b], 512, bank)
                    self.act(self.zatt[0:bss[b], b, j * 512:(j + 1) * 512], self.PS(bank, 0, 512, 0, bss[b]), AF.Silu,
                             writes=[self.B_zatt[b]], banks=self.bk(bank))
            self.S.tag = "attn"
            self.attention(Tn, slot, tile_idx)
            self.S.tag = "attproj"
            self.dump("OgT", self.OgT[:, :, 0:Tn], [self.B_OgT])
            for j in range(2):
                wt, wbuf = self.wget("ga%d" % j)
                for c2 in range(2):
                    bank = self.bank_get()
                    for u in range(2):
                        self.proj_fm(wt, wbuf, 2 * c2 + u, Tn, bank, u * T)
                    for u in range(2):
                        dtile = 4 * j + 2 * c2 + u
                        self.act(self.sig[:, dtile, 0:Tn], self.PS(bank, u * T, u * T + Tn), AF.Sigmoid,
                                 writes=[self.B_sig[dtile]], banks=self.bk(bank))
            for j in range(2):
                wt, wbuf = self.wget("wa%d" % j)
                for c2 in range(2):
                    bank = self.bank_get()
                    for u in range(2):
                        for kc in range(8):
                            self.mm(self.PS(bank, u * T, u * T + Tn), wt[:, kc, (2 * c2 + u) * 128:(2 * c2 + u + 1) * 128],
                                    self.OgT[:, kc, 0:Tn], kc == 0, kc == 7, reads=[wbuf, self.B_OgT], banks=self.bk(bank))
                    for u in range(2):
                        dtile = 4 * j + 2 * c2 + u
                        self.tt("dve", self.pre[:, dtile, 0:Tn], self.PS(bank, u * T, u * T + Tn), self.sig[:, dtile, 0:Tn], ALU.mult,
                                reads=[self.B_sig[dtile]], writes=[self.B_pre[dtile]], banks=self.bk(bank))
        self.S.tag = "xbc"
        wt, wbuf = self.wget("dt")
        for b in range(nbk):
            bank = self.bank_get()
            self.proj_tm(wt, wbuf, b, bss[b], 32, bank)
            self.tt("dve", self.dtraw[0:bss[b], b, :], self.PS(bank, 0, 32, 0, bss[b]), self.rows[0:bss[b], 0, :], ALU.add,
                    reads=[self.B_par], writes=[self.B_dtraw[b]], banks=self.bk(bank))
        for j in range(6 if (full or with_c) else 5):
            wt, wbuf = self.wget("xb%d" % j)
            for c2 in range(2):
                bank = self.bank_get()
                for u in range(2):
                    self.proj_fm(wt, wbuf, 2 * c2 + u, Tn, bank, u * T)
                for u in range(2):
                    self.conv_tile(4 * j + 2 * c2 + u, bank, u * T, Tn)
        if full:
            for j in range(4):
                wt, wbuf = self.wget("zs%d" % j)
                for b in range(nbk):
                    bank = self.bank_get()
                    self.proj_tm(wt, wbuf, b, bss[b], 512, bank)
                    self.act(self.zssd[0:bss[b], b, j * 512:(j + 1) * 512], self.PS(bank, 0, 512, 0, bss[b]), AF.Silu,
                             writes=[self.B_zssd[b]], banks=self.bk(bank))
        self.S.tag = "ssd"
        for b in range(nbk):
            self.ssd_block(b, bss[b], not full)
        self.S.tag = "tail"
        if not full:
            return
        self.dump("yT", self.yT[:, :, 0:Tn], self.B_yT)
        for j in range(2):
            wt, wbuf = self.wget("gs%d" % j)
            for c2 in range(2):
                bank = self.bank_get()
                for u in range(2):
                    self.proj_fm(wt, wbuf, 2 * c2 + u, Tn, bank, u * T)
                for u in range(2):
                    dtile = 4 * j + 2 * c2 + u
                    self.act(self.sig[:, dtile, 0:Tn], self.PS(bank, u * T, u * T + Tn), AF.Sigmoid,
                             writes=[self.B_sig[dtile]], banks=self.bk(bank))
        for j in range(4):
            wt, wbuf = self.wget("ws%d" % j)
            bank = self.bank_get()
            for u in range(2):
                for kc in range(16):
                    self.mm(self.PS(bank, u * T, u * T + Tn), wt[:, kc, u * 128:(u + 1) * 128], self.yT[:, kc, 0:Tn],
                            kc == 0, kc == 15, reads=[wbuf] + self.B_yT, banks=self.bk(bank))
            for u in range(2):
                dtile = 2 * j + u
                i = self.rotn("tmpm", 1)
                self.tt("dve", self.tmpm[i][:, 0:Tn], self.PS(bank, u * T, u * T + Tn), self.sig[:, dtile, 0:Tn], ALU.mult,
                        reads=[self.B_sig[dtile]], writes=[self.B_tmpm[i]], banks=self.bk(bank))
                self.tt("pool", self.pre[:, dtile, 0:Tn], self.pre[:, dtile, 0:Tn], self.tmpm[i][:, 0:Tn], ALU.add,
                        reads=[self.B_tmpm[i], self.B_pre[dtile]], writes=[self.B_pre[dtile]])
        self.dump("merged", self.pre[:, :, 0:Tn], self.B_pre)
        for j in range(2):
            wt, wbuf = self.wget("wo%d" % j)
            for b in range(nbk):
                bs = bss[b]
                bank = self.bank_get()
                for kc in range(8):
                    self.mm(self.PS(bank, 0, 512, 0, bs), self.pre[:, kc, 128 * b:128 * b + bs], wt[:, kc, 0:512], kc == 0, kc == 7,
                            reads=[wbuf, self.B_pre[kc]], banks=self.bk(bank))
                xb_ = 2 * par + b
                oc = self.xt[0:bs, xb_, j * 512:(j + 1) * 512]
                pso = self.PS(bank, 0, 512, 0, bs)
                self.tt("dve", pso, pso, self.gate_tok[0:bs, j * 512:(j + 1) * 512], ALU.mult,
                        reads=[self.B_gate], banks=self.bk(bank))
                self.tt("dve", oc, oc, pso, ALU.add, reads=[self.B_xt[xb_]], writes=[self.B_xt[xb_]], banks=self.bk(bank))
        for b in range(nbk):
            bs = bss[b]
            xb_ = 2 * par + b
            self.dma("sp", outs[0].ap()[out_rows + 128 * b: out_rows + 128 * b + bs, :], self.xt[0:bs, xb_, :], self.d_xo[xb_],
                     reads=[self.B_xt[xb_]], is_output=True)

    def emit_kv_out(self, slot, Tn, nk, nv, row0):
        nbk = (Tn + 127) // 128
        for b in range(nbk):
            bs = min(128, Tn - 128 * b)
            bank = self.bank_get()
            for hp in range(8):
                self.tp(self.PSB(bank, 1, hp * 128, hp * 128 + 128, 0, bs), self.kT[:, hp, slot * T + 128 * b: slot * T + 128 * b + bs],
                        self.identb[:, :], reads=[self.B_kT[slot][hp], self.B_cst], banks=self.bk(bank))
            i = self.rotn("kvo")
            self.cp("dve", self.ost[i][0:bs, :], self.PSB(bank, 1, 0, 1024, 0, bs), writes=[self.B_ost[i]], banks=self.bk(bank))
            self.dma("sp", nk.ap()[row0 + 128 * b: row0 + 128 * b + bs, :], self.ost[i][0:bs, :], self.d_ost[i],
                     reads=[self.B_ost[i]], is_output=True)
            i = self.rotn("kvo")
            self.cp("act", self.ost[i][0:bs, :].rearrange("p (h c) -> p h c", c=64),
                    self.Vr[0:bs, slot * 2 + b, :].rearrange("p (h c) -> p h c", c=65)[:, :, 0:64],
                    reads=[self.B_V[slot][b]], writes=[self.B_ost[i]])
            self.dma("sp", nv.ap()[row0 + 128 * b: row0 + 128 * b + bs, :], self.ost[i][0:bs, :], self.d_ost[i],
                     reads=[self.B_ost[i]], is_output=True)

    def emit_conv_out(self, out):
        cst = self.cst
        for q3 in range(3):
            bank = self.bank_get(2)
            for u in range(8):
                t = 8 * q3 + u
                self.tp(self.psum[0:3, bank * 512 + u * 128: bank * 512 + u * 128 + 128], self.halo[:, t, :], cst[:, C_ID:C_ID + 128],
                        reads=[self.B_halo[t], self.B_cst], banks=self.bk(bank + u // 4))
            i = self.rotn("kvo")
            self.cp("dve", self.ost[i][0:3, :], self.psum[0:3, bank * 512: bank * 512 + 1024], writes=[self.B_ost[i]], banks=self.bk(bank, 2))
            self.dma("sp", out.ap()[:, q3 * 1024:(q3 + 1) * 1024], self.ost[i][0:3, :], self.d_ost[i], reads=[self.B_ost[i]], is_output=True)

    def emit_ssm_out(self, out):
        cst = self.cst
        for q4 in range(4):
            bank = self.bank_get()
            for u in range(4):
                t = 4 * q4 + u
                self.tp(self.PS(bank, u * 128, u * 128 + 128), self.H[:, t * 128:(t + 1) * 128], cst[:, C_ID:C_ID + 128],
                        reads=[self.B_H[t // 4], self.B_cst], banks=self.bk(bank))
            self.cp("dve", self.y[:, q4 * 512:(q4 + 1) * 512], self.PS(bank, 0, 512), writes=[self.B_y[q4]], banks=self.bk(bank))
        self.dma("sp", out.ap().rearrange("(t p) n -> p t n", p=128), self.y[:].rearrange("p (t n) -> p t n", n=128), self.d_sso,
                 reads=self.B_y, is_output=True)

    def zero_state(self):
        self.memset("pool", self.H[:], 0.0, writes=self.B_H)
        self.memset("pool", self.Hbf[:], 0.0, writes=self.B_Hbf)
        self.memset("pool", self.halo[:], 0.0, writes=self.B_halo)

    def sample_tile(self):
        I, O, cst = self.I, self.O, self.cst
        self.dma("sp", self.y[:].rearrange("p (t n) -> p t n", n=128), I["state_ssm"].ap().rearrange("(t p) n -> p t n", p=128),
                 self.d_ss, writes=self.B_y)
        for q4 in range(4):
            bank = self.bank_get()
            for u in range(4):
                t = 4 * q4 + u
                self.tp(self.PS(bank, u * 128, u * 128 + 128), self.y[:, t * 128:(t + 1) * 128], cst[:, C_ID:C_ID + 128],
                        reads=[self.B_y[t // 4], self.B_cst], banks=self.bk(bank))
            self.cp("dve", self.H[:, q4 * 512:(q4 + 1) * 512], self.PS(bank, 0, 512), writes=[self.B_H[q4]], banks=self.bk(bank))
            self.cp("act", self.Hbf[:, q4 * 512:(q4 + 1) * 512], self.H[:, q4 * 512:(q4 + 1) * 512], reads=[self.B_H[q4]],
                    writes=[self.B_Hbf[q4]])
        for j_ in range(3):
            self.dma("sp", self.halo[:, :, j_], I["state_conv"].ap()[j_, :].rearrange("(t p) -> p t", p=128), self.d_sc,
                     writes=self.B_halo, slow=True)
        self.S.seal(self.d_sc, list(self.B_halo))
        self.load_norm(I["cache_k"].ap(), 0, 256, 1, plain_kslot={0: (0, 0), 1: (0, 1)})
        self.load_norm(I["cache_k"].ap(), 256, 256, 1, plain_kslot={0: (1, 0), 1: (1, 1)})
        for blk in range(4):
            i = blk % 2
            self.dma("sp", self.ost[i][:, :], I["cache_v"].ap()[128 * blk:128 * blk + 128, :], self.d_ost[i], writes=[self.B_ost[i]])
            self.cp("dve", self.Vr[:, blk, :].rearrange("p (h c) -> p h c", c=65)[:, :, 0:64],
                    self.ost[i][:, :].rearrange("p (h c) -> p h c", c=64), reads=[self.B_ost[i]], writes=[self.B_V[blk // 2][blk % 2]])
        self.make_gate_tok(1)
        self.tile("main", I["x_s"].ap(), 0, TS, 1, 2, tile_idx=None, outs=[O["y_s"]], out_rows=0)
        self.emit_kv_out(2, TS, O["nk_s"], O["nv_s"], 0)
        self.emit_conv_out(O["nconv_s"])
        self.emit_ssm_out(O["nssm_s"])

    def state_pass(self):
        I = self.I
        self.zero_state()
        n = self.n_state
        xb_ = list(self.B_crawx[2:]) + list(self.B_caccx[2:])
        self.memset("pool", self.y[:, 2047:2048], 0.0, writes=list(self.B_y) + xb_)
        self.state_mode = True
        for i in range(n):
            kv = i >= n - 2
            slot = 1 if i == n - 2 else (2 if i == n - 1 else 0)
            self.tile("state", I["x_prev"].ap(), i * T, T, 0, slot, kv=kv, with_c=(i == n - 1))
        self.state_mode = False
        self.memset("pool", self.y[:, 2047:2048], 0.0, writes=list(self.B_y) + xb_)
        f = self.flg[:, 0:1]
        for g in range(4):
            Hg = self.H[:, g * 512:(g + 1) * 512]
            self.ts("pool", Hg, Hg, f, None, ALU.mult, reads=[self.B_par, self.B_H[g]], writes=[self.B_H[g]])
            self.cp("act", self.Hbf[:, g * 512:(g + 1) * 512], Hg, reads=[self.B_H[g]], writes=[self.B_Hbf[g]])
        self.ts("pool", self.halo[:], self.halo[:], f, None, ALU.mult, reads=[self.B_par] + list(self.B_halo), writes=self.B_halo)

    def main_pass(self):
        I, O = self.I, self.O
        self.make_gate_tok(0)
        n = self.n_main
        for t in range(n):
            self.tile("main", I["x_main"].ap(), t * T, T, 0, t % 3, tile_idx=t, outs=[O["y_main"]], out_rows=t * T)
        if n >= 2:
            self.emit_kv_out((n - 2) % 3, T, O["nk"], O["nv"], 0)
        self.emit_kv_out((n - 1) % 3, T, O["nk"], O["nv"], 256)
        self.emit_conv_out(O["nconv"])
        self.emit_ssm_out(O["nssm"])


def make_consts():
    c = np.zeros((128, NCONST), np.float32)
    s = np.arange(128)[:, None]
    l = np.arange(128)[None, :]
    same = (s // 64) == (l // 64)
    c[:, C_ID:C_ID + 128] = np.eye(128)
    c[:, C_J:C_J + 128] = np.eye(128)[::-1]
    c[:, C_TRI:C_TRI + 128] = (same & (s <= l))
    c[:, C_STRI:C_STRI + 128] = (same & (s > l))
    for ch in range(2):
        c[:, C_CSEL + 128 * ch:C_CSEL + 128 * ch + 128] = ((s // 64) == ch)
    c[:, C_TRIL:C_TRIL + 64] = ((s % 64) <= np.arange(64)[None, :])
    c[:, C_BONES:C_BONES + 128] = same
    return c


_CACHE = {}


def get_program(key=("full",), **kw):
    if key not in _CACHE:
        b = Builder(**{k: v for k, v in kw.items() if k == "dbg"})
        nc = b.build(**{k: v for k, v in kw.items() if k != "dbg"})
        _CACHE[key] = (nc, b)
    return _CACHE[key]


def make_in_maps(inp):
    f = lambda a: np.ascontiguousarray(np.asarray(a, dtype=np.float32))
    xp = f(inp["x_prompt"])
    consts = make_consts()
    shared = {
        "consts": consts,
        "norm_g": f(inp["norm_g"][0]), "w_ada": f(inp["w_ada"][0]), "b_ada": f(inp["b_ada"][0]), "w_in": f(inp["w_in"][0]),
        "q_norm_g": f(inp["q_norm_g"][0]), "k_norm_g": f(inp["k_norm_g"][0]), "rel_bias": f(inp["rel_bias"][0]),
        "w_att": f(inp["w_att_proj"][0]), "conv_w": f(inp["conv_w"][0]), "conv_b": f(inp["conv_b"][0]),
        "dt_bias": f(inp["dt_bias"][0]), "a_log": f(inp["a_log"][0]), "d_skip": f(inp["d_skip"][0]),
        "ssd_norm_g": f(inp["ssd_norm_g"][0]), "w_ssd": f(inp["w_ssd_proj"][0]), "w_out": f(inp["w_out"][0]),
    }
    maps = []
    for c in range(8):
        b, half = c // 2, c % 2
        flags = np.zeros((128, 2), np.float32)
        flags[:, 0] = float(half)
        flags[:, 1] = 0.0 if half else -30000.0
        m = dict(shared)
        m.update({
            "x_main": f(xp[b, half * 4096:(half + 1) * 4096]),
            "x_prev": f(xp[b, 0:4096]),
            "x_s": f(inp["x_sample"][c]),
            "c2": f(np.stack([np.asarray(inp["c_prompt"])[b], np.asarray(inp["c_sample"])[c]])),
            "cache_k": f(np.asarray(inp["cache_k"])[0, c].reshape(512, 1024)),
            "cache_v": f(np.asarray(inp["cache_v"])[0, c].reshape(512, 1024)),
            "state_conv": f(np.asarray(inp["state_conv"])[0, c]),
            "state_ssm": f(np.asarray(inp["state_ssm"])[0, c].reshape(2048, 128)),
            "flags": flags,
        })
        maps.append(m)
    return maps


def assemble(res):
    R = res
    y_prompt = np.zeros((4, 8192, 1024), np.float32)
    y_sample = np.zeros((8, 16, 1024), np.float32)
    nkp = np.zeros((1, 4, 512, 16, 64), np.float32)
    nvp = np.zeros((1, 4, 512, 16, 64), np.float32)
    ncp = np.zeros((1, 4, 3, 3072), np.float32)
    nhp = np.zeros((1, 4, 32, 64, 128), np.float32)
    nks = np.zeros((1, 8, 16, 16, 64), np.float32)
    nvs = np.zeros((1, 8, 16, 16, 64), np.float32)
    ncs = np.zeros((1, 8, 3, 3072), np.float32)
    nhs = np.zeros((1, 8, 32, 64, 128), np.float32)
    for c in range(8):
        b, half = c // 2, c % 2
        r = R[c]
        y_prompt[b, half * 4096:(half + 1) * 4096] = r["y_main"]
        y_sample[c] = r["y_s"]
        if half == 1:
            nkp[0, b] = r["nk"].reshape(512, 16, 64)
            nvp[0, b] = r["nv"].reshape(512, 16, 64)
            ncp[0, b] = r["nconv"]
            nhp[0, b] = r["nssm"].reshape(32, 64, 128)
        nks[0, c] = r["nk_s"].reshape(16, 16, 64)
        nvs[0, c] = r["nv_s"].reshape(16, 16, 64)
        ncs[0, c] = r["nconv_s"]
        nhs[0, c] = r["nssm_s"].reshape(32, 64, 128)
    return (y_prompt, y_sample, nkp, nvp, ncp, nhp, nks, nvs, ncs, nhs)


def kernel(**inputs):
    nc, _ = get_program()
    maps = make_in_maps(inputs)
    res = run_bass_kernel_spmd(nc, maps, core_ids=list(range(8)))
    return assemble(res.results)
```
